# Optimizing a Trainium2 kernel written in Bass

```python
import jax, jax.numpy as jnp
from jax import lax
import numpy as np

D_MODEL = 1024
BATCH = 32
SEQ = 256
DEPTH = 2
DEC_BATCH = 4
DEC_SEQ = 2048
PAST_LEN = 512

GRID_W = 64
N_EVEN = (DEPTH + 1) // 2
N_ODD = DEPTH // 2
MLA_HEADS = 8
Q_LORA = 384
KV_LORA = 256
QK_NOPE = 64
QK_ROPE = 32
V_HEAD = 64
CONV_CH = 512
CONV_W = 31
CONV_PAD = CONV_W // 2
GQA_HEADS = 16
GQA_KV_HEADS = 4
GQA_HEAD_DIM = 64
D_FF = 2816
MACARON = 0.5
N_MOD = 9
Q_BLOCK = 128
ROPE_BASE = 10000.0
EPS = 1e-6
IN_A = Q_LORA + KV_LORA + QK_ROPE + 2 * CONV_CH
MIX_A = MLA_HEADS * V_HEAD + CONV_CH
IN_C = (GQA_HEADS + 2 * GQA_KV_HEADS) * GQA_HEAD_DIM
MIX_C = GQA_HEADS * GQA_HEAD_DIM

kernel_name = 'hybrid_mla_conformer_gqa_diffusion_step'


def rms_norm(x, g):
    xf = x.astype(jnp.float32)
    y = xf * lax.rsqrt(jnp.mean(xf * xf, axis=-1, keepdims=True) + EPS)
    return y.astype(x.dtype) * g


def layer_norm(x, g, b):
    xf = x.astype(jnp.float32)
    mu = jnp.mean(xf, axis=-1, keepdims=True)
    var = jnp.mean(jnp.square(xf - mu), axis=-1, keepdims=True)
    return ((xf - mu) * lax.rsqrt(var + EPS)).astype(x.dtype) * g + b


def swiglu(h, w_in, w_out):
    a, b = jnp.split(h @ w_in, 2, axis=-1)
    return (jax.nn.silu(a) * b) @ w_out


def axial_rope(x):
    n, d = x.shape[1], x.shape[-1]
    half = d // 2
    rows = n // GRID_W
    pos_row = jnp.repeat(jnp.arange(rows), GRID_W)
    pos_col = jnp.tile(jnp.arange(GRID_W), rows)
    inv = ROPE_BASE ** (-jnp.arange(0, half, 2, dtype=jnp.float32) / half)

    def rot(xa, pos):
        ang = pos.astype(jnp.float32)[:, None] * inv[None, :]
        cos = jnp.cos(ang)[None, :, None, :].astype(x.dtype)
        sin = jnp.sin(ang)[None, :, None, :].astype(x.dtype)
        x1, x2 = jnp.split(xa, 2, axis=-1)
        return jnp.concatenate([x1 * cos - x2 * sin, x2 * cos + x1 * sin], axis=-1)

    return jnp.concatenate([rot(x[..., :half], pos_row), rot(x[..., half:], pos_col)], axis=-1)


def block_attention(q, k, v, scale):
    B, Sq, Hk, G, dk = q.shape
    nb = Sq // Q_BLOCK
    qb = q.reshape(B, nb, Q_BLOCK, Hk, G, dk).transpose(1, 0, 2, 3, 4, 5)

    def one(qblk):
        s = jnp.einsum('bqhgd,bshd->bhgqs', qblk, k).astype(jnp.float32) * scale
        p = jax.nn.softmax(s, axis=-1).astype(v.dtype)
        return jnp.einsum('bhgqs,bshd->bqhgd', p, v)

    o = lax.map(one, qb)
    return o.transpose(1, 0, 2, 3, 4, 5).reshape(B, Sq, Hk * G, v.shape[-1])


def depthwise_conv(u, w, b):
    y = lax.conv_general_dilated(u, w[:, None, :], window_strides=(1,), padding=[(CONV_PAD, CONV_PAD)],
                                 dimension_numbers=('NWC', 'WIO', 'NWC'), feature_group_count=u.shape[-1])
    return y + b


def mla_keys_values(ckv, kr, w_kv_up):
    B, S, _ = ckv.shape
    kv = (ckv @ w_kv_up).reshape(B, S, MLA_HEADS, QK_NOPE + V_HEAD)
    k_nope, v = kv[..., :QK_NOPE], kv[..., QK_NOPE:]
    k = jnp.concatenate([k_nope, jnp.broadcast_to(kr[:, :, None, :], (B, S, MLA_HEADS, QK_ROPE))], axis=-1)
    return k, v


def mla_conv_mixer(h, w_in, g_ql, w_qu, g_kvl, w_kvu, w_dw, b_dw, g_ln, b_ln, w_out, ctx):
    B, S, _ = h.shape
    cq, ckv, kr, u = jnp.split(h @ w_in, [Q_LORA, Q_LORA + KV_LORA, Q_LORA + KV_LORA + QK_ROPE], axis=-1)
    q = (rms_norm(cq, g_ql) @ w_qu).reshape(B, S, MLA_HEADS, QK_NOPE + QK_ROPE)
    ckv = rms_norm(ckv, g_kvl)
    if ctx is None:
        k, v = mla_keys_values(ckv, kr, w_kvu)
    else:
        q = jnp.concatenate([q[..., :QK_NOPE], axial_rope(q[..., QK_NOPE:])], axis=-1)
        kr_lat = axial_rope(kr[:, :, None, :])[:, :, 0, :]
        k_lat, v_lat = mla_keys_values(ckv, kr_lat, w_kvu)
        k_ctx, v_ctx = mla_keys_values(ctx[0], ctx[1], w_kvu)
        k = jnp.concatenate([k_ctx, k_lat], axis=1)
        v = jnp.concatenate([v_ctx, v_lat], axis=1)
    o = block_attention(q[:, :, :, None, :], k, v, (QK_NOPE + QK_ROPE) ** -0.5)
    attn = o.reshape(B, S, MLA_HEADS * V_HEAD)
    a, b = jnp.split(u, 2, axis=-1)
    g = depthwise_conv(a * jax.nn.sigmoid(b), w_dw, b_dw)
    g = jax.nn.silu(layer_norm(g, g_ln, b_ln))
    out = jnp.concatenate([attn, g], axis=-1) @ w_out
    return out, (ckv, kr)


def gqa_mixer(h, w_in, g_q, g_k, w_out, ctx):
    B, S, _ = h.shape
    q, k, v = jnp.split(h @ w_in, [MIX_C, MIX_C + GQA_KV_HEADS * GQA_HEAD_DIM], axis=-1)
    q = rms_norm(q.reshape(B, S, GQA_HEADS, GQA_HEAD_DIM), g_q)
    k = rms_norm(k.reshape(B, S, GQA_KV_HEADS, GQA_HEAD_DIM), g_k)
    v = v.reshape(B, S, GQA_KV_HEADS, GQA_HEAD_DIM)
    if ctx is None:
        keys, vals = k, v
    else:
        q = axial_rope(q)
        keys = jnp.concatenate([ctx[0], axial_rope(k)], axis=1)
        vals = jnp.concatenate([ctx[1], v], axis=1)
    grp = GQA_HEADS // GQA_KV_HEADS
    o = block_attention(q.reshape(B, S, GQA_KV_HEADS, grp, GQA_HEAD_DIM), keys, vals, GQA_HEAD_DIM ** -0.5)
    return o.reshape(B, S, MIX_C) @ w_out, (k, v)


def trunk(x, cond, ctx, p):
    e = jax.nn.silu(cond)
    saved_a, saved_c = [], []
    for l in range(DEPTH):
        mod = (e @ p['w_mod'][l] + p['b_mod'][l])[:, None, :]
        sh1, sc1, g1, shm, scm, gm, sh2, sc2, g2 = jnp.split(mod, N_MOD, axis=-1)
        h = rms_norm(x, p['g_ff1'][l]) * (1 + sc1) + sh1
        x = x + MACARON * g1 * swiglu(h, p['w_ff1_in'][l], p['w_ff1_out'][l])
        h = rms_norm(x, p['g_mix'][l]) * (1 + scm) + shm
        if l % 2 == 0:
            i = l // 2
            cc = None if ctx is None else (ctx[0][:, i], ctx[1][:, i])
            out, st = mla_conv_mixer(h, p['w_in_a'][i], p['g_q_lora'][i], p['w_q_up'][i], p['g_kv_lora'][i],
                                     p['w_kv_up'][i], p['w_dw'][i], p['b_dw'][i], p['g_conv_ln'][i],
                                     p['b_conv_ln'][i], p['w_out_a'][i], cc)
            saved_a.append(st)
        else:
            i = l // 2
            cc = None if ctx is None else (ctx[2][:, i], ctx[3][:, i])
            out, st = gqa_mixer(h, p['w_in_c'][i], p['g_q_head'][i], p['g_k_head'][i], p['w_out_c'][i], cc)
            saved_c.append(st)
        x = x + gm * out
        h = rms_norm(x, p['g_ff2'][l]) * (1 + sc2) + sh2
        x = x + MACARON * g2 * swiglu(h, p['w_ff2_in'][l], p['w_ff2_out'][l])
    return rms_norm(x, p['g_final']), saved_a, saved_c


def setup_inputs(seed: int = 0) -> dict:
    key = jax.random.key(seed)
    ks = iter(jax.random.split(key, 48))
    f32 = jnp.float32

    def nrm(shape, scale=1.0):
        return jax.random.normal(next(ks), shape, f32) * scale

    def gain(shape):
        return 1.0 + 0.05 * jax.random.normal(next(ks), shape, f32)

    return {
        'x_prompt': nrm((BATCH, SEQ, D_MODEL)),
        'x_sample': nrm((DEC_BATCH, DEC_SEQ, D_MODEL)),
        'cache_mla_ckv': nrm((DEC_BATCH, N_EVEN, PAST_LEN, KV_LORA)),
        'cache_mla_krope': nrm((DEC_BATCH, N_EVEN, PAST_LEN, QK_ROPE)),
        'cache_gqa_k': nrm((DEC_BATCH, N_ODD, PAST_LEN, GQA_KV_HEADS, GQA_HEAD_DIM)),
        'cache_gqa_v': nrm((DEC_BATCH, N_ODD, PAST_LEN, GQA_KV_HEADS, GQA_HEAD_DIM)),
        'c': nrm((DEC_BATCH, D_MODEL)),
        'c_ctx': nrm((D_MODEL,)),
        'g_ff1': gain((DEPTH, D_MODEL)),
        'w_ff1_in': nrm((DEPTH, D_MODEL, 2 * D_FF), D_MODEL ** -0.5),
        'w_ff1_out': nrm((DEPTH, D_FF, D_MODEL), D_FF ** -0.5),
        'g_mix': gain((DEPTH, D_MODEL)),
        'g_ff2': gain((DEPTH, D_MODEL)),
        'w_ff2_in': nrm((DEPTH, D_MODEL, 2 * D_FF), D_MODEL ** -0.5),
        'w_ff2_out': nrm((DEPTH, D_FF, D_MODEL), D_FF ** -0.5),
        'w_mod': nrm((DEPTH, D_MODEL, N_MOD * D_MODEL), D_MODEL ** -0.5),
        'b_mod': nrm((DEPTH, N_MOD * D_MODEL), 0.01),
        'w_in_a': nrm((N_EVEN, D_MODEL, IN_A), D_MODEL ** -0.5),
        'g_q_lora': gain((N_EVEN, Q_LORA)),
        'w_q_up': nrm((N_EVEN, Q_LORA, MLA_HEADS * (QK_NOPE + QK_ROPE)), Q_LORA ** -0.5),
        'g_kv_lora': gain((N_EVEN, KV_LORA)),
        'w_kv_up': nrm((N_EVEN, KV_LORA, MLA_HEADS * (QK_NOPE + V_HEAD)), KV_LORA ** -0.5),
        'w_dw': nrm((N_EVEN, CONV_W, CONV_CH), CONV_W ** -0.5),
        'b_dw': nrm((N_EVEN, CONV_CH), 0.01),
        'g_conv_ln': gain((N_EVEN, CONV_CH)),
        'b_conv_ln': nrm((N_EVEN, CONV_CH), 0.01),
        'w_out_a': nrm((N_EVEN, MIX_A, D_MODEL), MIX_A ** -0.5),
        'w_in_c': nrm((N_ODD, D_MODEL, IN_C), D_MODEL ** -0.5),
        'g_q_head': gain((N_ODD, GQA_HEAD_DIM)),
        'g_k_head': gain((N_ODD, GQA_HEAD_DIM)),
        'w_out_c': nrm((N_ODD, MIX_C, D_MODEL), MIX_C ** -0.5),
        'g_final': gain((D_MODEL,)),
    }


def reference(x_prompt, x_sample, cache_mla_ckv, cache_mla_krope, cache_gqa_k, cache_gqa_v, c, c_ctx,
              g_ff1, w_ff1_in, w_ff1_out, g_mix, g_ff2, w_ff2_in, w_ff2_out, w_mod, b_mod,
              w_in_a, g_q_lora, w_q_up, g_kv_lora, w_kv_up, w_dw, b_dw, g_conv_ln, b_conv_ln, w_out_a,
              w_in_c, g_q_head, g_k_head, w_out_c, g_final):
    p = dict(g_ff1=g_ff1, w_ff1_in=w_ff1_in, w_ff1_out=w_ff1_out, g_mix=g_mix, g_ff2=g_ff2,
             w_ff2_in=w_ff2_in, w_ff2_out=w_ff2_out, w_mod=w_mod, b_mod=b_mod,
             w_in_a=w_in_a, g_q_lora=g_q_lora, w_q_up=w_q_up, g_kv_lora=g_kv_lora, w_kv_up=w_kv_up,
             w_dw=w_dw, b_dw=b_dw, g_conv_ln=g_conv_ln, b_conv_ln=b_conv_ln, w_out_a=w_out_a,
             w_in_c=w_in_c, g_q_head=g_q_head, g_k_head=g_k_head, w_out_c=w_out_c, g_final=g_final)
    y_prompt, saved_a, saved_c = trunk(x_prompt, c_ctx[None, :], None, p)
    new_mla_ckv = jnp.stack([s[0] for s in saved_a], axis=1)
    new_mla_krope = jnp.stack([s[1] for s in saved_a], axis=1)
    new_gqa_k = jnp.stack([s[0] for s in saved_c], axis=1)
    new_gqa_v = jnp.stack([s[1] for s in saved_c], axis=1)
    y_sample, _, _ = trunk(x_sample, c, (cache_mla_ckv, cache_mla_krope, cache_gqa_k, cache_gqa_v), p)
    return (y_prompt, y_sample, new_mla_ckv, new_mla_krope, new_gqa_k, new_gqa_v)
```

```python
import numpy as np
import concourse.bass as bass
import concourse.mybir as mybir
from concourse.bass_utils import run_bass_kernel_spmd

F32 = mybir.dt.float32
BF16 = mybir.dt.bfloat16
ALU = mybir.AluOpType
AF = mybir.ActivationFunctionType

T = 2048
D = 1024
NCTX = 512
NK = NCTX + T
KT = NK // 128
NQB = T // 256
DFF = 2816
EPS = 1e-6

_COLS = {}
_o = 0
for _n, _w in [("cond", 8), ("gff1_0", 8), ("gmix_0", 8), ("gff2_0", 8), ("gff1_1", 8), ("gmix_1", 8),
               ("gff2_1", 8), ("gfinal", 8), ("gql", 3), ("gkvl", 2), ("bdw", 4), ("gln", 4), ("bln", 4),
               ("wdw", 124), ("gq", 1), ("gqp", 1), ("gk", 1), ("gkp", 1), ("hmask", 1), ("eps", 1),
               ("bmod", 144), ("abias", 160)]:
    _COLS[_n] = _o
    _o += _w
NCOLS = _o


def _rope_meta(d):
    half = d // 2
    nf = half // 2
    partner = np.zeros(d, np.int64)
    sign = np.zeros(d, np.float32)
    sec = np.zeros(d, np.int64)
    fr = np.zeros(d, np.int64)
    for i in range(d):
        s, ii = i // half, i % half
        f, part = ii % nf, ii // nf
        partner[i] = i + nf if part == 0 else i - nf
        sign[i] = -1.0 if part == 0 else 1.0
        sec[i] = s
        fr[i] = f
    inv = (np.float32(10000.0) ** (-np.arange(0, half, 2, dtype=np.float32) / np.float32(half))).astype(np.float32)
    return partner, sign, sec, fr, inv


def _rope_tables(d, sample):
    partner, sign, sec, fr, inv = _rope_meta(d)
    if not sample:
        return np.ones((d, T), np.float32), np.zeros((d, T), np.float32)
    t = np.arange(T)
    pos = np.stack([(t // 64).astype(np.float32), (t % 64).astype(np.float32)], 0)
    ang = pos[sec] * inv[fr][:, None]
    ang = ang.astype(np.float32)
    return np.cos(ang).astype(np.float32), (np.sin(ang).astype(np.float32) * sign[:, None]).astype(np.float32)


def build_nc():
    nc = bass.Bass("TRN2", target_bir_lowering=False)
    try:
        _build(nc)
    except Exception as ex:
        if type(ex).__name__ != "_Stop":
            raise
    return nc


def _build(nc):

    def din(name, shape):
        return nc.dram_tensor(name, list(shape), F32, kind="ExternalInput").ap()

    def dout(name, shape):
        return nc.dram_tensor(name, list(shape), F32, kind="ExternalOutput").ap()

    xT_d = din("xT", [D, T])
    cols_d = din("cols", [128, NCOLS])
    wmod_d = din("wmod", [2, D, 9216])
    wffin_d = din("wffin", [4, 22, 128, 2048])
    wffout_d = din("wffout", [4, DFF, D])
    wina_d = din("wina", [D, 1856])
    wqu_d = din("wqu2", [384, 1536])
    wkvu_d = din("wkvu", [256, 1024])
    wouta_d = din("wouta", [D, D])
    winc_d = din("winc", [D, 2816])
    woutc_d = din("woutc", [D, D])
    rope_d = din("rope", [2, 2, 128, T])
    ckvctx_d = din("ckvctxT", [256, NCTX])
    krctx_d = din("krctxT", [32, NCTX])
    gkctx_d = din("gkctxT", [256, NCTX])
    gvctx_d = din("gvctx", [NCTX, 256])
    ident_d = din("ident", [128, 128])
    yT_d = dout("yT", [D, T])
    ckvT_d = dout("ckvT", [256, T])
    krT_d = dout("krT", [32, T])
    gkT_d = dout("gkT", [256, T])
    gv_d = dout("gv", [T, 256])

    ARENA = 212736
    arena = nc.alloc_sbuf_tensor("arena", [128, ARENA], mybir.dt.uint8).ap()

    def view(off, shape, dt):
        esz = 4 if dt == F32 else 2
        n = int(np.prod(shape)) * esz
        assert off % 4 == 0 and off + n <= ARENA, (off, n)
        v = arena[:, off:off + n].bitcast(dt)
        if len(shape) == 2:
            v = v.rearrange("p (a b) -> p a b", a=shape[0])
        elif len(shape) == 3:
            v = v.rearrange("p (a b c) -> p a b c", a=shape[0], b=shape[1])
        return v

    ps = nc.alloc_psum_tensor("ps", [128, 4096], F32).ap()

    def bank(b, n=1):
        return ps[:, b * 512:(b + n) * 512]

    class Sch:
        def __init__(s):
            s.engs = {"pe": nc.tensor, "act": nc.scalar, "dve": nc.vector, "pool": nc.gpsimd, "sp": nc.sync}
            s.sems, s.cnt = {}, {}
            s.waited = {e: {} for e in s.engs}
            s.lastw, s.rds = {}, {}
            s.pend = {e: [] for e in s.engs}
            s.stores = set()
            s.floor = None

        def sem(s, key):
            if key not in s.sems:
                s.sems[key] = nc.alloc_semaphore("s%d" % len(s.sems))
                s.cnt[key] = 0
            return s.sems[key]

        def _deps(s, e, rd, wr):
            deps = {}

            def add(ev):
                if ev is None:
                    return
                k, v = ev
                if v is None:
                    assert k == e == "pe", ("dependency on unsignalled op", k, e)
                    return
                if deps.get(k, 0) < v:
                    deps[k] = v

            add(s.floor)
            for r in rd:
                add(s.lastw.get(r))
            for w in wr:
                add(s.lastw.get(w))
                for ev in s.rds.get(w, {}).values():
                    add(ev)
            for k, v in deps.items():
                if k == e and e == "pe":
                    continue
                if s.waited[e].get(k, 0) >= v:
                    continue
                s.engs[e].wait_ge(s.sem(k), v)
                s.waited[e][k] = v

        def _record(s, ev, rd, wr):
            for r in rd:
                s.rds.setdefault(r, {})[ev[0]] = ev
            for w in wr:
                s.lastw[w] = ev
                s.rds[w] = {}

        @staticmethod
        def _exp(lst):
            out = []
            for r in lst:
                if r == "hT" or r == "xT":
                    out += [(r, t) for t in range(4)]
                else:
                    out.append(r)
            return out

        def op(s, e, fn, rd=(), wr=(), sig=True):
            rd, wr = s._exp(rd), s._exp(wr)
            psr = [r for r in rd if isinstance(r, tuple) and r[0] == "ps"]
            if psr:
                rd = [r for r in rd if not (isinstance(r, tuple) and r[0] == "ps")]
                wr = list(wr) + psr
            s._deps(e, rd, wr)
            inst = fn()
            ev = [e, None]
            s.pend[e].append(ev)
            s._record(ev, rd, wr)
            if sig:
                s.sem(e)
                s.cnt[e] += 1
                inst.then_inc(s.sems[e], 1)
                for p in s.pend[e]:
                    p[1] = s.cnt[e]
                s.pend[e] = []
            return inst

        def dma(s, q, out, in_, rd=(), wr=(), key=None, store=False):
            rd, wr = s._exp(rd), s._exp(wr)
            s._deps(q, rd, wr)
            sm = s.sem(key)
            s.cnt[key] += 16
            s.engs[q].dma_start(out=out, in_=in_).then_inc(sm, 16)
            ev = [key, s.cnt[key]]
            s._record(ev, rd, wr)
            if store:
                s.stores.add(key)

        def fence(s):
            res = sorted(set(s.lastw.keys()) | set(s.rds.keys()), key=str)
            s.op("dve", lambda: nc.vector.memset(fcell, 0.0), rd=(), wr=res)
            s.floor = s.lastw[res[0]]

        def finish(s):
            for key in sorted(s.stores, key=str):
                nc.sync.wait_ge(s.sems[key], s.cnt[key])

    S = Sch()

    class _Stop(Exception):
        pass

    def checkpoint(name):
        import os
        if os.environ.get("KSTOP", "") == name:
            for c in range(8):
                S.dma("sp", yT_d[c * 128:(c + 1) * 128, :], xT[:, c, :], rd=["xT"], key="st_dbg", store=True)
            S.finish()
            raise _Stop()

    def mm(out, lhsT, rhs, start, stop, rd, wr, sig=True, skip=False):
        if skip:
            return S.op("pe", lambda: nc.tensor.matmul(out, lhsT, rhs, start=start, stop=stop, skip_group_check=True),
                        rd, wr, sig)
        return S.op("pe", lambda: nc.tensor.matmul(out, lhsT, rhs, start=start, stop=stop), rd, wr, sig)

    def act(out, in_, func, rd, wr, bias=None, scale=None):
        kw = {}
        if bias is not None:
            kw["bias"] = bias
        if scale is not None:
            kw["scale"] = scale
        return S.op("act", lambda: nc.scalar.activation(out=out, in_=in_, func=func, **kw), rd, wr)

    def tt(out, a, b, op, rd, wr, e="dve"):
        return S.op(e, lambda: S.engs[e].tensor_tensor(out, a, b, op), rd, wr)

    def ts(out, a, s1, s2, op0, op1, rd, wr, e="dve"):
        return S.op(e, lambda: S.engs[e].tensor_scalar(out, a, s1, s2, op0, op1), rd, wr)

    def stt(out, a, sc, b, op0, op1, rd, wr):
        return S.op("dve", lambda: nc.vector.scalar_tensor_tensor(out, a, sc, b, op0, op1), rd, wr)

    def recip(out, a, rd, wr):
        return S.op("dve", lambda: nc.vector.reciprocal(out, a), rd, wr)

    def cpy(out, a, rd, wr, e="dve"):
        return S.op(e, lambda: S.engs[e].tensor_copy(out, a), rd, wr)

    XT = 0
    HT = 65536
    CONST = 98304
    SC = 102400
    R = 126976
    RSZ = ARENA - R
    xT = view(XT, [8, T], F32)
    hT = view(HT, [8, T], BF16)
    cols = view(CONST, [NCOLS], F32)
    o = CONST + 2304
    ones = view(o, [128], BF16); o += 256
    blk = view(o, [128], BF16); o += 256
    ones32 = view(o, [64], F32); o += 256
    modT = view(o, [2, 72], F32); o += 576
    der = view(o, [2, 48], F32); o += 384
    e_bf = view(o, [8], BF16); o += 32
    fcell = view(o, [1], F32); o += 32
    assert o <= SC
    sq = [view(SC + i * 4096, [4, 512], BF16) for i in range(2)]
    rt = [view(SC + 8192 + i * 2048, [512], F32) for i in range(2)]
    tmp = [view(SC + 12288 + i * 2048, [512], F32) for i in range(2)]
    sa = [view(SC + 16384 + i * 1024, [512], BF16) for i in range(2)]
    tmp2 = [view(SC + 18432 + i * 2048, [512], F32) for i in range(2)]
    ident = view(SC + 22528, [128], BF16)
    dg = [view(SC + 22784 + i * 256, [128], BF16) for i in range(4)]
    rrow = view(SC, [1024], F32)
    bcs = [view(SC + 4096 + i * 1024, [256], F32) for i in range(2)]

    def col(name, i=0, n=1):
        o_ = _COLS[name] + i
        return cols[:, o_:o_ + n]

    S.dma("sp", cols, cols_d, wr=["cols"], key="ld_cols")
    for t4 in range(4):
        S.dma("sp", xT[:, :, t4 * 512:(t4 + 1) * 512],
              xT_d[:, t4 * 512:(t4 + 1) * 512].rearrange("(c p) t -> p c t", p=128),
              wr=[("xT", t4)], key=("ld_x", t4))
    S.op("dve", lambda: nc.vector.memset(ones, 1.0), wr=["ones"])
    S.op("dve", lambda: nc.vector.memset(blk, 0.0), wr=["blk"])
    S.op("dve", lambda: nc.vector.memset(blk[0:64, 0:64], 1.0), wr=["blk"])
    S.op("dve", lambda: nc.vector.memset(blk[64:128, 64:128], 1.0), wr=["blk"])
    S.op("dve", lambda: nc.vector.memset(ones32, 1.0), wr=["ones32"])
    act(e_bf, col("cond", 0, 8), AF.Silu, ["cols"], ["e"])
    S.dma("pool", ident, ident_d, wr=["ident"], key=("ld", "ident"))

    WM = [view(R + 65536 + i * 8192, [8, 512], BF16) for i in range(2)]
    mod_jobs = [(l, p) for l in range(2) for p in range(18)]

    def mod_issue(k):
        l, p = mod_jobs[k]
        slot = k % 2
        S.dma("pool", WM[slot], wmod_d[l, :, p * 512:(p + 1) * 512].rearrange("(kc p) n -> p kc n", p=128),
              wr=[("wm", slot)], key=("ld_wm", slot))

    def mod_compute(k):
        l, p = mod_jobs[k]
        slot = k % 2
        for c in range(4):
            cc = l * 72 + p * 4 + c
            for kc in range(8):
                mm(bank(7)[:, cc:cc + 1], WM[slot][:, kc, c * 128:(c + 1) * 128], e_bf[:, kc:kc + 1],
                   kc == 0, kc == 7, [("wm", slot), "e"], [("ps", 7)], sig=(kc == 7))

    def mod_evac(k0, k1):
        for l in range(2):
            ps_ = [p for (ll, p) in mod_jobs[k0:k1] if ll == l]
            if not ps_:
                continue
            c0, c1 = min(ps_) * 4, (max(ps_) + 1) * 4
            tt(modT[:, l, c0:c1], bank(7)[:, l * 72 + c0:l * 72 + c1], col("bmod", l * 72 + c0, c1 - c0), ALU.add,
               [("ps", 7), "cols"], ["mod"])

    def mod_der(l, i):
        gname, sci, gi, gs = [("gff1_%d" % l, 1, 2, 0.5), ("gmix_%d" % l, 4, 5, 1.0), ("gff2_%d" % l, 7, 8, 0.5)][i]
        A = der[:, l, i * 16:i * 16 + 8]
        G = der[:, l, i * 16 + 8:i * 16 + 16]
        ts(A, modT[:, l, sci * 8:sci * 8 + 8], 1.0, 1.0, ALU.mult, ALU.add, ["mod"], ["der"])
        tt(A, A, col(gname, 0, 8), ALU.mult, ["der", "cols"], ["der"])
        ts(G, modT[:, l, gi * 8:gi * 8 + 8], gs, 0.0, ALU.mult, ALU.add, ["mod"], ["der"])

    mod_issue(0)
    for k in range(6):
        if k + 1 < 6:
            mod_issue(k + 1)
        mod_compute(k)
    mod_evac(0, 6)
    mod_der(0, 0)

    checkpoint("mod")

    def rms_rstd(src_fn, nch, ncols, inv_n, rd, lhs, rti, sqi, psb):
        for c0 in range(0, nch, 4):
            n = min(4, nch - c0)
            sqt = sq[(sqi + c0 // 4) % 2]
            for c in range(n):
                act(sqt[:, c, 0:ncols], src_fn(c0 + c), AF.Square, rd, [("sq", (sqi + c0 // 4) % 2)])
            for c in range(n):
                mm(bank(psb)[:, 0:ncols], lhs, sqt[:, c, 0:ncols], (c0 + c) == 0, (c0 + c) == nch - 1,
                   [("sq", (sqi + c0 // 4) % 2), "ones", "blk"], [("ps", psb)], sig=((c0 + c) == nch - 1 or c == n - 1))
        act(rt[rti][:, 0:ncols], bank(psb)[:, 0:ncols], AF.Ln, [("ps", psb), "cols"], [("rt", rti)],
            bias=col("eps"), scale=inv_n)
        act(rt[rti][:, 0:ncols], rt[rti][:, 0:ncols], AF.Exp, [("rt", rti)], [("rt", rti)], scale=-0.5)

    def norm_t4(l, which, t4):
        A = der[:, l, which * 16:which * 16 + 8]
        shi = [0, 3, 6][which]
        tsl = slice(t4 * 512, (t4 + 1) * 512)
        rms_rstd(lambda c: xT[:, c, tsl], 8, 512, 1.0 / D, [("xT", t4)], ones, t4 % 2, 0, 6 + t4 % 2)
        for kc in range(8):
            k2 = kc % 2
            stt(tmp[k2], xT[:, kc, tsl], A[:, kc:kc + 1], rt[t4 % 2], ALU.mult, ALU.mult,
                [("xT", t4), "der", ("rt", t4 % 2)], [("tmp", k2)])
            act(hT[:, kc, tsl], tmp[k2], AF.Identity, [("tmp", k2), "mod"], [("hT", t4)],
                bias=modT[:, l, shi * 8 + kc:shi * 8 + kc + 1], scale=1.0)

    def norm_mod(l, which):
        for t4 in range(4):
            norm_t4(l, which, t4)

    def final_t4(t4):
        tsl = slice(t4 * 512, (t4 + 1) * 512)
        rms_rstd(lambda c: xT[:, c, tsl], 8, 512, 1.0 / D, [("xT", t4)], ones, t4 % 2, 0, 6 + t4 % 2)
        for kc in range(8):
            k2 = kc % 2
            stt(tmp2[k2], xT[:, kc, tsl], col("gfinal", kc), rt[t4 % 2], ALU.mult, ALU.mult,
                [("xT", t4), "cols", ("rt", t4 % 2)], [("tmp2", k2)])
            S.dma("sp", yT_d[kc * 128:(kc + 1) * 128, tsl], tmp2[k2], rd=[("tmp2", k2)], key=("st_t2", k2), store=True)

    def ffn(l, f, side=None, ders=(), fence=True, do_norm=True, tail=None):
        which = 0 if f == 0 else 2
        if fence:
            S.fence()
        if do_norm:
            norm_mod(l, which)
        G = der[:, l, which * 16 + 8:which * 16 + 16]
        wi_idx = l * 2 + f
        NB = 4
        gT = view(R, [8, T], BF16)
        wi = [view(R + 32768 + i * 4096, [8, 256], BF16) for i in range(NB)]
        wo = view(R + 49152, [8, 1024], BF16)

        def load_wi(j):
            S.dma("pool", wi[j % NB], wffin_d[wi_idx, j].rearrange("p (kc n) -> p kc n", kc=8),
                  wr=[("wi", j % NB)], key=("ld_wi", j % NB))

        for j in range(NB):
            load_wi(j)
        step = 0
        for (g0, gn) in [(0, 8), (8, 7), (15, 7)]:
            last = (g0 == 15)
            S.dma("pool", wo[:, 0:gn, :], wffout_d[wi_idx, g0 * 128:(g0 + gn) * 128, :].rearrange("(j p) d -> p j d", p=128),
                  wr=["wo"], key="ld_wo")
            for jj in range(gn):
                j = g0 + jj
                if side is not None:
                    k0, k1 = side
                    if k0 + j < k1:
                        mod_issue(k0 + j)
                    if j >= 1 and k0 + j - 1 < k1:
                        mod_compute(k0 + j - 1)
                w = wi[j % NB]
                for t4 in range(4):
                    tsl = slice(t4 * 512, (t4 + 1) * 512)
                    k = step % 2
                    step += 1
                    for half in range(2):
                        b = half * 2 + k
                        for kc in range(8):
                            mm(bank(b), w[:, kc, half * 128:(half + 1) * 128], hT[:, kc, tsl], kc == 0, kc == 7,
                               [("wi", j % NB), ("hT", t4)], [("ps", b)], sig=(kc == 7))
                    act(sa[k], bank(k), AF.Silu, [("ps", k)], [("sa", k)])
                    tt(gT[:, jj, tsl], sa[k], bank(2 + k), ALU.mult, [("sa", k), ("ps", 2 + k)], [("gT", jj)])
                if j + NB < 22:
                    load_wi(j + NB)
            if last and side is not None:
                mod_evac(*side)
                for (ll, ii) in ders:
                    mod_der(ll, ii)
            for t4 in range(4):
                for dc in range(8):
                    if last and tail is not None and t4 >= 1 and dc == 4:
                        tail(t4 - 1)
                    tsl = slice(t4 * 512, (t4 + 1) * 512)
                    b = 4 + (dc + t4) % 2
                    for jj in range(gn):
                        mm(bank(b), wo[:, jj, dc * 128:(dc + 1) * 128], gT[:, jj, tsl], jj == 0, jj == gn - 1,
                           ["wo", ("gT", jj)], [("ps", b)], sig=(jj == gn - 1))
                    stt(xT[:, dc, tsl], bank(b), G[:, dc:dc + 1], xT[:, dc, tsl], ALU.mult, ALU.add,
                        [("ps", b), "der", ("xT", t4)], [("xT", t4)])
            if last and tail is not None:
                tail(3)

    def load_w(dst, src, res, key=None):
        S.dma("pool", dst, src, wr=[res], key=("ld", res))

    osb = [view(SC + i * 4096, [1024], F32) for i in range(2)]
    rrow2 = view(SC + 8192, [1024], F32)
    rhi = view(SC + 18432, [1024], BF16)
    rlo = view(SC + 20480, [1024], BF16)

    def attention(q_fn, kT_fn, v_fn, scale, pT, attn_write, outproj_dc, qprep, shared, dummy=None, nbanks=(6, 7)):
        steps = [(qb, kt) for qb in range(NQB) for kt in range(KT)]
        work = []

        def emit_S(si):
            qb, kt = steps[si]
            sb = si % 2
            if shared:
                for ip in range(2):
                    lhsT, krd = kT_fn(0, kt)
                    rhs, qrd = q_fn(ip, qb)
                    mm(bank(sb * 2 + ip).rearrange("p (a b) -> p a b", a=2), lhsT, rhs, True, True, krd + qrd,
                       [("ps", sb * 2 + ip)], sig=True)
            else:
                for i in range(4):
                    lhsT, krd = kT_fn(i, kt)
                    rhs, qrd = q_fn(i, qb)
                    mm(bank(sb * 2, 2)[:, i * 256:(i + 1) * 256], lhsT, rhs, True, True, krd + qrd,
                       [("ps", sb * 2 + i // 2)], sig=(i % 2 == 1))

        def recip_item(qb, i):
            def f():
                slot = qb % 2
                csl = slice(i * 256, (i + 1) * 256)
                recip(rrow2[64:65, csl], osb[slot][64:65, csl], [("osb", slot)], [("rrow", i)])
                cpy(rhi[64:65, csl], rrow2[64:65, csl], [("rrow", i)], [("rhi", i)])
                tt(rlo[64:65, csl], rrow2[64:65, csl], rhi[64:65, csl], ALU.subtract, [("rrow", i), ("rhi", i)],
                   [("rlo", i)])
            return f

        def norm_item(qb, i):
            def f():
                slot = qb % 2
                csl = slice(i * 256, (i + 1) * 256)
                nb = nbanks[i % len(nbanks)]
                mm(bank(nb)[0:64, 0:256], ones[64:65, 0:64], rhi[64:65, csl], True, False,
                   [("rhi", i), "ones"], [("ps", nb)], sig=False)
                mm(bank(nb)[0:64, 0:256], ones[64:65, 0:64], rlo[64:65, csl], False, True,
                   [("rlo", i), "ones"], [("ps", nb)])
                attn_write(qb, i, osb[slot][0:64, csl], bank(nb)[0:64, 0:256], [("osb", slot), ("ps", nb)])
            return f

        for it in qprep(0):
            it()
        emit_S(0)
        qitems = []
        for si, (qb, kt) in enumerate(steps):
            if kt == 15 and qb + 1 < NQB:
                qitems = list(qprep(qb + 1))
            if kt >= 16 and qitems:
                qitems.pop(0)()
            if si + 1 < len(steps):
                emit_S(si + 1)
            sb = si % 2
            pslot = si % 3
            act(pT[pslot], bank(sb * 2, 2), AF.Exp, [("ps", sb * 2), ("ps", sb * 2 + 1), "cols"], [("pT", pslot)],
                bias=col("abias", qb * KT + kt), scale=scale)
            if shared:
                for ip in range(2):
                    vv, vrd = v_fn(0, kt)
                    mm(bank(4 + ip)[:, :], vv, pT[pslot][:, ip * 512:(ip + 1) * 512], kt == 0, kt == KT - 1,
                       vrd + [("pT", pslot)], [("ps", 4 + ip)], sig=(ip == 1), skip=True)
            else:
                for i in range(4):
                    vv, vrd = v_fn(i, kt)
                    ob = 4 + i // 2
                    mm(bank(ob)[:, (i % 2) * 256:(i % 2 + 1) * 256], vv, pT[pslot][:, i * 256:(i + 1) * 256],
                       (kt == 0 and i % 2 == 0), kt == KT - 1, vrd + [("pT", pslot)], [("ps", ob)],
                       sig=(i == 3), skip=True)
            if dummy is not None:
                dummy()
            if work:
                work.pop(0)()
            if kt == KT - 1:
                slot = qb % 2
                cpy(osb[slot][0:65, 0:512], bank(4)[0:65, :], [("ps", 4)], [("osb", slot)])
                cpy(osb[slot][0:65, 512:1024], bank(5)[0:65, :], [("ps", 5)], [("osb", slot)])
                for i in range(4):
                    work.append(recip_item(qb, i))
                for i in range(4):
                    work.append(norm_item(qb, i))
                for dc in range(8):
                    work.append((lambda qb=qb, dc=dc: outproj_dc(qb, dc)))
        while work:
            work.pop(0)()

    ffn(0, 0, side=(6, 24), ders=[(0, 1), (0, 2), (1, 0)], tail=lambda t4: norm_t4(0, 1, t4))
    checkpoint("ffn00")
    checkpoint("n01")
    Gm0 = der[:, 0, 24:32]
    cqn = view(R, [3, T], BF16)
    ckvn = view(R + 12288, [2, NK], BF16)
    krall = view(R + 22528, [NK], BF16)
    cosT = view(R + 27648, [T], F32)
    sinT = view(R + 35840, [T], F32)
    GLU = R + 44032
    glu = view(GLU, [4, 8, 286], BF16)
    gl = view(GLU, [4, T], BF16)
    WS = R + 62336
    wpc = [view(WS + i * 6144, [8, 384], BF16) for i in range(3)]
    yc = view(HT, [4, T], F32)

    import os
    SK = os.environ.get("KSKIP", "")
    S.fence()
    if "a" not in SK:
        S.dma("sp", cosT, rope_d[0, 0], wr=["cos"], key="ld_cos")
        S.dma("sp", sinT, rope_d[0, 1], wr=["sin"], key="ld_sin")
    if "b" not in SK:
        for c in range(2):
            load_w(ckvn[:, c, 0:NCTX], ckvctx_d[c * 128:(c + 1) * 128, :], "ckvn_ctx", "ld_ctx")
    if "e" not in SK:
        load_w(krall[64:96, 0:NCTX], krctx_d, "kr_ctx", "ld_ctx")
    if "c" not in SK:
        S.op("pool", lambda: nc.gpsimd.memset(glu[:, :, 0, 0:15], 0.0), wr=["glu_pad"])
        S.op("pool", lambda: nc.gpsimd.memset(glu[:, :, 7, 271:286], 0.0), wr=["glu_pad"])

    pieces = [(0, 384), (384, 256), (640, 192)] + [(832 + 256 * i, 256) for i in range(4)]

    def load_piece(pi):
        c0, w_ = pieces[pi]
        load_w(wpc[pi % 3][:, :, 0:w_], wina_d[:, c0:c0 + w_].rearrange("(kc p) n -> p kc n", p=128),
               ("wpc", pi % 3), ("ld_wpc", pi % 3))

    if "d" not in SK:
        for pi in range(3):
            load_piece(pi)

    checkpoint("mla_ld")

    def proj(w, col0, m, t4, b, wres):
        tsl = slice(t4 * 512, (t4 + 1) * 512)
        for kc in range(8):
            mm(bank(b)[0:m, :], w[:, kc, col0:col0 + m], hT[:, kc, tsl], kc == 0, kc == 7, [wres, ("hT", t4)],
               [("ps", b)], sig=(kc == 7))

    for pi, (nch, gname, inv_n) in enumerate([(3, "gql", 1.0 / 384), (2, "gkvl", 1.0 / 256)]):
        w = wpc[pi % 3]
        for t4 in range(4):
            tsl = slice(t4 * 512, (t4 + 1) * 512)
            for c in range(nch):
                proj(w, c * 128, 128, t4, c, ("wpc", pi % 3))
            rms_rstd(lambda c: bank(c), nch, 512, inv_n, [("ps", c_) for c_ in range(nch)], ones, t4 % 2, 0, 4 + t4 % 2)
            for c in range(nch):
                if pi == 0:
                    stt(tmp[c % 2], bank(c), col(gname, c), rt[t4 % 2], ALU.mult, ALU.mult,
                        [("ps", c), "cols", ("rt", t4 % 2)], [("tmp", c % 2)])
                    act(cqn[:, c, tsl], tmp[c % 2], AF.Copy, [("tmp", c % 2)], ["cqn"])
                else:
                    k2 = (t4 * 2 + c) % 2
                    stt(tmp2[k2], bank(c), col(gname, c), rt[t4 % 2], ALU.mult, ALU.mult,
                        [("ps", c), "cols", ("rt", t4 % 2)], [("tmp2", k2)])
                    act(ckvn[:, c, NCTX + t4 * 512:NCTX + (t4 + 1) * 512], tmp2[k2], AF.Copy, [("tmp2", k2)], ["ckvn"])
                    S.dma("sp", ckvT_d[c * 128:(c + 1) * 128, tsl], tmp2[k2], rd=[("tmp2", k2)], key=("st_t2", k2),
                          store=True)
        load_piece(pi + 3)
    checkpoint("mla_p1")
    w = wpc[2]
    for t4 in range(4):
        tsl = slice(t4 * 512, (t4 + 1) * 512)
        proj(w, 0, 96, t4, 0, ("wpc", 2))
        proj(w, 96, 96, t4, 1, ("wpc", 2))
        k2 = t4 % 2
        tt(tmp[0][64:96, :], bank(0)[64:96, :], cosT[64:96, tsl], ALU.mult, [("ps", 0), "cos"], [("tmp", 0)])
        tt(tmp[1][64:96, :], bank(1)[64:96, :], sinT[64:96, tsl], ALU.mult, [("ps", 1), "sin"], [("tmp", 1)])
        tt(tmp2[k2][64:96, :], tmp[0][64:96, :], tmp[1][64:96, :], ALU.add, [("tmp", 0), ("tmp", 1)], [("tmp2", k2)])
        act(krall[64:96, NCTX + t4 * 512:NCTX + (t4 + 1) * 512], tmp2[k2][64:96, :], AF.Copy, [("tmp2", k2)], ["krall"])
        S.dma("sp", krT_d[:, tsl], tmp2[k2][64:96, :], rd=[("tmp2", k2)], key=("st_t2", k2), store=True)
    load_piece(5)
    checkpoint("mla_p2")
    for c in range(4):
        pi = 3 + c
        w = wpc[pi % 3]
        for t4 in range(4):
            k = t4 % 2
            proj(w, 0, 128, t4, k, ("wpc", pi % 3))
            proj(w, 128, 128, t4, 2 + k, ("wpc", pi % 3))
            act(tmp[k], bank(2 + k), AF.Sigmoid, [("ps", 2 + k)], [("tmp", k)])
            tt(glu[:, c, 2 * t4:2 * t4 + 2, 15:271], bank(k).rearrange("p (a b) -> p a b", a=2),
               tmp[k].rearrange("p (a b) -> p a b", a=2), ALU.mult, [("ps", k), ("tmp", k)], [("glu", c)])
        if pi + 3 < 7:
            load_piece(pi + 3)
        ts(glu[:, c, 1:8, 0:15], glu[:, c, 0:7, 256:271], col("hmask"), 0.0, ALU.mult, ALU.add,
           [("glu", c), "cols"], [("glu", c)])
        ts(glu[:, c, 0:7, 271:286], glu[:, c, 1:8, 15:30], col("hmask"), 0.0, ALU.mult, ALU.add,
           [("glu", c), "cols"], [("glu", c)])
    checkpoint("mla_proj")
    wqu = view(WS, [3, 1536], BF16)
    wkvu = view(WS + 9216, [2, 1024], BF16)
    woa = view(WS + 13312, [4, 1024], BF16)
    S.fence()
    load_w(woa, wouta_d[512:1024, :].rearrange("(kc p) n -> p kc n", p=128), "woa", "ld_woa")
    for c in range(4):
        for j in range(31):
            ds_ = (c * 31 + j) % 4
            ts(dg[ds_], ident, col("wdw", c * 31 + j), 0.0, ALU.mult, ALU.add, ["ident", "cols"], [("dg", ds_)])
            for t4 in range(4):
                mm(bank(t4).rearrange("p (a b) -> p a b", a=2), dg[ds_], glu[:, c, 2 * t4:2 * t4 + 2, j:j + 256],
                   j == 0, j == 30, [("dg", ds_), ("glu", c), "glu_pad"], [("ps", t4)], sig=(t4 == 3))
        for t4 in range(4):
            act(yc[:, c, t4 * 512:(t4 + 1) * 512], bank(t4), AF.Identity, [("ps", t4), "cols", "hT"], ["hT"],
                bias=col("bdw", c), scale=1.0)
    for t4 in range(4):
        tsl = slice(t4 * 512, (t4 + 1) * 512)
        for c in range(4):
            act(sq[0][:, c, :], yc[:, c, tsl], AF.Copy, ["hT"], [("sq", 0)])
            act(sq[1][:, c, :], yc[:, c, tsl], AF.Square, ["hT"], [("sq", 1)])
        for c in range(4):
            mm(bank(6), ones, sq[0][:, c, :], c == 0, c == 3, [("sq", 0), "ones"], [("ps", 6)], sig=(c == 3))
        for c in range(4):
            mm(bank(7), ones, sq[1][:, c, :], c == 0, c == 3, [("sq", 1), "ones"], [("ps", 7)], sig=(c == 3))
        mean, m2 = tmp2[0], tmp2[1]
        ts(mean, bank(6), 1.0 / 512, 0.0, ALU.mult, ALU.add, [("ps", 6)], [("tmp2", 0)])
        tt(m2, mean, mean, ALU.mult, [("tmp2", 0)], [("tmp2", 1)])
        stt(m2, bank(7), 1.0 / 512, m2, ALU.mult, ALU.subtract, [("ps", 7), ("tmp2", 1)], [("tmp2", 1)])
        act(rt[0], m2, AF.Ln, [("tmp2", 1), "cols"], [("rt", 0)], bias=col("eps"), scale=1.0)
        act(rt[0], rt[0], AF.Exp, [("rt", 0)], [("rt", 0)], scale=-0.5)
        for c in range(4):
            k = c % 2
            tt(tmp[k], yc[:, c, tsl], mean, ALU.subtract, ["hT", ("tmp2", 0)], [("tmp", k)])
            tt(tmp[k], tmp[k], rt[0], ALU.mult, [("tmp", k), ("rt", 0)], [("tmp", k)])
            act(gl[:, c, tsl], tmp[k], AF.Silu, [("tmp", k), "cols"] + [("glu", cc) for cc in range(4)],
                [("gl", c)], bias=col("bln", c), scale=col("gln", c))
    for dc in range(8):
        for t4 in range(4):
            tsl = slice(t4 * 512, (t4 + 1) * 512)
            b = 6 + (dc * 4 + t4) % 2
            for c in range(4):
                mm(bank(b), woa[:, c, dc * 128:(dc + 1) * 128], gl[:, c, tsl], c == 0, c == 3,
                   ["woa", ("gl", c)], [("ps", b)], sig=(c == 3))
            stt(xT[:, dc, tsl], bank(b), Gm0[:, dc:dc + 1], xT[:, dc, tsl], ALU.mult, ALU.add,
                [("ps", b), "der", ("xT", t4)], [("xT", t4)])
    checkpoint("mla_conv")
    S.fence()
    load_w(wqu, wqu_d.rearrange("(kc p) n -> p kc n", p=128), "wqu", "ld_wqu")
    load_w(wkvu, wkvu_d.rearrange("(kc p) n -> p kc n", p=128), "wkvu", "ld_wkvu")
    Vt = view(HT, [KT, 4, 65], BF16)
    VtF = view(HT, [KT * 4 * 65 + 63], BF16)
    KTt = view(HT + 10496, [4, NK], BF16)
    ATT = GLU
    Qt = [view(ATT + i * 2048, [4, 256], BF16) for i in range(2)]
    pT = [view(ATT + 4096 + i * 2048, [1024], BF16) for i in range(3)]
    attnb = [view(ATT + 10240 + i * 1024, [2, 256], BF16) for i in range(2)]
    gl_res = [("gl", c) for c in range(4)]
    for i_ in range(2):
        S.op("pool", lambda i_=i_: nc.gpsimd.memset(Qt[i_][64:128, :, :], 0.0), wr=[("Qt", i_)])
    for g in range(2):
        load_w(woa[:, 0:2, :], wouta_d[g * 256:(g + 1) * 256, :].rearrange("(kc p) n -> p kc n", p=128), "woa", "ld_woa")
        S.op("pool", lambda: nc.gpsimd.memset(Vt[:, :, :, 64:65], 1.0), rd=["hT"], wr=["Vt1"])
        for kt in range(KT):
            b = 6 + kt % 2
            for c in range(2):
                rhs = wkvu[:, c, g * 512:(g + 1) * 512].rearrange("p (h d) -> p h d", h=4)[:, :, 64:128]
                mm(bank(b)[:, 0:256].rearrange("p (h d) -> p h d", h=4), ckvn[:, c, kt * 128:(kt + 1) * 128], rhs,
                   c == 0, c == 1, ["ckvn", "ckvn_ctx", "wkvu"], [("ps", b)], sig=(c == 1))
            if kt % 2 == 0:
                cpy(Vt[:, kt, :, 0:64], bank(b)[:, 0:256].rearrange("p (h d) -> p h d", h=4), [("ps", b), "hT"], ["Vt"])
            else:
                act(Vt[:, kt, :, 0:64], bank(b)[:, 0:256].rearrange("p (h d) -> p h d", h=4), AF.Copy,
                    [("ps", b), "hT"], ["Vt"])
        for i in range(4):
            h = 4 * g + i
            for k5 in range(5):
                b = 6 + (i * 5 + k5) % 2
                for c in range(2):
                    mm(bank(b)[0:64, :], wkvu[:, c, h * 128:h * 128 + 64], ckvn[:, c, k5 * 512:(k5 + 1) * 512],
                       c == 0, c == 1, ["ckvn", "ckvn_ctx", "wkvu"], [("ps", b)], sig=(c == 1))
                if k5 % 2 == 0:
                    cpy(KTt[0:64, i, k5 * 512:(k5 + 1) * 512], bank(b)[0:64, :], [("ps", b), "hT"], [("KT", i)])
                else:
                    act(KTt[0:64, i, k5 * 512:(k5 + 1) * 512], bank(b)[0:64, :], AF.Copy, [("ps", b), "hT"], [("KT", i)])
            S.op("pool", lambda i=i: nc.gpsimd.memset(KTt[64:128, i, :], 0.0), wr=[("KT", i)])
            cpy(KTt[64:96, i, :], krall[64:96, :], ["krall", "kr_ctx", "hT"], [("KT", i)], e="pool")

        def qprep(qb, g=g):
            return [(lambda i2=i2: qprep_i2(qb, i2)) for i2 in range(2)]

        def qprep_i2(qb, i2, g=g):
            qs = slice(qb * 256, (qb + 1) * 256)
            qt = Qt[qb % 2]
            if True:
                for ii in range(2):
                    i = i2 * 2 + ii
                    h = 4 * g + i
                    for ab in range(2):
                        for c in range(3):
                            mm(bank(6 + ab)[0:96, ii * 256:(ii + 1) * 256],
                               wqu[:, c, ab * 768 + h * 96:ab * 768 + (h + 1) * 96], cqn[:, c, qs], c == 0, c == 2,
                               ["wqu", "cqn"], [("ps", 6 + ab)], sig=(c == 2))
                for ii in range(2):
                    i = i2 * 2 + ii
                    A = bank(6)[:, ii * 256:(ii + 1) * 256]
                    B = bank(7)[:, ii * 256:(ii + 1) * 256]
                    cpy(qt[0:64, i, :], A[0:64, :], [("ps", 6)] + gl_res, [("Qt", qb % 2)])
                    tt(tmp[0][64:96, 0:256], A[64:96, :], cosT[64:96, qs], ALU.mult, [("ps", 6), "cos"], [("tmp", 0)])
                    tt(tmp[1][64:96, 0:256], B[64:96, :], sinT[64:96, qs], ALU.mult, [("ps", 7), "sin"], [("tmp", 1)])
                    tt(qt[64:96, i, :], tmp[0][64:96, 0:256], tmp[1][64:96, 0:256], ALU.add,
                       [("tmp", 0), ("tmp", 1)] + gl_res, [("Qt", qb % 2)])

        def q_fn(i, qb):
            return Qt[qb % 2][:, i, :], [("Qt", qb % 2)]

        def kT_fn(i, kt):
            return KTt[:, i, kt * 128:(kt + 1) * 128], [("KT", i)]

        def v_fn(i, kt):
            o_ = (kt * 4 + i) * 65
            return VtF[:, o_:o_ + 128], ["Vt", "Vt1"]

        def attn_write(qb, i, o_sb, b_ps, rd):
            ab = attnb[qb % 2]
            tt(ab[(i % 2) * 64:(i % 2) * 64 + 64, i // 2, :], o_sb, b_ps, ALU.mult, rd, [("attn", qb % 2)])

        def outproj_dc(qb, dc, g=g):
            qs = slice(qb * 256, (qb + 1) * 256)
            ab = attnb[qb % 2]
            b = 6 + dc % 2
            for pr in range(2):
                mm(bank(b)[:, 0:256], woa[:, pr, dc * 128:(dc + 1) * 128], ab[:, pr, :], pr == 0, pr == 1,
                   ["woa", ("attn", qb % 2)], [("ps", b)], sig=(pr == 1))
            stt(xT[:, dc, qs], bank(b)[:, 0:256], Gm0[:, dc:dc + 1], xT[:, dc, qs], ALU.mult, ALU.add,
                [("ps", b), "der", ("xT", qb // 2)], [("xT", qb // 2)])

        attention(q_fn, kT_fn, v_fn, float(96 ** -0.5), pT, attn_write, outproj_dc, qprep, False)
    checkpoint("mla_attn")
    ffn(0, 1, side=(24, 36), ders=[(1, 1), (1, 2)], tail=lambda t4: norm_t4(1, 0, t4))
    checkpoint("ffn01")

    ffn(1, 0, fence=False, do_norm=False, tail=lambda t4: norm_t4(1, 1, t4))
    checkpoint("ffn10")
    Gm1 = der[:, 1, 24:32]
    QTa = view(R, [8, T], BF16)
    KTa = view(R + 32768, [2, NK], BF16)
    Vg = view(R + 43008, [KT, 4, 65], BF16)
    VgF = view(R + 43008, [KT * 4 * 65 + 63], BF16)
    cos1 = view(R + 53440, [T], F32)
    sin1 = view(R + 61632, [T], F32)
    wsl = [view(R + 69824 + i * 4096, [8, 256], BF16) for i in range(3)]
    S.fence()
    S.dma("sp", cos1, rope_d[1, 0], wr=["cos"], key="ld_cos")
    S.dma("sp", sin1, rope_d[1, 1], wr=["sin"], key="ld_sin")
    for m in range(2):
        load_w(KTa[:, m, 0:NCTX], gkctx_d[m * 128:(m + 1) * 128, :], "KTa_ctx", "ld_ctx")
    vstage = view(SC + 12288, [4, 256], BF16)
    S.dma("pool", vstage, gvctx_d.rearrange("(kt p) n -> p kt n", p=128), wr=[("tmp", 0)], key=("ld", "vstage"))
    for kt_ in range(4):
        cpy(Vg[:, kt_, :, 0:64], vstage[:, kt_, :].rearrange("p (h d) -> p h d", d=64), [("tmp", 0)], ["Vg_ctx"])
    S.op("pool", lambda: nc.gpsimd.memset(Vg[:, :, :, 64:65], 1.0), wr=["Vg1"])

    def load_pc(pi):
        load_w(wsl[pi % 3], winc_d[:, pi * 256:(pi + 1) * 256].rearrange("(kc p) n -> p kc n", p=128),
               ("wsl", pi % 3), ("ld_wsl", pi % 3))

    for pi in range(3):
        load_pc(pi)
    checkpoint("g_ld")
    gtasks = [(pi, t4) for pi in range(10) for t4 in range(4)]

    def g_chains(idx):
        pi, t4 = gtasks[idx]
        k = idx % 2
        proj(wsl[pi % 3], 0, 128, t4, k, ("wsl", pi % 3))
        proj(wsl[pi % 3], 128, 128, t4, 2 + k, ("wsl", pi % 3))
        if t4 == 3 and pi + 3 < 11:
            load_pc(pi + 3)

    def g_post(idx):
        pi, t4 = gtasks[idx]
        k = idx % 2
        isq = pi < 8
        gname, gpname = ("gq", "gqp") if isq else ("gk", "gkp")
        tsl = slice(t4 * 512, (t4 + 1) * 512)
        rms_rstd(lambda c: bank(k), 1, 512, 1.0 / 64, [("ps", k)], blk, k, k, 4 + k)
        stt(tmp[0], bank(k), col(gname), cos1[:, tsl], ALU.mult, ALU.mult, [("ps", k), "cols", "cos"], [("tmp", 0)])
        stt(tmp[1], bank(2 + k), col(gpname), sin1[:, tsl], ALU.mult, ALU.mult, [("ps", 2 + k), "cols", "sin"],
            [("tmp", 1)])
        tt(tmp[0], tmp[0], tmp[1], ALU.add, [("tmp", 0), ("tmp", 1)], [("tmp", 0)])
        if isq:
            tt(QTa[:, pi, tsl], tmp[0], rt[k], ALU.mult, [("tmp", 0), ("rt", k)], ["QTa"])
        else:
            m = pi - 8
            tt(tmp2[k], tmp[0], rt[k], ALU.mult, [("tmp", 0), ("rt", k)], [("tmp2", k)])
            act(KTa[:, m, NCTX + t4 * 512:NCTX + (t4 + 1) * 512], tmp2[k], AF.Copy, [("tmp2", k)], ["KTa"])
            S.dma("sp", gkT_d[m * 128:(m + 1) * 128, tsl], tmp2[k], rd=[("tmp2", k)], key=("st_t2", k), store=True)

    for idx in range(len(gtasks) + 1):
        if idx < len(gtasks):
            g_chains(idx)
        if idx >= 1:
            g_post(idx - 1)
    w = wsl[10 % 3]
    for t16 in range(16):
        b = t16 % 2
        for kc in range(8):
            mm(bank(b)[:, 0:256], hT[:, kc, t16 * 128:(t16 + 1) * 128], w[:, kc, :], kc == 0, kc == 7,
               [("wsl", 10 % 3), ("hT", t16 // 4)], [("ps", b)], sig=(kc == 7))
        if "x" not in SK:
            act(tmp2[b][:, 0:256], bank(b)[:, 0:256], AF.Copy, [("ps", b)], [("tmp2", b)])
        if "w" not in SK:
            S.dma("sp", gv_d[t16 * 128:(t16 + 1) * 128, :], tmp2[b][:, 0:256], rd=[("tmp2", b)], key=("st_t2", b), store=True)
        if "y" not in SK:
            cpy(Vg[:, 4 + t16, :, 0:64], bank(b)[:, 0:256].rearrange("p (h d) -> p h d", h=4), [("ps", b)], ["Vg"])
    checkpoint("gqa_proj")
    S.fence()
    woc = view(HT, [8, 1024], BF16)
    load_w(woc, woutc_d.rearrange("(kc p) n -> p kc n", p=128), "hT", "ld_woc")
    ATT1 = R + 53440
    pT1 = [view(ATT1 + i * 2048, [1024], BF16) for i in range(3)]
    attn1 = [view(ATT1 + 6144 + i * 2048, [4, 256], BF16) for i in range(2)]
    for g in range(4):
        m, r = g // 2, g % 2

        Qz = [view(ATT1 + 10240 + i * 2048, [4, 256], BF16) for i in range(2)]
        for i_ in range(2):
            S.op("pool", lambda i_=i_, r=r: nc.gpsimd.memset(Qz[i_][(1 - r) * 64:(1 - r) * 64 + 64, :, :], 0.0),
                 wr=[("Qz", i_)])

        def qprep(qb, m=m, r=r):
            return [lambda: qprep1(qb)]

        def qprep1(qb, m=m, r=r):
            cpy(Qz[qb % 2][r * 64:r * 64 + 64, :, :], QTa[r * 64:r * 64 + 64, 4 * m:4 * m + 4, qb * 256:(qb + 1) * 256],
                ["QTa"], [("Qz", qb % 2)], e="pool")

        def q_fn(ip, qb):
            return Qz[qb % 2][:, 2 * ip:2 * ip + 2, :], [("Qz", qb % 2)]

        def kT_fn(i, kt, m=m):
            return KTa[:, m, kt * 128:(kt + 1) * 128], ["KTa", "KTa_ctx"]

        def v_fn(i, kt, g=g):
            o_ = (kt * 4 + g) * 65
            return VgF[:, o_:o_ + 128], ["Vg", "Vg1", "Vg_ctx"]

        def attn_write(qb, i, o_sb, b_ps, rd, r=r):
            ab = attn1[qb % 2]
            tt(ab[r * 64:r * 64 + 64, i, :], o_sb, b_ps, ALU.mult, rd, [("attn", qb % 2)])

        def outproj_dc(qb, dc, m=m, r=r):
            qs = slice(qb * 256, (qb + 1) * 256)
            ab = attn1[qb % 2]
            b = 6 + dc % 2
            for i in range(4):
                mm(bank(b)[:, 0:256], woc[r * 64:r * 64 + 64, 4 * m + i, dc * 128:(dc + 1) * 128],
                   ab[r * 64:r * 64 + 64, i, :], i == 0, i == 3, ["hT", ("attn", qb % 2)], [("ps", b)], sig=(i == 3))
            stt(xT[:, dc, qs], bank(b)[:, 0:256], Gm1[:, dc:dc + 1], xT[:, dc, qs], ALU.mult, ALU.add,
                [("ps", b), "der", ("xT", qb // 2)], [("xT", qb // 2)])

        NDUM = int(os.environ.get("KDUM", "0"))

        def dummy(m=m):
            for _ in range(NDUM):
                mm(bank(6)[:, 0:256], woc[:, 0, 0:128], QTa[:, 0, 0:256], True, True, ["hT", "QTa"], [("ps", 6)],
                   sig=False)

        attention(q_fn, kT_fn, v_fn, float(64 ** -0.5), pT1, attn_write, outproj_dc, qprep, True,
                  dummy=dummy if NDUM else None, nbanks=(6, 7) if not NDUM else (7,))
    checkpoint("gqa_attn")
    ffn(1, 1, tail=final_t4)
    checkpoint("ffn11")

    S.finish()


def _prep_shared(inp):
    f = np.float32
    sh = {}
    sh["wmod"] = np.ascontiguousarray(inp["w_mod"], dtype=f)
    sh["ident"] = np.eye(128, dtype=f)
    wffin = np.empty((4, 22, 128, 2048), f)
    wffout = np.empty((4, DFF, D), f)
    for l in range(2):
        for fi, (ni, no) in enumerate([("w_ff1_in", "w_ff1_out"), ("w_ff2_in", "w_ff2_out")]):
            w = np.asarray(inp[ni][l], f)
            a = w[:, :DFF].reshape(8, 128, 22, 128)
            b = w[:, DFF:].reshape(8, 128, 22, 128)
            ab = np.stack([a, b], axis=3)
            wffin[l * 2 + fi] = ab.transpose(2, 1, 0, 3, 4).reshape(22, 128, 2048)
            wffout[l * 2 + fi] = np.asarray(inp[no][l], f)
    sh["wffin"], sh["wffout"] = wffin, wffout
    wa = np.asarray(inp["w_in_a"][0], f)
    p32, _, _, _, _ = _rope_meta(32)
    krA = np.zeros((D, 96), f); krB = np.zeros((D, 96), f)
    krA[:, 64:] = wa[:, 640:672]
    krB[:, 64:] = wa[:, 640:672][:, p32]
    u = wa[:, 672:]
    ab = [np.concatenate([u[:, c * 128:(c + 1) * 128], u[:, 512 + c * 128:512 + (c + 1) * 128]], 1) for c in range(4)]
    sh["wina"] = np.ascontiguousarray(np.concatenate([wa[:, 0:640], krA, krB] + ab, 1))
    wq = np.asarray(inp["w_q_up"][0], f)
    wqB = np.zeros_like(wq)
    for h in range(8):
        wqB[:, h * 96 + 64:(h + 1) * 96] = wq[:, h * 96 + 64:(h + 1) * 96][:, p32]
    sh["wqu2"] = np.ascontiguousarray(np.concatenate([wq, wqB], 1))
    sh["wkvu"] = np.ascontiguousarray(inp["w_kv_up"][0], dtype=f)
    sh["wouta"] = np.ascontiguousarray(inp["w_out_a"][0], dtype=f)
    wc = np.asarray(inp["w_in_c"][0], f)
    p64, _, _, _, _ = _rope_meta(64)
    pcs = []
    for m in range(2):
        for j in range(4):
            hs = [8 * m + j, 8 * m + 4 + j]
            A = np.concatenate([wc[:, h * 64:(h + 1) * 64] for h in hs], 1)
            B = np.concatenate([wc[:, h * 64:(h + 1) * 64][:, p64] for h in hs], 1)
            pcs += [A, B]
    for m in range(2):
        hs = [2 * m, 2 * m + 1]
        A = np.concatenate([wc[:, 1024 + h * 64:1024 + (h + 1) * 64] for h in hs], 1)
        B = np.concatenate([wc[:, 1024 + h * 64:1024 + (h + 1) * 64][:, p64] for h in hs], 1)
        pcs += [A, B]
    pcs.append(wc[:, 1280:1536])
    sh["winc"] = np.ascontiguousarray(np.concatenate(pcs, 1))
    wo = np.asarray(inp["w_out_c"][0], f)
    wop = np.empty_like(wo)
    for m in range(2):
        for j in range(4):
            for r in range(2):
                h = 8 * m + 4 * r + j
                wop[(4 * m + j) * 128 + r * 64:(4 * m + j) * 128 + r * 64 + 64] = wo[h * 64:(h + 1) * 64]
    sh["woutc"] = wop
    return sh


def _colT(v, n):
    return np.asarray(v, np.float32).reshape(n, 128).T


def _prep_core(inp, core, sh):
    f = np.float32
    sample = core >= 4
    m = dict(sh)
    if sample:
        b = core - 4
        x = np.asarray(inp["x_sample"][b], f)
        cond = np.asarray(inp["c"][b], f)
        m["ckvctxT"] = np.ascontiguousarray(np.asarray(inp["cache_mla_ckv"][b, 0], f).T)
        m["krctxT"] = np.ascontiguousarray(np.asarray(inp["cache_mla_krope"][b, 0], f).T)
        m["gkctxT"] = np.ascontiguousarray(np.asarray(inp["cache_gqa_k"][b, 0], f).reshape(NCTX, 256).T)
        m["gvctx"] = np.ascontiguousarray(np.asarray(inp["cache_gqa_v"][b, 0], f).reshape(NCTX, 256))
    else:
        x = np.asarray(inp["x_prompt"][core * 8:(core + 1) * 8], f).reshape(T, D)
        cond = np.asarray(inp["c_ctx"], f)
        m["ckvctxT"] = np.zeros((256, NCTX), f)
        m["krctxT"] = np.zeros((32, NCTX), f)
        m["gkctxT"] = np.zeros((256, NCTX), f)
        m["gvctx"] = np.zeros((NCTX, 256), f)
    m["xT"] = np.ascontiguousarray(x.T)
    cols = np.zeros((128, NCOLS), f)

    def put(name, arr):
        arr = np.asarray(arr, f)
        cols[:, _COLS[name]:_COLS[name] + arr.shape[1]] = arr

    put("cond", _colT(cond, 8))
    for l in range(2):
        put("gff1_%d" % l, _colT(inp["g_ff1"][l], 8))
        put("gmix_%d" % l, _colT(inp["g_mix"][l], 8))
        put("gff2_%d" % l, _colT(inp["g_ff2"][l], 8))
    put("gfinal", _colT(inp["g_final"], 8))
    put("gql", _colT(inp["g_q_lora"][0], 3))
    put("gkvl", _colT(inp["g_kv_lora"][0], 2))
    put("bdw", _colT(inp["b_dw"][0], 4))
    put("gln", _colT(inp["g_conv_ln"][0], 4))
    put("bln", _colT(inp["b_conv_ln"][0], 4))
    wdw = np.asarray(inp["w_dw"][0], f)
    put("wdw", wdw.T.reshape(4, 128, 31).transpose(1, 0, 2).reshape(128, 124))
    p64, _, _, _, _ = _rope_meta(64)
    gq = np.asarray(inp["g_q_head"][0], f); gk = np.asarray(inp["g_k_head"][0], f)
    put("gq", np.tile(gq, 2)[:, None]); put("gqp", np.tile(gq[p64], 2)[:, None])
    put("gk", np.tile(gk, 2)[:, None]); put("gkp", np.tile(gk[p64], 2)[:, None])
    put("hmask", np.full((128, 1), 1.0 if sample else 0.0, f))
    put("eps", np.full((128, 1), EPS, f))
    bm = np.asarray(inp["b_mod"], f)
    put("bmod", np.concatenate([_colT(bm[0], 72), _colT(bm[1], 72)], 1))
    ab = np.zeros((128, NQB, KT), f)
    if not sample:
        ab[:] = -30000.0
        for qb in range(NQB):
            ab[:, qb, 4 + 2 * qb:4 + 2 * qb + 2] = 0.0
    put("abias", ab.reshape(128, NQB * KT))
    m["cols"] = cols
    rope = np.zeros((2, 2, 128, T), f)
    c32, s32 = _rope_tables(32, sample)
    rope[0, 0, 64:96], rope[0, 1, 64:96] = c32, s32
    c64, s64 = _rope_tables(64, sample)
    rope[1, 0] = np.concatenate([c64, c64], 0)
    rope[1, 1] = np.concatenate([s64, s64], 0)
    m["rope"] = rope
    return m


_NC_CACHE = {}


def kernel(**inputs):
    inp = {k: np.asarray(v) for k, v in inputs.items()}
    if "nc" not in _NC_CACHE:
        _NC_CACHE["nc"] = build_nc()
    nc = _NC_CACHE["nc"]
    sh = _prep_shared(inp)
    in_maps = [_prep_core(inp, c, sh) for c in range(8)]
    res = run_bass_kernel_spmd(nc, in_maps, core_ids=list(range(8)))
    r = res.results
    f = np.float32
    y_prompt = np.concatenate([np.asarray(r[c]["yT"], f).T.reshape(8, 256, D) for c in range(4)], 0)
    y_sample = np.stack([np.asarray(r[c]["yT"], f).T for c in range(4, 8)], 0)
    ckv = np.concatenate([np.asarray(r[c]["ckvT"], f).T.reshape(8, 1, 256, 256) for c in range(4)], 0)
    kr = np.concatenate([np.asarray(r[c]["krT"], f).T.reshape(8, 1, 256, 32) for c in range(4)], 0)
    gk = np.concatenate([np.asarray(r[c]["gkT"], f).T.reshape(8, 1, 256, 4, 64) for c in range(4)], 0)
    gvv = np.concatenate([np.asarray(r[c]["gv"], f).reshape(8, 1, 256, 4, 64) for c in range(4)], 0)
    return (np.ascontiguousarray(y_prompt), np.ascontiguousarray(y_sample), np.ascontiguousarray(ckv),
            np.ascontiguousarray(kr), np.ascontiguousarray(gk), np.ascontiguousarray(gvv))
```

```python
import numpy as np
import concourse.bass as bass
import concourse.mybir as mybir
from concourse.bass_utils import run_bass_kernel_spmd

F32 = mybir.dt.float32
BF16 = mybir.dt.bfloat16
ALU = mybir.AluOpType
AF = mybir.ActivationFunctionType

T = 2048
D = 1024
NCTX = 512
NK = NCTX + T
KT = NK // 128
NQB = T // 256
DFF = 2816
EPS = 1e-6

_COLS = {}
_o = 0
for _n, _w in [("cond", 8), ("gff1_0", 8), ("gmix_0", 8), ("gff2_0", 8), ("gff1_1", 8), ("gmix_1", 8),
               ("gff2_1", 8), ("gfinal", 8), ("gql", 3), ("gkvl", 2), ("bdw", 4), ("gln", 4), ("bln", 4),
               ("wdw", 124), ("gq", 1), ("gqp", 1), ("gk", 1), ("gkp", 1), ("hmask", 1), ("eps", 1),
               ("bmod", 144), ("abias", 160)]:
    _COLS[_n] = _o
    _o += _w
NCOLS = _o


def _rope_meta(d):
    half = d // 2
    nf = half // 2
    partner = np.zeros(d, np.int64)
    sign = np.zeros(d, np.float32)
    sec = np.zeros(d, np.int64)
    fr = np.zeros(d, np.int64)
    for i in range(d):
        s, ii = i // half, i % half
        f, part = ii % nf, ii // nf
        partner[i] = i + nf if part == 0 else i - nf
        sign[i] = -1.0 if part == 0 else 1.0
        sec[i] = s
        fr[i] = f
    inv = (np.float32(10000.0) ** (-np.arange(0, half, 2, dtype=np.float32) / np.float32(half))).astype(np.float32)
    return partner, sign, sec, fr, inv


def _rope_tables(d, sample):
    partner, sign, sec, fr, inv = _rope_meta(d)
    if not sample:
        return np.ones((d, T), np.float32), np.zeros((d, T), np.float32)
    t = np.arange(T)
    pos = np.stack([(t // 64).astype(np.float32), (t % 64).astype(np.float32)], 0)
    ang = pos[sec] * inv[fr][:, None]
    ang = ang.astype(np.float32)
    return np.cos(ang).astype(np.float32), (np.sin(ang).astype(np.float32) * sign[:, None]).astype(np.float32)


def build_nc():
    nc = bass.Bass("TRN2", target_bir_lowering=False)
    try:
        _build(nc)
    except Exception as ex:
        if type(ex).__name__ != "_Stop":
            raise
    return nc


def _build(nc):

    def din(name, shape):
        return nc.dram_tensor(name, list(shape), F32, kind="ExternalInput").ap()

    def dout(name, shape):
        return nc.dram_tensor(name, list(shape), F32, kind="ExternalOutput").ap()

    xT_d = din("xT", [D, T])
    cols_d = din("cols", [128, NCOLS])
    wmod_d = din("wmod", [2, D, 9216])
    wffin_d = din("wffin", [4, 22, 128, 2048])
    wffout_d = din("wffout", [4, DFF, D])
    wina_d = din("wina", [D, 1856])
    wqu_d = din("wqu2", [384, 1536])
    wkvu_d = din("wkvu", [256, 1024])
    wouta_d = din("wouta", [D, D])
    winc_d = din("winc", [D, 2816])
    woutc_d = din("woutc", [D, D])
    rope_d = din("rope", [2, 2, 128, T])
    ckvctx_d = din("ckvctxT", [256, NCTX])
    krctx_d = din("krctxT", [32, NCTX])
    gkctx_d = din("gkctxT", [256, NCTX])
    gvctx_d = din("gvctx", [NCTX, 256])
    ident_d = din("ident", [128, 128])
    yT_d = dout("yT", [D, T])
    ckvT_d = dout("ckvT", [256, T])
    krT_d = dout("krT", [32, T])
    gkT_d = dout("gkT", [256, T])
    gv_d = dout("gv", [T, 256])

    ARENA = 212736
    arena = nc.alloc_sbuf_tensor("arena", [128, ARENA], mybir.dt.uint8).ap()

    def view(off, shape, dt):
        esz = 4 if dt == F32 else 2
        n = int(np.prod(shape)) * esz
        assert off % 4 == 0 and off + n <= ARENA, (off, n)
        v = arena[:, off:off + n].bitcast(dt)
        if len(shape) == 2:
            v = v.rearrange("p (a b) -> p a b", a=shape[0])
        elif len(shape) == 3:
            v = v.rearrange("p (a b c) -> p a b c", a=shape[0], b=shape[1])
        return v

    ps = nc.alloc_psum_tensor("ps", [128, 4096], F32).ap()

    def bank(b, n=1):
        return ps[:, b * 512:(b + n) * 512]

    class Sch:
        def __init__(s):
            s.engs = {"pe": nc.tensor, "act": nc.scalar, "dve": nc.vector, "pool": nc.gpsimd, "sp": nc.sync}
            s.sems, s.cnt = {}, {}
            s.waited = {e: {} for e in s.engs}
            s.lastw, s.rds = {}, {}
            s.pend = {e: [] for e in s.engs}
            s.stores = set()
            s.floor = None

        def sem(s, key):
            if key not in s.sems:
                s.sems[key] = nc.alloc_semaphore("s%d" % len(s.sems))
                s.cnt[key] = 0
            return s.sems[key]

        def _deps(s, e, rd, wr):
            deps = {}

            def add(ev):
                if ev is None:
                    return
                k, v = ev
                if v is None:
                    assert k == e == "pe", ("dependency on unsignalled op", k, e)
                    return
                if deps.get(k, 0) < v:
                    deps[k] = v

            add(s.floor)
            for r in rd:
                add(s.lastw.get(r))
            for w in wr:
                add(s.lastw.get(w))
                for ev in s.rds.get(w, {}).values():
                    add(ev)
            for k, v in deps.items():
                if k == e and e == "pe":
                    continue
                if s.waited[e].get(k, 0) >= v:
                    continue
                s.engs[e].wait_ge(s.sem(k), v)
                s.waited[e][k] = v

        def _record(s, ev, rd, wr):
            for r in rd:
                s.rds.setdefault(r, {})[ev[0]] = ev
            for w in wr:
                s.lastw[w] = ev
                s.rds[w] = {}

        @staticmethod
        def _exp(lst):
            out = []
            for r in lst:
                if r == "hT" or r == "xT":
                    out += [(r, t) for t in range(4)]
                else:
                    out.append(r)
            return out

        def op(s, e, fn, rd=(), wr=(), sig=True):
            rd, wr = s._exp(rd), s._exp(wr)
            psr = [r for r in rd if isinstance(r, tuple) and r[0] == "ps"]
            if psr:
                rd = [r for r in rd if not (isinstance(r, tuple) and r[0] == "ps")]
                wr = list(wr) + psr
            s._deps(e, rd, wr)
            inst = fn()
            ev = [e, None]
            s.pend[e].append(ev)
            s._record(ev, rd, wr)
            if sig:
                s.sem(e)
                s.cnt[e] += 1
                inst.then_inc(s.sems[e], 1)
                for p in s.pend[e]:
                    p[1] = s.cnt[e]
                s.pend[e] = []
            return inst

        def dma(s, q, out, in_, rd=(), wr=(), key=None, store=False):
            rd, wr = s._exp(rd), s._exp(wr)
            s._deps(q, rd, wr)
            sm = s.sem(key)
            s.cnt[key] += 16
            s.engs[q].dma_start(out=out, in_=in_).then_inc(sm, 16)
            ev = [key, s.cnt[key]]
            s._record(ev, rd, wr)
            if store:
                s.stores.add(key)

        def fence(s):
            res = sorted(set(s.lastw.keys()) | set(s.rds.keys()), key=str)
            s.op("dve", lambda: nc.vector.memset(fcell, 0.0), rd=(), wr=res)
            s.floor = s.lastw[res[0]]

        def finish(s):
            for key in sorted(s.stores, key=str):
                nc.sync.wait_ge(s.sems[key], s.cnt[key])

    S = Sch()

    class _Stop(Exception):
        pass

    def checkpoint(name):
        import os
        if os.environ.get("KSTOP", "") == name:
            for c in range(8):
                S.dma("sp", yT_d[c * 128:(c + 1) * 128, :], xT[:, c, :], rd=["xT"], key="st_dbg", store=True)
            S.finish()
            raise _Stop()

    def mm(out, lhsT, rhs, start, stop, rd, wr, sig=True, skip=False):
        if skip:
            return S.op("pe", lambda: nc.tensor.matmul(out, lhsT, rhs, start=start, stop=stop, skip_group_check=True),
                        rd, wr, sig)
        return S.op("pe", lambda: nc.tensor.matmul(out, lhsT, rhs, start=start, stop=stop), rd, wr, sig)

    def act(out, in_, func, rd, wr, bias=None, scale=None):
        kw = {}
        if bias is not None:
            kw["bias"] = bias
        if scale is not None:
            kw["scale"] = scale
        return S.op("act", lambda: nc.scalar.activation(out=out, in_=in_, func=func, **kw), rd, wr)

    def tt(out, a, b, op, rd, wr, e="dve"):
        return S.op(e, lambda: S.engs[e].tensor_tensor(out, a, b, op), rd, wr)

    def ts(out, a, s1, s2, op0, op1, rd, wr, e="dve"):
        return S.op(e, lambda: S.engs[e].tensor_scalar(out, a, s1, s2, op0, op1), rd, wr)

    def stt(out, a, sc, b, op0, op1, rd, wr):
        return S.op("dve", lambda: nc.vector.scalar_tensor_tensor(out, a, sc, b, op0, op1), rd, wr)

    def recip(out, a, rd, wr):
        return S.op("dve", lambda: nc.vector.reciprocal(out, a), rd, wr)

    def cpy(out, a, rd, wr, e="dve"):
        return S.op(e, lambda: S.engs[e].tensor_copy(out, a), rd, wr)

    XT = 0
    HT = 65536
    CONST = 98304
    SC = 102400
    R = 126976
    RSZ = ARENA - R
    xT = view(XT, [8, T], F32)
    hT = view(HT, [8, T], BF16)
    cols = view(CONST, [NCOLS], F32)
    o = CONST + 2304
    ones = view(o, [128], BF16); o += 256
    blk = view(o, [128], BF16); o += 256
    ones32 = view(o, [64], F32); o += 256
    modT = view(o, [2, 72], F32); o += 576
    der = view(o, [2, 48], F32); o += 384
    e_bf = view(o, [8], BF16); o += 32
    fcell = view(o, [1], F32); o += 32
    assert o <= SC
    sq = [view(SC + i * 4096, [4, 512], BF16) for i in range(2)]
    rt = [view(SC + 8192 + i * 2048, [512], F32) for i in range(2)]
    tmp = [view(SC + 12288 + i * 2048, [512], F32) for i in range(2)]
    sa = [view(SC + 16384 + i * 1024, [512], BF16) for i in range(2)]
    tmp2 = [view(SC + 18432 + i * 2048, [512], F32) for i in range(2)]
    ident = view(SC + 22528, [128], BF16)
    dg = [view(SC + 22784 + i * 256, [128], BF16) for i in range(4)]
    rrow = view(SC, [1024], F32)
    bcs = [view(SC + 4096 + i * 1024, [256], F32) for i in range(2)]

    def col(name, i=0, n=1):
        o_ = _COLS[name] + i
        return cols[:, o_:o_ + n]

    S.dma("sp", cols, cols_d, wr=["cols"], key="ld_cols")
    for t4 in range(4):
        S.dma("sp", xT[:, :, t4 * 512:(t4 + 1) * 512],
              xT_d[:, t4 * 512:(t4 + 1) * 512].rearrange("(c p) t -> p c t", p=128),
              wr=[("xT", t4)], key=("ld_x", t4))
    S.op("dve", lambda: nc.vector.memset(ones, 1.0), wr=["ones"])
    S.op("dve", lambda: nc.vector.memset(blk, 0.0), wr=["blk"])
    S.op("dve", lambda: nc.vector.memset(blk[0:64, 0:64], 1.0), wr=["blk"])
    S.op("dve", lambda: nc.vector.memset(blk[64:128, 64:128], 1.0), wr=["blk"])
    S.op("dve", lambda: nc.vector.memset(ones32, 1.0), wr=["ones32"])
    act(e_bf, col("cond", 0, 8), AF.Silu, ["cols"], ["e"])
    S.dma("pool", ident, ident_d, wr=["ident"], key=("ld", "ident"))

    WM = [view(R + 65536 + i * 8192, [8, 512], BF16) for i in range(2)]
    mod_jobs = [(l, p) for l in range(2) for p in range(18)]

    def mod_issue(k):
        l, p = mod_jobs[k]
        slot = k % 2
        S.dma("pool", WM[slot], wmod_d[l, :, p * 512:(p + 1) * 512].rearrange("(kc p) n -> p kc n", p=128),
              wr=[("wm", slot)], key=("ld_wm", slot))

    def mod_compute(k):
        l, p = mod_jobs[k]
        slot = k % 2
        for c in range(4):
            cc = l * 72 + p * 4 + c
            for kc in range(8):
                mm(bank(7)[:, cc:cc + 1], WM[slot][:, kc, c * 128:(c + 1) * 128], e_bf[:, kc:kc + 1],
                   kc == 0, kc == 7, [("wm", slot), "e"], [("ps", 7)], sig=(kc == 7))

    def mod_evac(k0, k1):
        for l in range(2):
            ps_ = [p for (ll, p) in mod_jobs[k0:k1] if ll == l]
            if not ps_:
                continue
            c0, c1 = min(ps_) * 4, (max(ps_) + 1) * 4
            tt(modT[:, l, c0:c1], bank(7)[:, l * 72 + c0:l * 72 + c1], col("bmod", l * 72 + c0, c1 - c0), ALU.add,
               [("ps", 7), "cols"], ["mod"])

    def mod_der(l, i):
        gname, sci, gi, gs = [("gff1_%d" % l, 1, 2, 0.5), ("gmix_%d" % l, 4, 5, 1.0), ("gff2_%d" % l, 7, 8, 0.5)][i]
        A = der[:, l, i * 16:i * 16 + 8]
        G = der[:, l, i * 16 + 8:i * 16 + 16]
        ts(A, modT[:, l, sci * 8:sci * 8 + 8], 1.0, 1.0, ALU.mult, ALU.add, ["mod"], ["der"])
        tt(A, A, col(gname, 0, 8), ALU.mult, ["der", "cols"], ["der"])
        ts(G, modT[:, l, gi * 8:gi * 8 + 8], gs, 0.0, ALU.mult, ALU.add, ["mod"], ["der"])

    mod_issue(0)
    for k in range(6):
        if k + 1 < 6:
            mod_issue(k + 1)
        mod_compute(k)
    mod_evac(0, 6)
    mod_der(0, 0)

    checkpoint("mod")

    def rms_rstd(src_fn, nch, ncols, inv_n, rd, lhs, rti, sqi, psb):
        for c0 in range(0, nch, 4):
            n = min(4, nch - c0)
            sqt = sq[(sqi + c0 // 4) % 2]
            for c in range(n):
                act(sqt[:, c, 0:ncols], src_fn(c0 + c), AF.Square, rd, [("sq", (sqi + c0 // 4) % 2)])
            for c in range(n):
                mm(bank(psb)[:, 0:ncols], lhs, sqt[:, c, 0:ncols], (c0 + c) == 0, (c0 + c) == nch - 1,
                   [("sq", (sqi + c0 // 4) % 2), "ones", "blk"], [("ps", psb)], sig=((c0 + c) == nch - 1 or c == n - 1))
        act(rt[rti][:, 0:ncols], bank(psb)[:, 0:ncols], AF.Ln, [("ps", psb), "cols"], [("rt", rti)],
            bias=col("eps"), scale=inv_n)
        act(rt[rti][:, 0:ncols], rt[rti][:, 0:ncols], AF.Exp, [("rt", rti)], [("rt", rti)], scale=-0.5)

    def norm_t4(l, which, t4):
        A = der[:, l, which * 16:which * 16 + 8]
        shi = [0, 3, 6][which]
        tsl = slice(t4 * 512, (t4 + 1) * 512)
        rms_rstd(lambda c: xT[:, c, tsl], 8, 512, 1.0 / D, [("xT", t4)], ones, t4 % 2, 0, 6 + t4 % 2)
        for kc in range(8):
            k2 = kc % 2
            stt(tmp[k2], xT[:, kc, tsl], A[:, kc:kc + 1], rt[t4 % 2], ALU.mult, ALU.mult,
                [("xT", t4), "der", ("rt", t4 % 2)], [("tmp", k2)])
            act(hT[:, kc, tsl], tmp[k2], AF.Identity, [("tmp", k2), "mod"], [("hT", t4)],
                bias=modT[:, l, shi * 8 + kc:shi * 8 + kc + 1], scale=1.0)

    def norm_mod(l, which):
        for t4 in range(4):
            norm_t4(l, which, t4)

    def final_t4(t4):
        tsl = slice(t4 * 512, (t4 + 1) * 512)
        rms_rstd(lambda c: xT[:, c, tsl], 8, 512, 1.0 / D, [("xT", t4)], ones, t4 % 2, 0, 6 + t4 % 2)
        for kc in range(8):
            k2 = kc % 2
            stt(tmp2[k2], xT[:, kc, tsl], col("gfinal", kc), rt[t4 % 2], ALU.mult, ALU.mult,
                [("xT", t4), "cols", ("rt", t4 % 2)], [("tmp2", k2)])
            S.dma("sp", yT_d[kc * 128:(kc + 1) * 128, tsl], tmp2[k2], rd=[("tmp2", k2)], key=("st_t2", k2), store=True)

    def ffn(l, f, side=None, ders=(), fence=True, do_norm=True, tail=None):
        which = 0 if f == 0 else 2
        if fence:
            S.fence()
        if do_norm:
            norm_mod(l, which)
        G = der[:, l, which * 16 + 8:which * 16 + 16]
        wi_idx = l * 2 + f
        NB = 4
        gT = view(R, [8, T], BF16)
        wi = [view(R + 32768 + i * 4096, [8, 256], BF16) for i in range(NB)]
        wo = view(R + 49152, [8, 1024], BF16)

        def load_wi(j):
            S.dma("pool", wi[j % NB], wffin_d[wi_idx, j].rearrange("p (kc n) -> p kc n", kc=8),
                  wr=[("wi", j % NB)], key=("ld_wi", j % NB))

        for j in range(NB):
            load_wi(j)
        step = 0
        for (g0, gn) in [(0, 8), (8, 7), (15, 7)]:
            last = (g0 == 15)
            S.dma("pool", wo[:, 0:gn, :], wffout_d[wi_idx, g0 * 128:(g0 + gn) * 128, :].rearrange("(j p) d -> p j d", p=128),
                  wr=["wo"], key="ld_wo")
            for jj in range(gn):
                j = g0 + jj
                if side is not None:
                    k0, k1 = side
                    if k0 + j < k1:
                        mod_issue(k0 + j)
                    if j >= 1 and k0 + j - 1 < k1:
                        mod_compute(k0 + j - 1)
                w = wi[j % NB]
                for t4 in range(4):
                    tsl = slice(t4 * 512, (t4 + 1) * 512)
                    k = step % 2
                    step += 1
                    for half in range(2):
                        b = half * 2 + k
                        for kc in range(8):
                            mm(bank(b), w[:, kc, half * 128:(half + 1) * 128], hT[:, kc, tsl], kc == 0, kc == 7,
                               [("wi", j % NB), ("hT", t4)], [("ps", b)], sig=(kc == 7))
                    act(sa[k], bank(k), AF.Silu, [("ps", k)], [("sa", k)])
                    tt(gT[:, jj, tsl], sa[k], bank(2 + k), ALU.mult, [("sa", k), ("ps", 2 + k)], [("gT", jj)])
                if j + NB < 22:
                    load_wi(j + NB)
            if last and side is not None:
                mod_evac(*side)
                for (ll, ii) in ders:
                    mod_der(ll, ii)
            for t4 in range(4):
                for dc in range(8):
                    if last and tail is not None and t4 >= 1 and dc == 4:
                        tail(t4 - 1)
                    tsl = slice(t4 * 512, (t4 + 1) * 512)
                    b = 4 + (dc + t4) % 2
                    for jj in range(gn):
                        mm(bank(b), wo[:, jj, dc * 128:(dc + 1) * 128], gT[:, jj, tsl], jj == 0, jj == gn - 1,
                           ["wo", ("gT", jj)], [("ps", b)], sig=(jj == gn - 1))
                    stt(xT[:, dc, tsl], bank(b), G[:, dc:dc + 1], xT[:, dc, tsl], ALU.mult, ALU.add,
                        [("ps", b), "der", ("xT", t4)], [("xT", t4)])
            if last and tail is not None:
                tail(3)

    def load_w(dst, src, res, key=None):
        S.dma("pool", dst, src, wr=[res], key=("ld", res))

    osb = [view(SC + i * 4096, [1024], F32) for i in range(2)]
    rrow2 = view(SC + 8192, [1024], F32)

    def attention(q_fn, kT_fn, v_fn, scale, pT, attn_write, outproj_dc, qprep, shared, dummy=None, nbanks=(6, 7),
                  work=None, flush=True):
        steps = [(qb, kt) for qb in range(NQB) for kt in range(KT)]
        if work is None:
            work = []

        def emit_S(si):
            qb, kt = steps[si]
            sb = si % 2
            if shared:
                for ip in range(2):
                    lhsT, krd = kT_fn(0, kt)
                    rhs, qrd = q_fn(ip, qb)
                    mm(bank(sb * 2 + ip).rearrange("p (a b) -> p a b", a=2), lhsT, rhs, True, True, krd + qrd,
                       [("ps", sb * 2 + ip)], sig=True)
            else:
                for i in range(4):
                    lhsT, krd = kT_fn(i, kt)
                    rhs, qrd = q_fn(i, qb)
                    mm(bank(sb * 2, 2)[:, i * 256:(i + 1) * 256], lhsT, rhs, True, True, krd + qrd,
                       [("ps", sb * 2 + i // 2)], sig=(i % 2 == 1))

        def recip_item(qb, i):
            def f():
                slot = qb % 2
                csl = slice(i * 256, (i + 1) * 256)
                recip(rrow2[64:65, csl], osb[slot][64:65, csl], [("osb", slot)], [("rrow", i)])
            return f

        def norm_item(qb, i):
            def f():
                slot = qb % 2
                csl = slice(i * 256, (i + 1) * 256)
                nb = nbanks[i % len(nbanks)]
                mm(bank(nb)[0:64, 0:256], ones32[64:65, 0:64], rrow2[64:65, csl], True, True,
                   [("rrow", i), "ones32"], [("ps", nb)])
                attn_write(qb, i, osb[slot][0:64, csl], bank(nb)[0:64, 0:256], [("osb", slot), ("ps", nb)])
            return f

        for it in qprep(0):
            it()
        emit_S(0)
        qitems = []
        for si, (qb, kt) in enumerate(steps):
            if kt == 15 and qb + 1 < NQB:
                qitems = list(qprep(qb + 1))
            if kt >= 16 and qitems:
                qitems.pop(0)()
            if si + 1 < len(steps):
                emit_S(si + 1)
            sb = si % 2
            pslot = si % 3
            act(pT[pslot], bank(sb * 2, 2), AF.Exp, [("ps", sb * 2), ("ps", sb * 2 + 1), "cols"], [("pT", pslot)],
                bias=col("abias", qb * KT + kt), scale=scale)
            if shared:
                for ip in range(2):
                    vv, vrd = v_fn(0, kt)
                    mm(bank(4 + ip)[:, :], vv, pT[pslot][:, ip * 512:(ip + 1) * 512], kt == 0, kt == KT - 1,
                       vrd + [("pT", pslot)], [("ps", 4 + ip)], sig=(ip == 1), skip=True)
            else:
                for i in range(4):
                    vv, vrd = v_fn(i, kt)
                    ob = 4 + i // 2
                    mm(bank(ob)[:, (i % 2) * 256:(i % 2 + 1) * 256], vv, pT[pslot][:, i * 256:(i + 1) * 256],
                       (kt == 0 and i % 2 == 0), kt == KT - 1, vrd + [("pT", pslot)], [("ps", ob)],
                       sig=(i == 3), skip=True)
            if dummy is not None:
                dummy()
            if work:
                work.pop(0)()
            if kt == KT - 1:
                slot = qb % 2
                cpy(osb[slot][0:65, 0:512], bank(4)[0:65, :], [("ps", 4)], [("osb", slot)])
                cpy(osb[slot][0:65, 512:1024], bank(5)[0:65, :], [("ps", 5)], [("osb", slot)])
                for i in range(4):
                    work.append(recip_item(qb, i))
                for i in range(4):
                    work.append(norm_item(qb, i))
                for dc in range(8):
                    work.append((lambda qb=qb, dc=dc: outproj_dc(qb, dc)))
        while flush and work:
            work.pop(0)()

    ffn(0, 0, side=(6, 24), ders=[(0, 1), (0, 2), (1, 0)], tail=lambda t4: norm_t4(0, 1, t4))
    checkpoint("ffn00")
    checkpoint("n01")
    Gm0 = der[:, 0, 24:32]
    cqn = view(R, [3, T], BF16)
    ckvn = view(R + 12288, [2, NK], BF16)
    krall = view(R + 22528, [NK], BF16)
    cosT = view(R + 27648, [T], F32)
    sinT = view(R + 35840, [T], F32)
    GLU = R + 44032
    glu = view(GLU, [4, 8, 286], BF16)
    gl = view(GLU, [4, T], BF16)
    WS = R + 62336
    wpc = [view(WS + i * 6144, [8, 384], BF16) for i in range(3)]
    yc = view(HT, [4, T], F32)

    import os
    SK = os.environ.get("KSKIP", "")
    S.fence()
    if "a" not in SK:
        S.dma("sp", cosT, rope_d[0, 0], wr=["cos"], key="ld_cos")
        S.dma("sp", sinT, rope_d[0, 1], wr=["sin"], key="ld_sin")
    if "b" not in SK:
        for c in range(2):
            load_w(ckvn[:, c, 0:NCTX], ckvctx_d[c * 128:(c + 1) * 128, :], "ckvn_ctx", "ld_ctx")
    if "e" not in SK:
        load_w(krall[64:96, 0:NCTX], krctx_d, "kr_ctx", "ld_ctx")
    if "c" not in SK:
        S.op("pool", lambda: nc.gpsimd.memset(glu[:, :, 0, 0:15], 0.0), wr=["glu_pad"])
        S.op("pool", lambda: nc.gpsimd.memset(glu[:, :, 7, 271:286], 0.0), wr=["glu_pad"])

    pieces = [(0, 384), (384, 256), (640, 192)] + [(832 + 256 * i, 256) for i in range(4)]

    def load_piece(pi):
        c0, w_ = pieces[pi]
        load_w(wpc[pi % 3][:, :, 0:w_], wina_d[:, c0:c0 + w_].rearrange("(kc p) n -> p kc n", p=128),
               ("wpc", pi % 3), ("ld_wpc", pi % 3))

    if "d" not in SK:
        for pi in range(3):
            load_piece(pi)

    checkpoint("mla_ld")

    def proj(w, col0, m, t4, b, wres):
        tsl = slice(t4 * 512, (t4 + 1) * 512)
        for kc in range(8):
            mm(bank(b)[0:m, :], w[:, kc, col0:col0 + m], hT[:, kc, tsl], kc == 0, kc == 7, [wres, ("hT", t4)],
               [("ps", b)], sig=(kc == 7))

    for pi, (nch, gname, inv_n) in enumerate([(3, "gql", 1.0 / 384), (2, "gkvl", 1.0 / 256)]):
        w = wpc[pi % 3]
        for t4 in range(4):
            tsl = slice(t4 * 512, (t4 + 1) * 512)
            for c in range(nch):
                proj(w, c * 128, 128, t4, c, ("wpc", pi % 3))
            rms_rstd(lambda c: bank(c), nch, 512, inv_n, [("ps", c_) for c_ in range(nch)], ones, t4 % 2, 0, 4 + t4 % 2)
            for c in range(nch):
                if pi == 0:
                    stt(tmp[c % 2], bank(c), col(gname, c), rt[t4 % 2], ALU.mult, ALU.mult,
                        [("ps", c), "cols", ("rt", t4 % 2)], [("tmp", c % 2)])
                    act(cqn[:, c, tsl], tmp[c % 2], AF.Copy, [("tmp", c % 2)], ["cqn"])
                else:
                    k2 = (t4 * 2 + c) % 2
                    stt(tmp2[k2], bank(c), col(gname, c), rt[t4 % 2], ALU.mult, ALU.mult,
                        [("ps", c), "cols", ("rt", t4 % 2)], [("tmp2", k2)])
                    act(ckvn[:, c, NCTX + t4 * 512:NCTX + (t4 + 1) * 512], tmp2[k2], AF.Copy, [("tmp2", k2)], ["ckvn"])
                    S.dma("sp", ckvT_d[c * 128:(c + 1) * 128, tsl], tmp2[k2], rd=[("tmp2", k2)], key=("st_t2", k2),
                          store=True)
        load_piece(pi + 3)
    checkpoint("mla_p1")
    w = wpc[2]
    for t4 in range(4):
        tsl = slice(t4 * 512, (t4 + 1) * 512)
        proj(w, 0, 96, t4, 0, ("wpc", 2))
        proj(w, 96, 96, t4, 1, ("wpc", 2))
        k2 = t4 % 2
        tt(tmp[0][64:96, :], bank(0)[64:96, :], cosT[64:96, tsl], ALU.mult, [("ps", 0), "cos"], [("tmp", 0)])
        tt(tmp[1][64:96, :], bank(1)[64:96, :], sinT[64:96, tsl], ALU.mult, [("ps", 1), "sin"], [("tmp", 1)])
        tt(tmp2[k2][64:96, :], tmp[0][64:96, :], tmp[1][64:96, :], ALU.add, [("tmp", 0), ("tmp", 1)], [("tmp2", k2)])
        act(krall[64:96, NCTX + t4 * 512:NCTX + (t4 + 1) * 512], tmp2[k2][64:96, :], AF.Copy, [("tmp2", k2)], ["krall"])
        S.dma("sp", krT_d[:, tsl], tmp2[k2][64:96, :], rd=[("tmp2", k2)], key=("st_t2", k2), store=True)
    load_piece(5)
    checkpoint("mla_p2")
    for c in range(4):
        pi = 3 + c
        w = wpc[pi % 3]
        for t4 in range(4):
            k = t4 % 2
            proj(w, 0, 128, t4, k, ("wpc", pi % 3))
            proj(w, 128, 128, t4, 2 + k, ("wpc", pi % 3))
            act(tmp[k], bank(2 + k), AF.Sigmoid, [("ps", 2 + k)], [("tmp", k)])
            tt(glu[:, c, 2 * t4:2 * t4 + 2, 15:271], bank(k).rearrange("p (a b) -> p a b", a=2),
               tmp[k].rearrange("p (a b) -> p a b", a=2), ALU.mult, [("ps", k), ("tmp", k)], [("glu", c)])
        if pi + 3 < 7:
            load_piece(pi + 3)
        ts(glu[:, c, 1:8, 0:15], glu[:, c, 0:7, 256:271], col("hmask"), 0.0, ALU.mult, ALU.add,
           [("glu", c), "cols"], [("glu", c)])
        ts(glu[:, c, 0:7, 271:286], glu[:, c, 1:8, 15:30], col("hmask"), 0.0, ALU.mult, ALU.add,
           [("glu", c), "cols"], [("glu", c)])
    checkpoint("mla_proj")
    wqu = view(WS, [3, 1536], BF16)
    wkvu = view(WS + 9216, [2, 1024], BF16)
    woa = view(WS + 13312, [4, 1024], BF16)
    S.fence()
    load_w(woa, wouta_d[512:1024, :].rearrange("(kc p) n -> p kc n", p=128), "woa", "ld_woa")
    for c in range(4):
        for j in range(31):
            ds_ = (c * 31 + j) % 4
            ts(dg[ds_], ident, col("wdw", c * 31 + j), 0.0, ALU.mult, ALU.add, ["ident", "cols"], [("dg", ds_)])
            for t4 in range(4):
                mm(bank(t4).rearrange("p (a b) -> p a b", a=2), dg[ds_], glu[:, c, 2 * t4:2 * t4 + 2, j:j + 256],
                   j == 0, j == 30, [("dg", ds_), ("glu", c), "glu_pad"], [("ps", t4)], sig=(t4 == 3))
        for t4 in range(4):
            act(yc[:, c, t4 * 512:(t4 + 1) * 512], bank(t4), AF.Identity, [("ps", t4), "cols", "hT"], ["hT"],
                bias=col("bdw", c), scale=1.0)
    for t4 in range(4):
        tsl = slice(t4 * 512, (t4 + 1) * 512)
        for c in range(4):
            act(sq[0][:, c, :], yc[:, c, tsl], AF.Copy, ["hT"], [("sq", 0)])
            act(sq[1][:, c, :], yc[:, c, tsl], AF.Square, ["hT"], [("sq", 1)])
        for c in range(4):
            mm(bank(6), ones, sq[0][:, c, :], c == 0, c == 3, [("sq", 0), "ones"], [("ps", 6)], sig=(c == 3))
        for c in range(4):
            mm(bank(7), ones, sq[1][:, c, :], c == 0, c == 3, [("sq", 1), "ones"], [("ps", 7)], sig=(c == 3))
        mean, m2 = tmp2[0], tmp2[1]
        ts(mean, bank(6), 1.0 / 512, 0.0, ALU.mult, ALU.add, [("ps", 6)], [("tmp2", 0)])
        tt(m2, mean, mean, ALU.mult, [("tmp2", 0)], [("tmp2", 1)])
        stt(m2, bank(7), 1.0 / 512, m2, ALU.mult, ALU.subtract, [("ps", 7), ("tmp2", 1)], [("tmp2", 1)])
        act(rt[0], m2, AF.Ln, [("tmp2", 1), "cols"], [("rt", 0)], bias=col("eps"), scale=1.0)
        act(rt[0], rt[0], AF.Exp, [("rt", 0)], [("rt", 0)], scale=-0.5)
        for c in range(4):
            k = c % 2
            tt(tmp[k], yc[:, c, tsl], mean, ALU.subtract, ["hT", ("tmp2", 0)], [("tmp", k)])
            tt(tmp[k], tmp[k], rt[0], ALU.mult, [("tmp", k), ("rt", 0)], [("tmp", k)])
            act(gl[:, c, tsl], tmp[k], AF.Silu, [("tmp", k), "cols"] + [("glu", cc) for cc in range(4)],
                [("gl", c)], bias=col("bln", c), scale=col("gln", c))
    for dc in range(8):
        for t4 in range(4):
            tsl = slice(t4 * 512, (t4 + 1) * 512)
            b = 6 + (dc * 4 + t4) % 2
            for c in range(4):
                mm(bank(b), woa[:, c, dc * 128:(dc + 1) * 128], gl[:, c, tsl], c == 0, c == 3,
                   ["woa", ("gl", c)], [("ps", b)], sig=(c == 3))
            stt(xT[:, dc, tsl], bank(b), Gm0[:, dc:dc + 1], xT[:, dc, tsl], ALU.mult, ALU.add,
                [("ps", b), "der", ("xT", t4)], [("xT", t4)])
    checkpoint("mla_conv")
    S.fence()
    load_w(wqu, wqu_d.rearrange("(kc p) n -> p kc n", p=128), "wqu", "ld_wqu")
    load_w(wkvu, wkvu_d.rearrange("(kc p) n -> p kc n", p=128), "wkvu", "ld_wkvu")
    Vt = view(HT, [KT, 4, 65], BF16)
    VtF = view(HT, [KT * 4 * 65 + 63], BF16)
    KTt = view(HT + 10496, [4, NK], BF16)
    ATT = GLU
    Qt = [view(ATT + i * 2048, [4, 256], BF16) for i in range(2)]
    pT = [view(ATT + 4096 + i * 2048, [1024], BF16) for i in range(3)]
    attnb = [view(ATT + 10240 + i * 1024, [2, 256], BF16) for i in range(2)]
    gl_res = [("gl", c) for c in range(4)]
    for i_ in range(2):
        S.op("pool", lambda i_=i_: nc.gpsimd.memset(Qt[i_][64:128, :, :], 0.0), wr=[("Qt", i_)])
    for i_ in range(4):
        S.op("pool", lambda i_=i_: nc.gpsimd.memset(KTt[64:128, i_, :], 0.0), wr=[("KT", i_)])
    mla_work = []
    for g in range(2):
        load_w(woa[:, 2 * g:2 * g + 2, :], wouta_d[g * 256:(g + 1) * 256, :].rearrange("(kc p) n -> p kc n", p=128),
               "woa", "ld_woa")
        S.op("pool", lambda: nc.gpsimd.memset(Vt[:, :, :, 64:65], 1.0), rd=["hT"], wr=["Vt1"])
        for kt in range(KT):
            b = 6 + kt % 2
            for c in range(2):
                rhs = wkvu[:, c, g * 512:(g + 1) * 512].rearrange("p (h d) -> p h d", h=4)[:, :, 64:128]
                mm(bank(b)[:, 0:256].rearrange("p (h d) -> p h d", h=4), ckvn[:, c, kt * 128:(kt + 1) * 128], rhs,
                   c == 0, c == 1, ["ckvn", "ckvn_ctx", "wkvu"], [("ps", b)], sig=(c == 1))
            if kt % 2 == 0:
                cpy(Vt[:, kt, :, 0:64], bank(b)[:, 0:256].rearrange("p (h d) -> p h d", h=4), [("ps", b), "hT"], ["Vt"])
            else:
                act(Vt[:, kt, :, 0:64], bank(b)[:, 0:256].rearrange("p (h d) -> p h d", h=4), AF.Copy,
                    [("ps", b), "hT"], ["Vt"])
        for i in range(4):
            h = 4 * g + i
            for k5 in range(5):
                b = 6 + (i * 5 + k5) % 2
                for c in range(2):
                    mm(bank(b)[0:64, :], wkvu[:, c, h * 128:h * 128 + 64], ckvn[:, c, k5 * 512:(k5 + 1) * 512],
                       c == 0, c == 1, ["ckvn", "ckvn_ctx", "wkvu"], [("ps", b)], sig=(c == 1))
                if k5 % 2 == 0:
                    cpy(KTt[0:64, i, k5 * 512:(k5 + 1) * 512], bank(b)[0:64, :], [("ps", b), "hT"], [("KT", i)])
                else:
                    act(KTt[0:64, i, k5 * 512:(k5 + 1) * 512], bank(b)[0:64, :], AF.Copy, [("ps", b), "hT"], [("KT", i)])
            S.dma("sp", KTt[64:96, i, :], krall[64:96, :], rd=["krall", "kr_ctx"], wr=[("KT", i)],
                  key=("ld", ("KTkr", i)))

        def qprep(qb, g=g):
            return [(lambda i2=i2: qprep_i2(qb, i2)) for i2 in range(2)]

        def qprep_i2(qb, i2, g=g):
            qs = slice(qb * 256, (qb + 1) * 256)
            qt = Qt[qb % 2]
            if True:
                for ii in range(2):
                    i = i2 * 2 + ii
                    h = 4 * g + i
                    for ab in range(2):
                        for c in range(3):
                            mm(bank(6 + ab)[0:96, ii * 256:(ii + 1) * 256],
                               wqu[:, c, ab * 768 + h * 96:ab * 768 + (h + 1) * 96], cqn[:, c, qs], c == 0, c == 2,
                               ["wqu", "cqn"], [("ps", 6 + ab)], sig=(c == 2))
                for ii in range(2):
                    i = i2 * 2 + ii
                    A = bank(6)[:, ii * 256:(ii + 1) * 256]
                    B = bank(7)[:, ii * 256:(ii + 1) * 256]
                    cpy(qt[0:64, i, :], A[0:64, :], [("ps", 6)] + gl_res, [("Qt", qb % 2)])
                    tt(tmp[0][64:96, 0:256], A[64:96, :], cosT[64:96, qs], ALU.mult, [("ps", 6), "cos"], [("tmp", 0)])
                    tt(tmp[1][64:96, 0:256], B[64:96, :], sinT[64:96, qs], ALU.mult, [("ps", 7), "sin"], [("tmp", 1)])
                    tt(qt[64:96, i, :], tmp[0][64:96, 0:256], tmp[1][64:96, 0:256], ALU.add,
                       [("tmp", 0), ("tmp", 1)] + gl_res, [("Qt", qb % 2)])

        def q_fn(i, qb):
            return Qt[qb % 2][:, i, :], [("Qt", qb % 2)]

        def kT_fn(i, kt):
            return KTt[:, i, kt * 128:(kt + 1) * 128], [("KT", i)]

        def v_fn(i, kt):
            o_ = (kt * 4 + i) * 65
            return VtF[:, o_:o_ + 128], ["Vt", "Vt1"]

        def attn_write(qb, i, o_sb, b_ps, rd):
            ab = attnb[qb % 2]
            tt(ab[(i % 2) * 64:(i % 2) * 64 + 64, i // 2, :], o_sb, b_ps, ALU.mult, rd, [("attn", qb % 2)])

        def outproj_dc(qb, dc, g=g):
            qs = slice(qb * 256, (qb + 1) * 256)
            ab = attnb[qb % 2]
            b = 6 + dc % 2
            for pr in range(2):
                mm(bank(b)[:, 0:256], woa[:, 2 * g + pr, dc * 128:(dc + 1) * 128], ab[:, pr, :], pr == 0, pr == 1,
                   ["woa", ("attn", qb % 2)], [("ps", b)], sig=(pr == 1))
            stt(xT[:, dc, qs], bank(b)[:, 0:256], Gm0[:, dc:dc + 1], xT[:, dc, qs], ALU.mult, ALU.add,
                [("ps", b), "der", ("xT", qb // 2)], [("xT", qb // 2)])

        attention(q_fn, kT_fn, v_fn, float(96 ** -0.5), pT, attn_write, outproj_dc, qprep, False,
                  work=mla_work, flush=(g == 1))
    checkpoint("mla_attn")
    ffn(0, 1, side=(24, 36), ders=[(1, 1), (1, 2)], tail=lambda t4: norm_t4(1, 0, t4))
    checkpoint("ffn01")

    ffn(1, 0, fence=False, do_norm=False, tail=lambda t4: norm_t4(1, 1, t4))
    checkpoint("ffn10")
    Gm1 = der[:, 1, 24:32]
    QTa = view(R, [8, T], BF16)
    KTa = view(R + 32768, [2, NK], BF16)
    Vg = view(R + 43008, [KT, 4, 65], BF16)
    VgF = view(R + 43008, [KT * 4 * 65 + 63], BF16)
    cos1 = view(R + 53440, [T], F32)
    sin1 = view(R + 61632, [T], F32)
    wsl = [view(R + 69824 + i * 4096, [8, 256], BF16) for i in range(3)]
    S.fence()
    S.dma("sp", cos1, rope_d[1, 0], wr=["cos"], key="ld_cos")
    S.dma("sp", sin1, rope_d[1, 1], wr=["sin"], key="ld_sin")
    for m in range(2):
        load_w(KTa[:, m, 0:NCTX], gkctx_d[m * 128:(m + 1) * 128, :], "KTa_ctx", "ld_ctx")
    vstage = view(SC + 12288, [4, 256], BF16)
    S.dma("pool", vstage, gvctx_d.rearrange("(kt p) n -> p kt n", p=128), wr=[("tmp", 0)], key=("ld", "vstage"))
    for kt_ in range(4):
        cpy(Vg[:, kt_, :, 0:64], vstage[:, kt_, :].rearrange("p (h d) -> p h d", d=64), [("tmp", 0)], ["Vg_ctx"])
    S.op("pool", lambda: nc.gpsimd.memset(Vg[:, :, :, 64:65], 1.0), wr=["Vg1"])

    def load_pc(pi):
        load_w(wsl[pi % 3], winc_d[:, pi * 256:(pi + 1) * 256].rearrange("(kc p) n -> p kc n", p=128),
               ("wsl", pi % 3), ("ld_wsl", pi % 3))

    for pi in range(3):
        load_pc(pi)
    checkpoint("g_ld")
    gtasks = [(pi, t4) for pi in range(10) for t4 in range(4)]

    def g_chains(idx):
        pi, t4 = gtasks[idx]
        k = idx % 2
        proj(wsl[pi % 3], 0, 128, t4, k, ("wsl", pi % 3))
        proj(wsl[pi % 3], 128, 128, t4, 2 + k, ("wsl", pi % 3))
        if t4 == 3 and pi + 3 < 11:
            load_pc(pi + 3)

    def g_post(idx):
        pi, t4 = gtasks[idx]
        k = idx % 2
        isq = pi < 8
        gname, gpname = ("gq", "gqp") if isq else ("gk", "gkp")
        tsl = slice(t4 * 512, (t4 + 1) * 512)
        rms_rstd(lambda c: bank(k), 1, 512, 1.0 / 64, [("ps", k)], blk, k, k, 4 + k)
        stt(tmp[0], bank(k), col(gname), cos1[:, tsl], ALU.mult, ALU.mult, [("ps", k), "cols", "cos"], [("tmp", 0)])
        stt(tmp[1], bank(2 + k), col(gpname), sin1[:, tsl], ALU.mult, ALU.mult, [("ps", 2 + k), "cols", "sin"],
            [("tmp", 1)])
        tt(tmp[0], tmp[0], tmp[1], ALU.add, [("tmp", 0), ("tmp", 1)], [("tmp", 0)])
        if isq:
            tt(QTa[:, pi, tsl], tmp[0], rt[k], ALU.mult, [("tmp", 0), ("rt", k)], ["QTa"])
        else:
            m = pi - 8
            tt(tmp2[k], tmp[0], rt[k], ALU.mult, [("tmp", 0), ("rt", k)], [("tmp2", k)])
            act(KTa[:, m, NCTX + t4 * 512:NCTX + (t4 + 1) * 512], tmp2[k], AF.Copy, [("tmp2", k)], ["KTa"])
            S.dma("sp", gkT_d[m * 128:(m + 1) * 128, tsl], tmp2[k], rd=[("tmp2", k)], key=("st_t2", k), store=True)

    for idx in range(len(gtasks) + 1):
        if idx < len(gtasks):
            g_chains(idx)
        if idx >= 1:
            g_post(idx - 1)
    w = wsl[10 % 3]
    for t16 in range(16):
        b = t16 % 2
        for kc in range(8):
            mm(bank(b)[:, 0:256], hT[:, kc, t16 * 128:(t16 + 1) * 128], w[:, kc, :], kc == 0, kc == 7,
               [("wsl", 10 % 3), ("hT", t16 // 4)], [("ps", b)], sig=(kc == 7))
        if "x" not in SK:
            act(tmp2[b][:, 0:256], bank(b)[:, 0:256], AF.Copy, [("ps", b)], [("tmp2", b)])
        if "w" not in SK:
            S.dma("sp", gv_d[t16 * 128:(t16 + 1) * 128, :], tmp2[b][:, 0:256], rd=[("tmp2", b)], key=("st_t2", b), store=True)
        if "y" not in SK:
            cpy(Vg[:, 4 + t16, :, 0:64], bank(b)[:, 0:256].rearrange("p (h d) -> p h d", h=4), [("ps", b)], ["Vg"])
    checkpoint("gqa_proj")
    S.fence()
    woc = view(HT, [8, 1024], BF16)
    load_w(woc, woutc_d.rearrange("(kc p) n -> p kc n", p=128), "hT", "ld_woc")
    ATT1 = R + 53440
    pT1 = [view(ATT1 + i * 2048, [1024], BF16) for i in range(3)]
    attn1 = [view(ATT1 + 6144 + i * 2048, [4, 256], BF16) for i in range(2)]
    gqa_work = []
    for g in range(4):
        m, r = g // 2, g % 2

        Qz = [view(ATT1 + 10240 + i * 2048, [4, 256], BF16) for i in range(2)]
        for i_ in range(2):
            S.op("pool", lambda i_=i_, r=r: nc.gpsimd.memset(Qz[i_][(1 - r) * 64:(1 - r) * 64 + 64, :, :], 0.0),
                 wr=[("Qz", i_)])

        def qprep(qb, m=m, r=r):
            return [lambda: qprep1(qb)]

        def qprep1(qb, m=m, r=r):
            cpy(Qz[qb % 2][r * 64:r * 64 + 64, :, :], QTa[r * 64:r * 64 + 64, 4 * m:4 * m + 4, qb * 256:(qb + 1) * 256],
                ["QTa"], [("Qz", qb % 2)], e="pool")

        def q_fn(ip, qb):
            return Qz[qb % 2][:, 2 * ip:2 * ip + 2, :], [("Qz", qb % 2)]

        def kT_fn(i, kt, m=m):
            return KTa[:, m, kt * 128:(kt + 1) * 128], ["KTa", "KTa_ctx"]

        def v_fn(i, kt, g=g):
            o_ = (kt * 4 + g) * 65
            return VgF[:, o_:o_ + 128], ["Vg", "Vg1", "Vg_ctx"]

        def attn_write(qb, i, o_sb, b_ps, rd, r=r):
            ab = attn1[qb % 2]
            tt(ab[r * 64:r * 64 + 64, i, :], o_sb, b_ps, ALU.mult, rd, [("attn", qb % 2)])

        def outproj_dc(qb, dc, m=m, r=r):
            qs = slice(qb * 256, (qb + 1) * 256)
            ab = attn1[qb % 2]
            b = 6 + dc % 2
            for i in range(4):
                mm(bank(b)[:, 0:256], woc[r * 64:r * 64 + 64, 4 * m + i, dc * 128:(dc + 1) * 128],
                   ab[r * 64:r * 64 + 64, i, :], i == 0, i == 3, ["hT", ("attn", qb % 2)], [("ps", b)], sig=(i == 3))
            stt(xT[:, dc, qs], bank(b)[:, 0:256], Gm1[:, dc:dc + 1], xT[:, dc, qs], ALU.mult, ALU.add,
                [("ps", b), "der", ("xT", qb // 2)], [("xT", qb // 2)])

        NDUM = int(os.environ.get("KDUM", "0"))

        def dummy(m=m):
            for _ in range(NDUM):
                mm(bank(6)[:, 0:256], woc[:, 0, 0:128], QTa[:, 0, 0:256], True, True, ["hT", "QTa"], [("ps", 6)],
                   sig=False)

        attention(q_fn, kT_fn, v_fn, float(64 ** -0.5), pT1, attn_write, outproj_dc, qprep, True,
                  dummy=dummy if NDUM else None, nbanks=(6, 7) if not NDUM else (7,),
                  work=gqa_work, flush=(g == 3))
    checkpoint("gqa_attn")
    ffn(1, 1, tail=final_t4)
    checkpoint("ffn11")

    S.finish()


def _prep_shared(inp):
    f = np.float32
    sh = {}
    sh["wmod"] = np.ascontiguousarray(inp["w_mod"], dtype=f)
    sh["ident"] = np.eye(128, dtype=f)
    wffin = np.empty((4, 22, 128, 2048), f)
    wffout = np.empty((4, DFF, D), f)
    for l in range(2):
        for fi, (ni, no) in enumerate([("w_ff1_in", "w_ff1_out"), ("w_ff2_in", "w_ff2_out")]):
            w = np.asarray(inp[ni][l], f)
            a = w[:, :DFF].reshape(8, 128, 22, 128)
            b = w[:, DFF:].reshape(8, 128, 22, 128)
            ab = np.stack([a, b], axis=3)
            wffin[l * 2 + fi] = ab.transpose(2, 1, 0, 3, 4).reshape(22, 128, 2048)
            wffout[l * 2 + fi] = np.asarray(inp[no][l], f)
    sh["wffin"], sh["wffout"] = wffin, wffout
    wa = np.asarray(inp["w_in_a"][0], f)
    p32, _, _, _, _ = _rope_meta(32)
    krA = np.zeros((D, 96), f); krB = np.zeros((D, 96), f)
    krA[:, 64:] = wa[:, 640:672]
    krB[:, 64:] = wa[:, 640:672][:, p32]
    u = wa[:, 672:]
    ab = [np.concatenate([u[:, c * 128:(c + 1) * 128], u[:, 512 + c * 128:512 + (c + 1) * 128]], 1) for c in range(4)]
    sh["wina"] = np.ascontiguousarray(np.concatenate([wa[:, 0:640], krA, krB] + ab, 1))
    wq = np.asarray(inp["w_q_up"][0], f)
    wqB = np.zeros_like(wq)
    for h in range(8):
        wqB[:, h * 96 + 64:(h + 1) * 96] = wq[:, h * 96 + 64:(h + 1) * 96][:, p32]
    sh["wqu2"] = np.ascontiguousarray(np.concatenate([wq, wqB], 1))
    sh["wkvu"] = np.ascontiguousarray(inp["w_kv_up"][0], dtype=f)
    sh["wouta"] = np.ascontiguousarray(inp["w_out_a"][0], dtype=f)
    wc = np.asarray(inp["w_in_c"][0], f)
    p64, _, _, _, _ = _rope_meta(64)
    pcs = []
    for m in range(2):
        for j in range(4):
            hs = [8 * m + j, 8 * m + 4 + j]
            A = np.concatenate([wc[:, h * 64:(h + 1) * 64] for h in hs], 1)
            B = np.concatenate([wc[:, h * 64:(h + 1) * 64][:, p64] for h in hs], 1)
            pcs += [A, B]
    for m in range(2):
        hs = [2 * m, 2 * m + 1]
        A = np.concatenate([wc[:, 1024 + h * 64:1024 + (h + 1) * 64] for h in hs], 1)
        B = np.concatenate([wc[:, 1024 + h * 64:1024 + (h + 1) * 64][:, p64] for h in hs], 1)
        pcs += [A, B]
    pcs.append(wc[:, 1280:1536])
    sh["winc"] = np.ascontiguousarray(np.concatenate(pcs, 1))
    wo = np.asarray(inp["w_out_c"][0], f)
    wop = np.empty_like(wo)
    for m in range(2):
        for j in range(4):
            for r in range(2):
                h = 8 * m + 4 * r + j
                wop[(4 * m + j) * 128 + r * 64:(4 * m + j) * 128 + r * 64 + 64] = wo[h * 64:(h + 1) * 64]
    sh["woutc"] = wop
    return sh


def _colT(v, n):
    return np.asarray(v, np.float32).reshape(n, 128).T


def _prep_core(inp, core, sh):
    f = np.float32
    sample = core >= 4
    m = dict(sh)
    if sample:
        b = core - 4
        x = np.asarray(inp["x_sample"][b], f)
        cond = np.asarray(inp["c"][b], f)
        m["ckvctxT"] = np.ascontiguousarray(np.asarray(inp["cache_mla_ckv"][b, 0], f).T)
        m["krctxT"] = np.ascontiguousarray(np.asarray(inp["cache_mla_krope"][b, 0], f).T)
        m["gkctxT"] = np.ascontiguousarray(np.asarray(inp["cache_gqa_k"][b, 0], f).reshape(NCTX, 256).T)
        m["gvctx"] = np.ascontiguousarray(np.asarray(inp["cache_gqa_v"][b, 0], f).reshape(NCTX, 256))
    else:
        x = np.asarray(inp["x_prompt"][core * 8:(core + 1) * 8], f).reshape(T, D)
        cond = np.asarray(inp["c_ctx"], f)
        m["ckvctxT"] = np.zeros((256, NCTX), f)
        m["krctxT"] = np.zeros((32, NCTX), f)
        m["gkctxT"] = np.zeros((256, NCTX), f)
        m["gvctx"] = np.zeros((NCTX, 256), f)
    m["xT"] = np.ascontiguousarray(x.T)
    cols = np.zeros((128, NCOLS), f)

    def put(name, arr):
        arr = np.asarray(arr, f)
        cols[:, _COLS[name]:_COLS[name] + arr.shape[1]] = arr

    put("cond", _colT(cond, 8))
    for l in range(2):
        put("gff1_%d" % l, _colT(inp["g_ff1"][l], 8))
        put("gmix_%d" % l, _colT(inp["g_mix"][l], 8))
        put("gff2_%d" % l, _colT(inp["g_ff2"][l], 8))
    put("gfinal", _colT(inp["g_final"], 8))
    put("gql", _colT(inp["g_q_lora"][0], 3))
    put("gkvl", _colT(inp["g_kv_lora"][0], 2))
    put("bdw", _colT(inp["b_dw"][0], 4))
    put("gln", _colT(inp["g_conv_ln"][0], 4))
    put("bln", _colT(inp["b_conv_ln"][0], 4))
    wdw = np.asarray(inp["w_dw"][0], f)
    put("wdw", wdw.T.reshape(4, 128, 31).transpose(1, 0, 2).reshape(128, 124))
    p64, _, _, _, _ = _rope_meta(64)
    gq = np.asarray(inp["g_q_head"][0], f); gk = np.asarray(inp["g_k_head"][0], f)
    put("gq", np.tile(gq, 2)[:, None]); put("gqp", np.tile(gq[p64], 2)[:, None])
    put("gk", np.tile(gk, 2)[:, None]); put("gkp", np.tile(gk[p64], 2)[:, None])
    put("hmask", np.full((128, 1), 1.0 if sample else 0.0, f))
    put("eps", np.full((128, 1), EPS, f))
    bm = np.asarray(inp["b_mod"], f)
    put("bmod", np.concatenate([_colT(bm[0], 72), _colT(bm[1], 72)], 1))
    ab = np.zeros((128, NQB, KT), f)
    if not sample:
        ab[:] = -30000.0
        for qb in range(NQB):
            ab[:, qb, 4 + 2 * qb:4 + 2 * qb + 2] = 0.0
    put("abias", ab.reshape(128, NQB * KT))
    m["cols"] = cols
    rope = np.zeros((2, 2, 128, T), f)
    c32, s32 = _rope_tables(32, sample)
    rope[0, 0, 64:96], rope[0, 1, 64:96] = c32, s32
    c64, s64 = _rope_tables(64, sample)
    rope[1, 0] = np.concatenate([c64, c64], 0)
    rope[1, 1] = np.concatenate([s64, s64], 0)
    m["rope"] = rope
    return m


_NC_CACHE = {}


def kernel(**inputs):
    inp = {k: np.asarray(v) for k, v in inputs.items()}
    if "nc" not in _NC_CACHE:
        _NC_CACHE["nc"] = build_nc()
    nc = _NC_CACHE["nc"]
    sh = _prep_shared(inp)
    in_maps = [_prep_core(inp, c, sh) for c in range(8)]
    res = run_bass_kernel_spmd(nc, in_maps, core_ids=list(range(8)))
    r = res.results
    f = np.float32
    y_prompt = np.concatenate([np.asarray(r[c]["yT"], f).T.reshape(8, 256, D) for c in range(4)], 0)
    y_sample = np.stack([np.asarray(r[c]["yT"], f).T for c in range(4, 8)], 0)
    ckv = np.concatenate([np.asarray(r[c]["ckvT"], f).T.reshape(8, 1, 256, 256) for c in range(4)], 0)
    kr = np.concatenate([np.asarray(r[c]["krT"], f).T.reshape(8, 1, 256, 32) for c in range(4)], 0)
    gk = np.concatenate([np.asarray(r[c]["gkT"], f).T.reshape(8, 1, 256, 4, 64) for c in range(4)], 0)
    gvv = np.concatenate([np.asarray(r[c]["gv"], f).reshape(8, 1, 256, 4, 64) for c in range(4)], 0)
    return (np.ascontiguousarray(y_prompt), np.ascontiguousarray(y_sample), np.ascontiguousarray(ckv),
            np.ascontiguousarray(kr), np.ascontiguousarray(gk), np.ascontiguousarray(gvv))
```

```python
import numpy as np
import concourse.bass as bass
import concourse.mybir as mybir
from concourse.bass_utils import run_bass_kernel_spmd

F32 = mybir.dt.float32
BF16 = mybir.dt.bfloat16
ALU = mybir.AluOpType
AF = mybir.ActivationFunctionType

T = 2048
D = 1024
NCTX = 512
NK = NCTX + T
KT = NK // 128
NQB = T // 256
DFF = 2816
EPS = 1e-6

_COLS = {}
_o = 0
for _n, _w in [("cond", 8), ("gff1_0", 8), ("gmix_0", 8), ("gff2_0", 8), ("gff1_1", 8), ("gmix_1", 8),
               ("gff2_1", 8), ("gfinal", 8), ("gql", 3), ("gkvl", 2), ("bdw", 4), ("gln", 4), ("bln", 4),
               ("wdw", 124), ("gq", 1), ("gqp", 1), ("gk", 1), ("gkp", 1), ("hmask", 1), ("eps", 1),
               ("bmod", 144), ("abias", 160)]:
    _COLS[_n] = _o
    _o += _w
NCOLS = _o


def _rope_meta(d):
    half = d // 2
    nf = half // 2
    partner = np.zeros(d, np.int64)
    sign = np.zeros(d, np.float32)
    sec = np.zeros(d, np.int64)
    fr = np.zeros(d, np.int64)
    for i in range(d):
        s, ii = i // half, i % half
        f, part = ii % nf, ii // nf
        partner[i] = i + nf if part == 0 else i - nf
        sign[i] = -1.0 if part == 0 else 1.0
        sec[i] = s
        fr[i] = f
    inv = (np.float32(10000.0) ** (-np.arange(0, half, 2, dtype=np.float32) / np.float32(half))).astype(np.float32)
    return partner, sign, sec, fr, inv


def _rope_tables(d, sample):
    partner, sign, sec, fr, inv = _rope_meta(d)
    if not sample:
        return np.ones((d, T), np.float32), np.zeros((d, T), np.float32)
    t = np.arange(T)
    pos = np.stack([(t // 64).astype(np.float32), (t % 64).astype(np.float32)], 0)
    ang = pos[sec] * inv[fr][:, None]
    ang = ang.astype(np.float32)
    return np.cos(ang).astype(np.float32), (np.sin(ang).astype(np.float32) * sign[:, None]).astype(np.float32)


def build_nc():
    nc = bass.Bass("TRN2", target_bir_lowering=False)
    try:
        _build(nc)
    except Exception as ex:
        if type(ex).__name__ != "_Stop":
            raise
    return nc


def _build(nc):

    def din(name, shape):
        return nc.dram_tensor(name, list(shape), F32, kind="ExternalInput").ap()

    def dout(name, shape):
        return nc.dram_tensor(name, list(shape), F32, kind="ExternalOutput").ap()

    xT_d = din("xT", [D, T])
    cols_d = din("cols", [128, NCOLS])
    wmod_d = din("wmod", [2, D, 9216])
    wffin_d = din("wffin", [4, 22, 128, 2048])
    wffout_d = din("wffout", [4, DFF, D])
    wina_d = din("wina", [D, 1856])
    wqu_d = din("wqu2", [384, 1536])
    wkvu_d = din("wkvu", [256, 1024])
    wouta_d = din("wouta", [D, D])
    winc_d = din("winc", [D, 2816])
    woutc_d = din("woutc", [D, D])
    rope_d = din("rope", [2, 2, 128, T])
    ckvctx_d = din("ckvctxT", [256, NCTX])
    krctx_d = din("krctxT", [32, NCTX])
    gkctx_d = din("gkctxT", [256, NCTX])
    gvctx_d = din("gvctx", [NCTX, 256])
    ident_d = din("ident", [128, 128])
    yT_d = dout("yT", [D, T])
    ckvT_d = dout("ckvT", [256, T])
    krT_d = dout("krT", [32, T])
    gkT_d = dout("gkT", [256, T])
    gv_d = dout("gv", [T, 256])

    ARENA = 212736
    arena = nc.alloc_sbuf_tensor("arena", [128, ARENA], mybir.dt.uint8).ap()

    def view(off, shape, dt):
        esz = 4 if dt == F32 else 2
        n = int(np.prod(shape)) * esz
        assert off % 4 == 0 and off + n <= ARENA, (off, n)
        v = arena[:, off:off + n].bitcast(dt)
        if len(shape) == 2:
            v = v.rearrange("p (a b) -> p a b", a=shape[0])
        elif len(shape) == 3:
            v = v.rearrange("p (a b c) -> p a b c", a=shape[0], b=shape[1])
        return v

    ps = nc.alloc_psum_tensor("ps", [128, 4096], F32).ap()

    def bank(b, n=1):
        return ps[:, b * 512:(b + n) * 512]

    class Sch:
        def __init__(s):
            s.engs = {"pe": nc.tensor, "act": nc.scalar, "dve": nc.vector, "pool": nc.gpsimd, "sp": nc.sync}
            s.sems, s.cnt = {}, {}
            s.waited = {e: {} for e in s.engs}
            s.lastw, s.rds = {}, {}
            s.pend = {e: [] for e in s.engs}
            s.stores = set()
            s.floor = None

        def sem(s, key):
            if key not in s.sems:
                s.sems[key] = nc.alloc_semaphore("s%d" % len(s.sems))
                s.cnt[key] = 0
            return s.sems[key]

        def _deps(s, e, rd, wr):
            deps = {}

            def add(ev):
                if ev is None:
                    return
                k, v = ev
                if v is None:
                    assert k == e == "pe", ("dependency on unsignalled op", k, e)
                    return
                if deps.get(k, 0) < v:
                    deps[k] = v

            add(s.floor)
            for r in rd:
                add(s.lastw.get(r))
            for w in wr:
                add(s.lastw.get(w))
                for ev in s.rds.get(w, {}).values():
                    add(ev)
            for k, v in deps.items():
                if k == e and e == "pe":
                    continue
                if s.waited[e].get(k, 0) >= v:
                    continue
                s.engs[e].wait_ge(s.sem(k), v)
                s.waited[e][k] = v

        def _record(s, ev, rd, wr):
            for r in rd:
                s.rds.setdefault(r, {})[ev[0]] = ev
            for w in wr:
                s.lastw[w] = ev
                s.rds[w] = {}

        @staticmethod
        def _exp(lst):
            out = []
            for r in lst:
                if r == "hT" or r == "xT":
                    out += [(r, t) for t in range(4)]
                else:
                    out.append(r)
            return out

        def op(s, e, fn, rd=(), wr=(), sig=True):
            rd, wr = s._exp(rd), s._exp(wr)
            psr = [r for r in rd if isinstance(r, tuple) and r[0] == "ps"]
            if psr:
                rd = [r for r in rd if not (isinstance(r, tuple) and r[0] == "ps")]
                wr = list(wr) + psr
            s._deps(e, rd, wr)
            inst = fn()
            ev = [e, None]
            s.pend[e].append(ev)
            s._record(ev, rd, wr)
            if sig:
                s.sem(e)
                s.cnt[e] += 1
                inst.then_inc(s.sems[e], 1)
                for p in s.pend[e]:
                    p[1] = s.cnt[e]
                s.pend[e] = []
            return inst

        def dma(s, q, out, in_, rd=(), wr=(), key=None, store=False):
            rd, wr = s._exp(rd), s._exp(wr)
            s._deps(q, rd, wr)
            sm = s.sem(key)
            s.cnt[key] += 16
            s.engs[q].dma_start(out=out, in_=in_).then_inc(sm, 16)
            ev = [key, s.cnt[key]]
            s._record(ev, rd, wr)
            if store:
                s.stores.add(key)

        def fence(s):
            res = sorted(set(s.lastw.keys()) | set(s.rds.keys()), key=str)
            s.op("dve", lambda: nc.vector.memset(fcell, 0.0), rd=(), wr=res)
            s.floor = s.lastw[res[0]]

        def finish(s):
            for key in sorted(s.stores, key=str):
                nc.sync.wait_ge(s.sems[key], s.cnt[key])

    S = Sch()

    class _Stop(Exception):
        pass

    def checkpoint(name):
        import os
        if os.environ.get("KSTOP", "") == name:
            for c in range(8):
                S.dma("sp", yT_d[c * 128:(c + 1) * 128, :], xT[:, c, :], rd=["xT"], key="st_dbg", store=True)
            S.finish()
            raise _Stop()

    def mm(out, lhsT, rhs, start, stop, rd, wr, sig=True, skip=False):
        if skip:
            return S.op("pe", lambda: nc.tensor.matmul(out, lhsT, rhs, start=start, stop=stop, skip_group_check=True),
                        rd, wr, sig)
        return S.op("pe", lambda: nc.tensor.matmul(out, lhsT, rhs, start=start, stop=stop), rd, wr, sig)

    def act(out, in_, func, rd, wr, bias=None, scale=None):
        kw = {}
        if bias is not None:
            kw["bias"] = bias
        if scale is not None:
            kw["scale"] = scale
        return S.op("act", lambda: nc.scalar.activation(out=out, in_=in_, func=func, **kw), rd, wr)

    def tt(out, a, b, op, rd, wr, e="dve"):
        return S.op(e, lambda: S.engs[e].tensor_tensor(out, a, b, op), rd, wr)

    def ts(out, a, s1, s2, op0, op1, rd, wr, e="dve"):
        return S.op(e, lambda: S.engs[e].tensor_scalar(out, a, s1, s2, op0, op1), rd, wr)

    def stt(out, a, sc, b, op0, op1, rd, wr):
        return S.op("dve", lambda: nc.vector.scalar_tensor_tensor(out, a, sc, b, op0, op1), rd, wr)

    def recip(out, a, rd, wr):
        return S.op("dve", lambda: nc.vector.reciprocal(out, a), rd, wr)

    def cpy(out, a, rd, wr, e="dve"):
        return S.op(e, lambda: S.engs[e].tensor_copy(out, a), rd, wr)

    XT = 0
    HT = 65536
    CONST = 98304
    SC = 102400
    R = 126976
    RSZ = ARENA - R
    xT = view(XT, [8, T], F32)
    hT = view(HT, [8, T], BF16)
    cols = view(CONST, [NCOLS], F32)
    o = CONST + 2304
    ones = view(o, [128], BF16); o += 256
    blk = view(o, [128], BF16); o += 256
    ones32 = view(o, [64], F32); o += 256
    modT = view(o, [2, 72], F32); o += 576
    der = view(o, [2, 48], F32); o += 384
    e_bf = view(o, [8], BF16); o += 32
    fcell = view(o, [1], F32); o += 32
    assert o <= SC
    sq = [view(SC + i * 4096, [4, 512], BF16) for i in range(2)]
    rt = [view(SC + 8192 + i * 2048, [512], F32) for i in range(2)]
    tmp = [view(SC + 12288 + i * 2048, [512], F32) for i in range(2)]
    sa = [view(SC + 16384 + i * 1024, [512], BF16) for i in range(2)]
    tmp2 = [view(SC + 18432 + i * 2048, [512], F32) for i in range(2)]
    ident = view(SC + 22528, [128], BF16)
    dg = [view(SC + 22784 + i * 256, [128], BF16) for i in range(4)]
    rrow = view(SC, [1024], F32)
    bcs = [view(SC + 4096 + i * 1024, [256], F32) for i in range(2)]

    def col(name, i=0, n=1):
        o_ = _COLS[name] + i
        return cols[:, o_:o_ + n]

    S.dma("sp", cols, cols_d, wr=["cols"], key="ld_cols")
    for t4 in range(4):
        S.dma("sp", xT[:, :, t4 * 512:(t4 + 1) * 512],
              xT_d[:, t4 * 512:(t4 + 1) * 512].rearrange("(c p) t -> p c t", p=128),
              wr=[("xT", t4)], key=("ld_x", t4))
    S.op("dve", lambda: nc.vector.memset(ones, 1.0), wr=["ones"])
    S.op("dve", lambda: nc.vector.memset(blk, 0.0), wr=["blk"])
    S.op("dve", lambda: nc.vector.memset(blk[0:64, 0:64], 1.0), wr=["blk"])
    S.op("dve", lambda: nc.vector.memset(blk[64:128, 64:128], 1.0), wr=["blk"])
    S.op("dve", lambda: nc.vector.memset(ones32, 1.0), wr=["ones32"])
    act(e_bf, col("cond", 0, 8), AF.Silu, ["cols"], ["e"])
    S.dma("pool", ident, ident_d, wr=["ident"], key=("ld", "ident"))

    WM = [view(R + 65536 + i * 8192, [8, 512], BF16) for i in range(2)]
    mod_jobs = [(l, p) for l in range(2) for p in range(18)]

    def mod_issue(k):
        l, p = mod_jobs[k]
        slot = k % 2
        S.dma("pool", WM[slot], wmod_d[l, :, p * 512:(p + 1) * 512].rearrange("(kc p) n -> p kc n", p=128),
              wr=[("wm", slot)], key=("ld_wm", slot))

    def mod_compute(k):
        l, p = mod_jobs[k]
        slot = k % 2
        for c in range(4):
            cc = l * 72 + p * 4 + c
            for kc in range(8):
                mm(bank(7)[:, cc:cc + 1], WM[slot][:, kc, c * 128:(c + 1) * 128], e_bf[:, kc:kc + 1],
                   kc == 0, kc == 7, [("wm", slot), "e"], [("ps", 7)], sig=(kc == 7))

    def mod_evac(k0, k1):
        for l in range(2):
            ps_ = [p for (ll, p) in mod_jobs[k0:k1] if ll == l]
            if not ps_:
                continue
            c0, c1 = min(ps_) * 4, (max(ps_) + 1) * 4
            tt(modT[:, l, c0:c1], bank(7)[:, l * 72 + c0:l * 72 + c1], col("bmod", l * 72 + c0, c1 - c0), ALU.add,
               [("ps", 7), "cols"], ["mod"])

    MODP = [("gff1_%d", 1, 2, 0.5), ("gmix_%d", 4, 5, 1.0), ("gff2_%d", 7, 8, 0.5)]

    def mod_der_A(l, i):
        gname, sci, gi, gs = MODP[i]
        A = der[:, l, i * 16:i * 16 + 8]
        ts(A, modT[:, l, sci * 8:sci * 8 + 8], 1.0, 1.0, ALU.mult, ALU.add, ["mod"], ["derA"])
        tt(A, A, col(gname % l, 0, 8), ALU.mult, ["derA", "cols"], ["derA"])

    def mod_der_G(l, i):
        gname, sci, gi, gs = MODP[i]
        G = der[:, l, i * 16 + 8:i * 16 + 16]
        ts(G, modT[:, l, gi * 8:gi * 8 + 8], gs, 0.0, ALU.mult, ALU.add, ["mod"], ["der"])

    def mod_der(l, i):
        mod_der_A(l, i)
        mod_der_G(l, i)

    mod_issue(0)
    for k in range(6):
        if k + 1 < 6:
            mod_issue(k + 1)
        mod_compute(k)
        if k == 3:
            mod_evac(0, 4)
            mod_der_A(0, 0)
    mod_evac(4, 6)
    mod_der_G(0, 0)

    checkpoint("mod")

    def rms_rstd(src_fn, nch, ncols, inv_n, rd, lhs, rti, sqi, psb):
        for c0 in range(0, nch, 4):
            n = min(4, nch - c0)
            sqt = sq[(sqi + c0 // 4) % 2]
            for c in range(n):
                act(sqt[:, c, 0:ncols], src_fn(c0 + c), AF.Square, rd, [("sq", (sqi + c0 // 4) % 2)])
            for c in range(n):
                mm(bank(psb)[:, 0:ncols], lhs, sqt[:, c, 0:ncols], (c0 + c) == 0, (c0 + c) == nch - 1,
                   [("sq", (sqi + c0 // 4) % 2), "ones", "blk"], [("ps", psb)], sig=((c0 + c) == nch - 1 or c == n - 1))
        act(rt[rti][:, 0:ncols], bank(psb)[:, 0:ncols], AF.Ln, [("ps", psb), "cols"], [("rt", rti)],
            bias=col("eps"), scale=inv_n)
        act(rt[rti][:, 0:ncols], rt[rti][:, 0:ncols], AF.Exp, [("rt", rti)], [("rt", rti)], scale=-0.5)

    def norm_t4(l, which, t4):
        A = der[:, l, which * 16:which * 16 + 8]
        shi = [0, 3, 6][which]
        tsl = slice(t4 * 512, (t4 + 1) * 512)
        rms_rstd(lambda c: xT[:, c, tsl], 8, 512, 1.0 / D, [("xT", t4)], ones, t4 % 2, 0, 6 + t4 % 2)
        for kc in range(8):
            k2 = kc % 2
            stt(tmp[k2], xT[:, kc, tsl], A[:, kc:kc + 1], rt[t4 % 2], ALU.mult, ALU.mult,
                [("xT", t4), "derA", ("rt", t4 % 2)], [("tmp", k2)])
            act(hT[:, kc, tsl], tmp[k2], AF.Identity, [("tmp", k2), "mod"], [("hT", t4)],
                bias=modT[:, l, shi * 8 + kc:shi * 8 + kc + 1], scale=1.0)

    def norm_mod(l, which):
        for t4 in range(4):
            norm_t4(l, which, t4)

    def final_t4(t4):
        tsl = slice(t4 * 512, (t4 + 1) * 512)
        rms_rstd(lambda c: xT[:, c, tsl], 8, 512, 1.0 / D, [("xT", t4)], ones, t4 % 2, 0, 6 + t4 % 2)
        for kc in range(8):
            k2 = kc % 2
            stt(tmp2[k2], xT[:, kc, tsl], col("gfinal", kc), rt[t4 % 2], ALU.mult, ALU.mult,
                [("xT", t4), "cols", ("rt", t4 % 2)], [("tmp2", k2)])
            S.dma("sp", yT_d[kc * 128:(kc + 1) * 128, tsl], tmp2[k2], rd=[("tmp2", k2)], key=("st_t2", k2), store=True)

    def ffn(l, f, side=None, ders=(), fence=True, do_norm=True, tail=None):
        which = 0 if f == 0 else 2
        if fence:
            S.fence()
        if do_norm:
            norm_mod(l, which)
        G = der[:, l, which * 16 + 8:which * 16 + 16]
        wi_idx = l * 2 + f
        NB = 4
        gT = view(R, [8, T], BF16)
        wi = [view(R + 32768 + i * 4096, [8, 256], BF16) for i in range(NB)]
        wo = view(R + 49152, [8, 1024], BF16)

        def load_wi(j):
            S.dma("pool", wi[j % NB], wffin_d[wi_idx, j].rearrange("p (kc n) -> p kc n", kc=8),
                  wr=[("wi", j % NB)], key=("ld_wi", j % NB))

        for j in range(NB):
            load_wi(j)
        step = 0
        for (g0, gn) in [(0, 8), (8, 7), (15, 7)]:
            last = (g0 == 15)
            S.dma("pool", wo[:, 0:gn, :], wffout_d[wi_idx, g0 * 128:(g0 + gn) * 128, :].rearrange("(j p) d -> p j d", p=128),
                  wr=["wo"], key="ld_wo")
            for jj in range(gn):
                j = g0 + jj
                if side is not None:
                    k0, k1 = side
                    if k0 + j < k1:
                        mod_issue(k0 + j)
                    if j >= 1 and k0 + j - 1 < k1:
                        mod_compute(k0 + j - 1)
                w = wi[j % NB]
                for t4 in range(4):
                    tsl = slice(t4 * 512, (t4 + 1) * 512)
                    k = step % 2
                    step += 1
                    for half in range(2):
                        b = half * 2 + k
                        for kc in range(8):
                            mm(bank(b), w[:, kc, half * 128:(half + 1) * 128], hT[:, kc, tsl], kc == 0, kc == 7,
                               [("wi", j % NB), ("hT", t4)], [("ps", b)], sig=(kc == 7))
                    act(sa[k], bank(k), AF.Silu, [("ps", k)], [("sa", k)])
                    tt(gT[:, jj, tsl], sa[k], bank(2 + k), ALU.mult, [("sa", k), ("ps", 2 + k)], [("gT", jj)])
                if j + NB < 22:
                    load_wi(j + NB)
            if last and side is not None:
                mod_evac(*side)
                for (ll, ii) in ders:
                    mod_der(ll, ii)
            for t4 in range(4):
                for dc in range(8):
                    if last and tail is not None and t4 >= 1 and dc == 4:
                        tail(t4 - 1)
                    tsl = slice(t4 * 512, (t4 + 1) * 512)
                    b = 4 + (dc + t4) % 2
                    for jj in range(gn):
                        mm(bank(b), wo[:, jj, dc * 128:(dc + 1) * 128], gT[:, jj, tsl], jj == 0, jj == gn - 1,
                           ["wo", ("gT", jj)], [("ps", b)], sig=(jj == gn - 1))
                    stt(xT[:, dc, tsl], bank(b), G[:, dc:dc + 1], xT[:, dc, tsl], ALU.mult, ALU.add,
                        [("ps", b), "der", ("xT", t4)], [("xT", t4)])
            if last and tail is not None:
                tail(3)

    def load_w(dst, src, res, key=None):
        S.dma("pool", dst, src, wr=[res], key=("ld", res))

    osb = [view(SC + i * 4096, [1024], F32) for i in range(2)]
    rrow2 = view(SC + 8192, [1024], F32)

    def attention(q_fn, kT_fn, v_fn, scale, pT, attn_write, outproj_dc, qprep, shared, dummy=None, nbanks=(6, 7),
                  work=None, flush=True):
        steps = [(qb, kt) for qb in range(NQB) for kt in range(KT)]
        if work is None:
            work = []

        def emit_S(si):
            qb, kt = steps[si]
            sb = si % 2
            if shared:
                for ip in range(2):
                    lhsT, krd = kT_fn(0, kt)
                    rhs, qrd = q_fn(ip, qb)
                    mm(bank(sb * 2 + ip).rearrange("p (a b) -> p a b", a=2), lhsT, rhs, True, True, krd + qrd,
                       [("ps", sb * 2 + ip)], sig=True)
            else:
                for i in range(4):
                    lhsT, krd = kT_fn(i, kt)
                    rhs, qrd = q_fn(i, qb)
                    mm(bank(sb * 2, 2)[:, i * 256:(i + 1) * 256], lhsT, rhs, True, True, krd + qrd,
                       [("ps", sb * 2 + i // 2)], sig=(i % 2 == 1))

        def recip_item(qb, i):
            def f():
                slot = qb % 2
                csl = slice(i * 256, (i + 1) * 256)
                recip(rrow2[64:65, csl], osb[slot][64:65, csl], [("osb", slot)], [("rrow", i)])
            return f

        def norm_item(qb, i):
            def f():
                slot = qb % 2
                csl = slice(i * 256, (i + 1) * 256)
                nb = nbanks[i % len(nbanks)]
                mm(bank(nb)[0:64, 0:256], ones32[64:65, 0:64], rrow2[64:65, csl], True, True,
                   [("rrow", i), "ones32"], [("ps", nb)])
                attn_write(qb, i, osb[slot][0:64, csl], bank(nb)[0:64, 0:256], [("osb", slot), ("ps", nb)])
            return f

        for it in qprep(0):
            it()
        emit_S(0)
        qitems = []
        for si, (qb, kt) in enumerate(steps):
            if kt == 15 and qb + 1 < NQB:
                qitems = list(qprep(qb + 1))
            if kt >= 16 and qitems:
                qitems.pop(0)()
            if si + 1 < len(steps):
                emit_S(si + 1)
            sb = si % 2
            pslot = si % 3
            act(pT[pslot], bank(sb * 2, 2), AF.Exp, [("ps", sb * 2), ("ps", sb * 2 + 1), "cols"], [("pT", pslot)],
                bias=col("abias", qb * KT + kt), scale=scale)
            if shared:
                for ip in range(2):
                    vv, vrd = v_fn(0, kt)
                    mm(bank(4 + ip)[:, :], vv, pT[pslot][:, ip * 512:(ip + 1) * 512], kt == 0, kt == KT - 1,
                       vrd + [("pT", pslot)], [("ps", 4 + ip)], sig=(ip == 1), skip=True)
            else:
                for i in range(4):
                    vv, vrd = v_fn(i, kt)
                    ob = 4 + i // 2
                    mm(bank(ob)[:, (i % 2) * 256:(i % 2 + 1) * 256], vv, pT[pslot][:, i * 256:(i + 1) * 256],
                       (kt == 0 and i % 2 == 0), kt == KT - 1, vrd + [("pT", pslot)], [("ps", ob)],
                       sig=(i == 3), skip=True)
            if dummy is not None:
                dummy()
            if work:
                work.pop(0)()
            if kt == KT - 1:
                slot = qb % 2
                cpy(osb[slot][0:65, 0:512], bank(4)[0:65, :], [("ps", 4)], [("osb", slot)])
                cpy(osb[slot][0:65, 512:1024], bank(5)[0:65, :], [("ps", 5)], [("osb", slot)])
                for i in range(4):
                    work.append(recip_item(qb, i))
                for i in range(4):
                    work.append(norm_item(qb, i))
                for dc in range(8):
                    work.append((lambda qb=qb, dc=dc: outproj_dc(qb, dc)))
        while flush and work:
            work.pop(0)()

    ffn(0, 0, side=(6, 24), ders=[(0, 1), (0, 2), (1, 0)], tail=lambda t4: norm_t4(0, 1, t4), fence=False)
    checkpoint("ffn00")
    checkpoint("n01")
    Gm0 = der[:, 0, 24:32]
    cqn = view(R, [3, T], BF16)
    ckvn = view(R + 12288, [2, NK], BF16)
    krall = view(R + 22528, [NK], BF16)
    cosT = view(R + 27648, [T], F32)
    sinT = view(R + 35840, [T], F32)
    GLU = R + 44032
    glu = view(GLU, [4, 8, 286], BF16)
    gl = view(GLU, [4, T], BF16)
    WS = R + 62336
    wpc = [view(WS + i * 6144, [8, 384], BF16) for i in range(3)]
    yc = view(HT, [4, T], F32)

    import os
    SK = os.environ.get("KSKIP", "")
    S.fence()
    if "a" not in SK:
        S.dma("sp", cosT, rope_d[0, 0], wr=["cos"], key="ld_cos")
        S.dma("sp", sinT, rope_d[0, 1], wr=["sin"], key="ld_sin")
    if "b" not in SK:
        for c in range(2):
            load_w(ckvn[:, c, 0:NCTX], ckvctx_d[c * 128:(c + 1) * 128, :], "ckvn_ctx", "ld_ctx")
    if "e" not in SK:
        load_w(krall[64:96, 0:NCTX], krctx_d, "kr_ctx", "ld_ctx")
    if "c" not in SK:
        S.op("pool", lambda: nc.gpsimd.memset(glu[:, :, 0, 0:15], 0.0), wr=["glu_pad"])
        S.op("pool", lambda: nc.gpsimd.memset(glu[:, :, 7, 271:286], 0.0), wr=["glu_pad"])

    pieces = [(0, 384), (384, 256), (640, 192)] + [(832 + 256 * i, 256) for i in range(4)]

    def load_piece(pi):
        c0, w_ = pieces[pi]
        load_w(wpc[pi % 3][:, :, 0:w_], wina_d[:, c0:c0 + w_].rearrange("(kc p) n -> p kc n", p=128),
               ("wpc", pi % 3), ("ld_wpc", pi % 3))

    if "d" not in SK:
        for pi in range(3):
            load_piece(pi)

    checkpoint("mla_ld")

    def proj(w, col0, m, t4, b, wres):
        tsl = slice(t4 * 512, (t4 + 1) * 512)
        for kc in range(8):
            mm(bank(b)[0:m, :], w[:, kc, col0:col0 + m], hT[:, kc, tsl], kc == 0, kc == 7, [wres, ("hT", t4)],
               [("ps", b)], sig=(kc == 7))

    for pi, (nch, gname, inv_n) in enumerate([(3, "gql", 1.0 / 384), (2, "gkvl", 1.0 / 256)]):
        w = wpc[pi % 3]
        for t4 in range(4):
            tsl = slice(t4 * 512, (t4 + 1) * 512)
            for c in range(nch):
                proj(w, c * 128, 128, t4, c, ("wpc", pi % 3))
            rms_rstd(lambda c: bank(c), nch, 512, inv_n, [("ps", c_) for c_ in range(nch)], ones, t4 % 2, 0, 4 + t4 % 2)
            for c in range(nch):
                if pi == 0:
                    stt(tmp[c % 2], bank(c), col(gname, c), rt[t4 % 2], ALU.mult, ALU.mult,
                        [("ps", c), "cols", ("rt", t4 % 2)], [("tmp", c % 2)])
                    act(cqn[:, c, tsl], tmp[c % 2], AF.Copy, [("tmp", c % 2)], ["cqn"])
                else:
                    k2 = (t4 * 2 + c) % 2
                    stt(tmp2[k2], bank(c), col(gname, c), rt[t4 % 2], ALU.mult, ALU.mult,
                        [("ps", c), "cols", ("rt", t4 % 2)], [("tmp2", k2)])
                    act(ckvn[:, c, NCTX + t4 * 512:NCTX + (t4 + 1) * 512], tmp2[k2], AF.Copy, [("tmp2", k2)], ["ckvn"])
                    S.dma("sp", ckvT_d[c * 128:(c + 1) * 128, tsl], tmp2[k2], rd=[("tmp2", k2)], key=("st_t2", k2),
                          store=True)
        load_piece(pi + 3)
    checkpoint("mla_p1")
    w = wpc[2]
    for t4 in range(4):
        tsl = slice(t4 * 512, (t4 + 1) * 512)
        proj(w, 0, 96, t4, 0, ("wpc", 2))
        proj(w, 96, 96, t4, 1, ("wpc", 2))
        k2 = t4 % 2
        tt(tmp[0][64:96, :], bank(0)[64:96, :], cosT[64:96, tsl], ALU.mult, [("ps", 0), "cos"], [("tmp", 0)])
        tt(tmp[1][64:96, :], bank(1)[64:96, :], sinT[64:96, tsl], ALU.mult, [("ps", 1), "sin"], [("tmp", 1)])
        tt(tmp2[k2][64:96, :], tmp[0][64:96, :], tmp[1][64:96, :], ALU.add, [("tmp", 0), ("tmp", 1)], [("tmp2", k2)])
        act(krall[64:96, NCTX + t4 * 512:NCTX + (t4 + 1) * 512], tmp2[k2][64:96, :], AF.Copy, [("tmp2", k2)], ["krall"])
        S.dma("sp", krT_d[:, tsl], tmp2[k2][64:96, :], rd=[("tmp2", k2)], key=("st_t2", k2), store=True)
    load_piece(5)
    checkpoint("mla_p2")
    for c in range(4):
        pi = 3 + c
        w = wpc[pi % 3]
        for t4 in range(4):
            k = t4 % 2
            proj(w, 0, 128, t4, k, ("wpc", pi % 3))
            proj(w, 128, 128, t4, 2 + k, ("wpc", pi % 3))
            act(tmp[k], bank(2 + k), AF.Sigmoid, [("ps", 2 + k)], [("tmp", k)])
            tt(glu[:, c, 2 * t4:2 * t4 + 2, 15:271], bank(k).rearrange("p (a b) -> p a b", a=2),
               tmp[k].rearrange("p (a b) -> p a b", a=2), ALU.mult, [("ps", k), ("tmp", k)], [("glu", c)])
        if pi + 3 < 7:
            load_piece(pi + 3)
        ts(glu[:, c, 1:8, 0:15], glu[:, c, 0:7, 256:271], col("hmask"), 0.0, ALU.mult, ALU.add,
           [("glu", c), "cols"], [("glu", c)])
        ts(glu[:, c, 0:7, 271:286], glu[:, c, 1:8, 15:30], col("hmask"), 0.0, ALU.mult, ALU.add,
           [("glu", c), "cols"], [("glu", c)])
    checkpoint("mla_proj")
    wqu = view(WS, [3, 1536], BF16)
    wkvu = view(WS + 9216, [2, 1024], BF16)
    woa = view(WS + 13312, [4, 1024], BF16)
    S.fence()
    load_w(woa, wouta_d[512:1024, :].rearrange("(kc p) n -> p kc n", p=128), "woa", "ld_woa")
    for c in range(4):
        for j in range(31):
            ds_ = (c * 31 + j) % 4
            ts(dg[ds_], ident, col("wdw", c * 31 + j), 0.0, ALU.mult, ALU.add, ["ident", "cols"], [("dg", ds_)])
            for t4 in range(4):
                mm(bank(t4).rearrange("p (a b) -> p a b", a=2), dg[ds_], glu[:, c, 2 * t4:2 * t4 + 2, j:j + 256],
                   j == 0, j == 30, [("dg", ds_), ("glu", c), "glu_pad"], [("ps", t4)], sig=(t4 == 3))
        for t4 in range(4):
            act(yc[:, c, t4 * 512:(t4 + 1) * 512], bank(t4), AF.Identity, [("ps", t4), "cols", "hT"], ["hT"],
                bias=col("bdw", c), scale=1.0)
    for t4 in range(4):
        tsl = slice(t4 * 512, (t4 + 1) * 512)
        for c in range(4):
            act(sq[0][:, c, :], yc[:, c, tsl], AF.Copy, ["hT"], [("sq", 0)])
            act(sq[1][:, c, :], yc[:, c, tsl], AF.Square, ["hT"], [("sq", 1)])
        for c in range(4):
            mm(bank(6), ones, sq[0][:, c, :], c == 0, c == 3, [("sq", 0), "ones"], [("ps", 6)], sig=(c == 3))
        for c in range(4):
            mm(bank(7), ones, sq[1][:, c, :], c == 0, c == 3, [("sq", 1), "ones"], [("ps", 7)], sig=(c == 3))
        mean, m2 = tmp2[0], tmp2[1]
        ts(mean, bank(6), 1.0 / 512, 0.0, ALU.mult, ALU.add, [("ps", 6)], [("tmp2", 0)])
        tt(m2, mean, mean, ALU.mult, [("tmp2", 0)], [("tmp2", 1)])
        stt(m2, bank(7), 1.0 / 512, m2, ALU.mult, ALU.subtract, [("ps", 7), ("tmp2", 1)], [("tmp2", 1)])
        act(rt[0], m2, AF.Ln, [("tmp2", 1), "cols"], [("rt", 0)], bias=col("eps"), scale=1.0)
        act(rt[0], rt[0], AF.Exp, [("rt", 0)], [("rt", 0)], scale=-0.5)
        for c in range(4):
            k = c % 2
            tt(tmp[k], yc[:, c, tsl], mean, ALU.subtract, ["hT", ("tmp2", 0)], [("tmp", k)])
            tt(tmp[k], tmp[k], rt[0], ALU.mult, [("tmp", k), ("rt", 0)], [("tmp", k)])
            act(gl[:, c, tsl], tmp[k], AF.Silu, [("tmp", k), "cols"] + [("glu", cc) for cc in range(4)],
                [("gl", c)], bias=col("bln", c), scale=col("gln", c))
    for dc in range(8):
        for t4 in range(4):
            tsl = slice(t4 * 512, (t4 + 1) * 512)
            b = 6 + (dc * 4 + t4) % 2
            for c in range(4):
                mm(bank(b), woa[:, c, dc * 128:(dc + 1) * 128], gl[:, c, tsl], c == 0, c == 3,
                   ["woa", ("gl", c)], [("ps", b)], sig=(c == 3))
            stt(xT[:, dc, tsl], bank(b), Gm0[:, dc:dc + 1], xT[:, dc, tsl], ALU.mult, ALU.add,
                [("ps", b), "der", ("xT", t4)], [("xT", t4)])
    checkpoint("mla_conv")
    S.fence()
    load_w(wqu, wqu_d.rearrange("(kc p) n -> p kc n", p=128), "wqu", "ld_wqu")
    load_w(wkvu, wkvu_d.rearrange("(kc p) n -> p kc n", p=128), "wkvu", "ld_wkvu")
    Vt = view(HT, [KT, 4, 65], BF16)
    VtF = view(HT, [KT * 4 * 65 + 63], BF16)
    KTt = view(HT + 10496, [4, NK], BF16)
    ATT = GLU
    Qt = [view(ATT + i * 2048, [4, 256], BF16) for i in range(2)]
    pT = [view(ATT + 4096 + i * 2048, [1024], BF16) for i in range(3)]
    attnb = [view(ATT + 10240 + i * 1024, [2, 256], BF16) for i in range(2)]
    gl_res = [("gl", c) for c in range(4)]
    for i_ in range(2):
        S.op("pool", lambda i_=i_: nc.gpsimd.memset(Qt[i_][64:128, :, :], 0.0), wr=[("Qt", i_)])
    for i_ in range(4):
        S.op("pool", lambda i_=i_: nc.gpsimd.memset(KTt[64:128, i_, :], 0.0), wr=[("KT", i_)])
    mla_work = []
    for g in range(2):
        load_w(woa[:, 2 * g:2 * g + 2, :], wouta_d[g * 256:(g + 1) * 256, :].rearrange("(kc p) n -> p kc n", p=128),
               "woa", "ld_woa")
        S.op("pool", lambda: nc.gpsimd.memset(Vt[:, :, :, 64:65], 1.0), rd=["hT"], wr=["Vt1"])
        for kt in range(KT):
            b = 6 + kt % 2
            for c in range(2):
                rhs = wkvu[:, c, g * 512:(g + 1) * 512].rearrange("p (h d) -> p h d", h=4)[:, :, 64:128]
                mm(bank(b)[:, 0:256].rearrange("p (h d) -> p h d", h=4), ckvn[:, c, kt * 128:(kt + 1) * 128], rhs,
                   c == 0, c == 1, ["ckvn", "ckvn_ctx", "wkvu"], [("ps", b)], sig=(c == 1))
            if kt % 2 == 0:
                cpy(Vt[:, kt, :, 0:64], bank(b)[:, 0:256].rearrange("p (h d) -> p h d", h=4), [("ps", b), "hT"], ["Vt"])
            else:
                act(Vt[:, kt, :, 0:64], bank(b)[:, 0:256].rearrange("p (h d) -> p h d", h=4), AF.Copy,
                    [("ps", b), "hT"], ["Vt"])
        for i in range(4):
            h = 4 * g + i
            for k5 in range(5):
                b = 6 + (i * 5 + k5) % 2
                for c in range(2):
                    mm(bank(b)[0:64, :], wkvu[:, c, h * 128:h * 128 + 64], ckvn[:, c, k5 * 512:(k5 + 1) * 512],
                       c == 0, c == 1, ["ckvn", "ckvn_ctx", "wkvu"], [("ps", b)], sig=(c == 1))
                if k5 % 2 == 0:
                    cpy(KTt[0:64, i, k5 * 512:(k5 + 1) * 512], bank(b)[0:64, :], [("ps", b), "hT"], [("KT", i)])
                else:
                    act(KTt[0:64, i, k5 * 512:(k5 + 1) * 512], bank(b)[0:64, :], AF.Copy, [("ps", b), "hT"], [("KT", i)])
            S.dma("sp", KTt[64:96, i, :], krall[64:96, :], rd=["krall", "kr_ctx"], wr=[("KT", i)],
                  key=("ld", ("KTkr", i)))

        def qprep(qb, g=g):
            return [(lambda i2=i2: qprep_i2(qb, i2)) for i2 in range(2)]

        def qprep_i2(qb, i2, g=g):
            qs = slice(qb * 256, (qb + 1) * 256)
            qt = Qt[qb % 2]
            if True:
                for ii in range(2):
                    i = i2 * 2 + ii
                    h = 4 * g + i
                    for ab in range(2):
                        for c in range(3):
                            mm(bank(6 + ab)[0:96, ii * 256:(ii + 1) * 256],
                               wqu[:, c, ab * 768 + h * 96:ab * 768 + (h + 1) * 96], cqn[:, c, qs], c == 0, c == 2,
                               ["wqu", "cqn"], [("ps", 6 + ab)], sig=(c == 2))
                for ii in range(2):
                    i = i2 * 2 + ii
                    A = bank(6)[:, ii * 256:(ii + 1) * 256]
                    B = bank(7)[:, ii * 256:(ii + 1) * 256]
                    cpy(qt[0:64, i, :], A[0:64, :], [("ps", 6)] + gl_res, [("Qt", qb % 2)])
                    tt(tmp[0][64:96, 0:256], A[64:96, :], cosT[64:96, qs], ALU.mult, [("ps", 6), "cos"], [("tmp", 0)])
                    tt(tmp[1][64:96, 0:256], B[64:96, :], sinT[64:96, qs], ALU.mult, [("ps", 7), "sin"], [("tmp", 1)])
                    tt(qt[64:96, i, :], tmp[0][64:96, 0:256], tmp[1][64:96, 0:256], ALU.add,
                       [("tmp", 0), ("tmp", 1)] + gl_res, [("Qt", qb % 2)])

        def q_fn(i, qb):
            return Qt[qb % 2][:, i, :], [("Qt", qb % 2)]

        def kT_fn(i, kt):
            return KTt[:, i, kt * 128:(kt + 1) * 128], [("KT", i)]

        def v_fn(i, kt):
            o_ = (kt * 4 + i) * 65
            return VtF[:, o_:o_ + 128], ["Vt", "Vt1"]

        def attn_write(qb, i, o_sb, b_ps, rd):
            ab = attnb[qb % 2]
            tt(ab[(i % 2) * 64:(i % 2) * 64 + 64, i // 2, :], o_sb, b_ps, ALU.mult, rd, [("attn", qb % 2)])

        def outproj_dc(qb, dc, g=g):
            qs = slice(qb * 256, (qb + 1) * 256)
            ab = attnb[qb % 2]
            b = 6 + dc % 2
            for pr in range(2):
                mm(bank(b)[:, 0:256], woa[:, 2 * g + pr, dc * 128:(dc + 1) * 128], ab[:, pr, :], pr == 0, pr == 1,
                   ["woa", ("attn", qb % 2)], [("ps", b)], sig=(pr == 1))
            stt(xT[:, dc, qs], bank(b)[:, 0:256], Gm0[:, dc:dc + 1], xT[:, dc, qs], ALU.mult, ALU.add,
                [("ps", b), "der", ("xT", qb // 2)], [("xT", qb // 2)])

        attention(q_fn, kT_fn, v_fn, float(96 ** -0.5), pT, attn_write, outproj_dc, qprep, False,
                  work=mla_work, flush=(g == 1))
    checkpoint("mla_attn")
    ffn(0, 1, side=(24, 36), ders=[(1, 1), (1, 2)], tail=lambda t4: norm_t4(1, 0, t4))
    checkpoint("ffn01")

    ffn(1, 0, fence=False, do_norm=False, tail=lambda t4: norm_t4(1, 1, t4))
    checkpoint("ffn10")
    Gm1 = der[:, 1, 24:32]
    QTa = view(R, [8, T], BF16)
    KTa = view(R + 32768, [2, NK], BF16)
    Vg = view(R + 43008, [KT, 4, 65], BF16)
    VgF = view(R + 43008, [KT * 4 * 65 + 63], BF16)
    cos1 = view(R + 53440, [T], F32)
    sin1 = view(R + 61632, [T], F32)
    wsl = [view(R + 69824 + i * 4096, [8, 256], BF16) for i in range(3)]
    S.fence()
    S.dma("sp", cos1, rope_d[1, 0], wr=["cos"], key="ld_cos")
    S.dma("sp", sin1, rope_d[1, 1], wr=["sin"], key="ld_sin")
    for m in range(2):
        load_w(KTa[:, m, 0:NCTX], gkctx_d[m * 128:(m + 1) * 128, :], "KTa_ctx", "ld_ctx")
    vstage = view(SC + 12288, [4, 256], BF16)
    S.dma("pool", vstage, gvctx_d.rearrange("(kt p) n -> p kt n", p=128), wr=[("tmp", 0)], key=("ld", "vstage"))
    for kt_ in range(4):
        cpy(Vg[:, kt_, :, 0:64], vstage[:, kt_, :].rearrange("p (h d) -> p h d", d=64), [("tmp", 0)], ["Vg_ctx"])
    S.op("pool", lambda: nc.gpsimd.memset(Vg[:, :, :, 64:65], 1.0), wr=["Vg1"])

    def load_pc(pi):
        load_w(wsl[pi % 3], winc_d[:, pi * 256:(pi + 1) * 256].rearrange("(kc p) n -> p kc n", p=128),
               ("wsl", pi % 3), ("ld_wsl", pi % 3))

    for pi in range(3):
        load_pc(pi)
    checkpoint("g_ld")
    gtasks = [(pi, t4) for pi in range(10) for t4 in range(4)]

    def g_chains(idx):
        pi, t4 = gtasks[idx]
        k = idx % 2
        proj(wsl[pi % 3], 0, 128, t4, k, ("wsl", pi % 3))
        proj(wsl[pi % 3], 128, 128, t4, 2 + k, ("wsl", pi % 3))
        if t4 == 3 and pi + 3 < 11:
            load_pc(pi + 3)

    def g_post(idx):
        pi, t4 = gtasks[idx]
        k = idx % 2
        isq = pi < 8
        gname, gpname = ("gq", "gqp") if isq else ("gk", "gkp")
        tsl = slice(t4 * 512, (t4 + 1) * 512)
        rms_rstd(lambda c: bank(k), 1, 512, 1.0 / 64, [("ps", k)], blk, k, k, 4 + k)
        stt(tmp[0], bank(k), col(gname), cos1[:, tsl], ALU.mult, ALU.mult, [("ps", k), "cols", "cos"], [("tmp", 0)])
        stt(tmp[1], bank(2 + k), col(gpname), sin1[:, tsl], ALU.mult, ALU.mult, [("ps", 2 + k), "cols", "sin"],
            [("tmp", 1)])
        tt(tmp[0], tmp[0], tmp[1], ALU.add, [("tmp", 0), ("tmp", 1)], [("tmp", 0)])
        if isq:
            tt(QTa[:, pi, tsl], tmp[0], rt[k], ALU.mult, [("tmp", 0), ("rt", k)], ["QTa"])
        else:
            m = pi - 8
            tt(tmp2[k], tmp[0], rt[k], ALU.mult, [("tmp", 0), ("rt", k)], [("tmp2", k)])
            act(KTa[:, m, NCTX + t4 * 512:NCTX + (t4 + 1) * 512], tmp2[k], AF.Copy, [("tmp2", k)], ["KTa"])
            S.dma("sp", gkT_d[m * 128:(m + 1) * 128, tsl], tmp2[k], rd=[("tmp2", k)], key=("st_t2", k), store=True)

    for idx in range(len(gtasks) + 1):
        if idx < len(gtasks):
            g_chains(idx)
        if idx >= 1:
            g_post(idx - 1)
    w = wsl[10 % 3]
    for t16 in range(16):
        b = t16 % 2
        for kc in range(8):
            mm(bank(b)[:, 0:256], hT[:, kc, t16 * 128:(t16 + 1) * 128], w[:, kc, :], kc == 0, kc == 7,
               [("wsl", 10 % 3), ("hT", t16 // 4)], [("ps", b)], sig=(kc == 7))
        if "x" not in SK:
            act(tmp2[b][:, 0:256], bank(b)[:, 0:256], AF.Copy, [("ps", b)], [("tmp2", b)])
        if "w" not in SK:
            S.dma("sp", gv_d[t16 * 128:(t16 + 1) * 128, :], tmp2[b][:, 0:256], rd=[("tmp2", b)], key=("st_t2", b), store=True)
        if "y" not in SK:
            cpy(Vg[:, 4 + t16, :, 0:64], bank(b)[:, 0:256].rearrange("p (h d) -> p h d", h=4), [("ps", b)], ["Vg"])
    checkpoint("gqa_proj")
    S.fence()
    woc = view(HT, [8, 1024], BF16)
    load_w(woc, woutc_d.rearrange("(kc p) n -> p kc n", p=128), "hT", "ld_woc")
    ATT1 = R + 53440
    pT1 = [view(ATT1 + i * 2048, [1024], BF16) for i in range(3)]
    attn1 = [view(ATT1 + 6144 + i * 2048, [4, 256], BF16) for i in range(2)]
    gqa_work = []
    for g in range(4):
        m, r = g // 2, g % 2

        Qz = [view(ATT1 + 10240 + i * 2048, [4, 256], BF16) for i in range(2)]
        for i_ in range(2):
            S.op("pool", lambda i_=i_, r=r: nc.gpsimd.memset(Qz[i_][(1 - r) * 64:(1 - r) * 64 + 64, :, :], 0.0),
                 wr=[("Qz", i_)])

        def qprep(qb, m=m, r=r):
            return [lambda: qprep1(qb)]

        def qprep1(qb, m=m, r=r):
            cpy(Qz[qb % 2][r * 64:r * 64 + 64, :, :], QTa[r * 64:r * 64 + 64, 4 * m:4 * m + 4, qb * 256:(qb + 1) * 256],
                ["QTa"], [("Qz", qb % 2)], e="pool")

        def q_fn(ip, qb):
            return Qz[qb % 2][:, 2 * ip:2 * ip + 2, :], [("Qz", qb % 2)]

        def kT_fn(i, kt, m=m):
            return KTa[:, m, kt * 128:(kt + 1) * 128], ["KTa", "KTa_ctx"]

        def v_fn(i, kt, g=g):
            o_ = (kt * 4 + g) * 65
            return VgF[:, o_:o_ + 128], ["Vg", "Vg1", "Vg_ctx"]

        def attn_write(qb, i, o_sb, b_ps, rd, r=r):
            ab = attn1[qb % 2]
            tt(ab[r * 64:r * 64 + 64, i, :], o_sb, b_ps, ALU.mult, rd, [("attn", qb % 2)])

        def outproj_dc(qb, dc, m=m, r=r):
            qs = slice(qb * 256, (qb + 1) * 256)
            ab = attn1[qb % 2]
            b = 6 + dc % 2
            for i in range(4):
                mm(bank(b)[:, 0:256], woc[r * 64:r * 64 + 64, 4 * m + i, dc * 128:(dc + 1) * 128],
                   ab[r * 64:r * 64 + 64, i, :], i == 0, i == 3, ["hT", ("attn", qb % 2)], [("ps", b)], sig=(i == 3))
            stt(xT[:, dc, qs], bank(b)[:, 0:256], Gm1[:, dc:dc + 1], xT[:, dc, qs], ALU.mult, ALU.add,
                [("ps", b), "der", ("xT", qb // 2)], [("xT", qb // 2)])

        NDUM = int(os.environ.get("KDUM", "0"))

        def dummy(m=m):
            for _ in range(NDUM):
                mm(bank(6)[:, 0:256], woc[:, 0, 0:128], QTa[:, 0, 0:256], True, True, ["hT", "QTa"], [("ps", 6)],
                   sig=False)

        attention(q_fn, kT_fn, v_fn, float(64 ** -0.5), pT1, attn_write, outproj_dc, qprep, True,
                  dummy=dummy if NDUM else None, nbanks=(6, 7) if not NDUM else (7,),
                  work=gqa_work, flush=(g == 3))
    checkpoint("gqa_attn")
    ffn(1, 1, tail=final_t4)
    checkpoint("ffn11")

    S.finish()


def _prep_shared(inp):
    f = np.float32
    sh = {}
    sh["wmod"] = np.ascontiguousarray(inp["w_mod"], dtype=f)
    sh["ident"] = np.eye(128, dtype=f)
    wffin = np.empty((4, 22, 128, 2048), f)
    wffout = np.empty((4, DFF, D), f)
    for l in range(2):
        for fi, (ni, no) in enumerate([("w_ff1_in", "w_ff1_out"), ("w_ff2_in", "w_ff2_out")]):
            w = np.asarray(inp[ni][l], f)
            a = w[:, :DFF].reshape(8, 128, 22, 128)
            b = w[:, DFF:].reshape(8, 128, 22, 128)
            ab = np.stack([a, b], axis=3)
            wffin[l * 2 + fi] = ab.transpose(2, 1, 0, 3, 4).reshape(22, 128, 2048)
            wffout[l * 2 + fi] = np.asarray(inp[no][l], f)
    sh["wffin"], sh["wffout"] = wffin, wffout
    wa = np.asarray(inp["w_in_a"][0], f)
    p32, _, _, _, _ = _rope_meta(32)
    krA = np.zeros((D, 96), f); krB = np.zeros((D, 96), f)
    krA[:, 64:] = wa[:, 640:672]
    krB[:, 64:] = wa[:, 640:672][:, p32]
    u = wa[:, 672:]
    ab = [np.concatenate([u[:, c * 128:(c + 1) * 128], u[:, 512 + c * 128:512 + (c + 1) * 128]], 1) for c in range(4)]
    sh["wina"] = np.ascontiguousarray(np.concatenate([wa[:, 0:640], krA, krB] + ab, 1))
    wq = np.asarray(inp["w_q_up"][0], f)
    wqB = np.zeros_like(wq)
    for h in range(8):
        wqB[:, h * 96 + 64:(h + 1) * 96] = wq[:, h * 96 + 64:(h + 1) * 96][:, p32]
    sh["wqu2"] = np.ascontiguousarray(np.concatenate([wq, wqB], 1))
    sh["wkvu"] = np.ascontiguousarray(inp["w_kv_up"][0], dtype=f)
    sh["wouta"] = np.ascontiguousarray(inp["w_out_a"][0], dtype=f)
    wc = np.asarray(inp["w_in_c"][0], f)
    p64, _, _, _, _ = _rope_meta(64)
    pcs = []
    for m in range(2):
        for j in range(4):
            hs = [8 * m + j, 8 * m + 4 + j]
            A = np.concatenate([wc[:, h * 64:(h + 1) * 64] for h in hs], 1)
            B = np.concatenate([wc[:, h * 64:(h + 1) * 64][:, p64] for h in hs], 1)
            pcs += [A, B]
    for m in range(2):
        hs = [2 * m, 2 * m + 1]
        A = np.concatenate([wc[:, 1024 + h * 64:1024 + (h + 1) * 64] for h in hs], 1)
        B = np.concatenate([wc[:, 1024 + h * 64:1024 + (h + 1) * 64][:, p64] for h in hs], 1)
        pcs += [A, B]
    pcs.append(wc[:, 1280:1536])
    sh["winc"] = np.ascontiguousarray(np.concatenate(pcs, 1))
    wo = np.asarray(inp["w_out_c"][0], f)
    wop = np.empty_like(wo)
    for m in range(2):
        for j in range(4):
            for r in range(2):
                h = 8 * m + 4 * r + j
                wop[(4 * m + j) * 128 + r * 64:(4 * m + j) * 128 + r * 64 + 64] = wo[h * 64:(h + 1) * 64]
    sh["woutc"] = wop
    return sh


def _colT(v, n):
    return np.asarray(v, np.float32).reshape(n, 128).T


def _prep_core(inp, core, sh):
    f = np.float32
    sample = core >= 4
    m = dict(sh)
    if sample:
        b = core - 4
        x = np.asarray(inp["x_sample"][b], f)
        cond = np.asarray(inp["c"][b], f)
        m["ckvctxT"] = np.ascontiguousarray(np.asarray(inp["cache_mla_ckv"][b, 0], f).T)
        m["krctxT"] = np.ascontiguousarray(np.asarray(inp["cache_mla_krope"][b, 0], f).T)
        m["gkctxT"] = np.ascontiguousarray(np.asarray(inp["cache_gqa_k"][b, 0], f).reshape(NCTX, 256).T)
        m["gvctx"] = np.ascontiguousarray(np.asarray(inp["cache_gqa_v"][b, 0], f).reshape(NCTX, 256))
    else:
        x = np.asarray(inp["x_prompt"][core * 8:(core + 1) * 8], f).reshape(T, D)
        cond = np.asarray(inp["c_ctx"], f)
        m["ckvctxT"] = np.zeros((256, NCTX), f)
        m["krctxT"] = np.zeros((32, NCTX), f)
        m["gkctxT"] = np.zeros((256, NCTX), f)
        m["gvctx"] = np.zeros((NCTX, 256), f)
    m["xT"] = np.ascontiguousarray(x.T)
    cols = np.zeros((128, NCOLS), f)

    def put(name, arr):
        arr = np.asarray(arr, f)
        cols[:, _COLS[name]:_COLS[name] + arr.shape[1]] = arr

    put("cond", _colT(cond, 8))
    for l in range(2):
        put("gff1_%d" % l, _colT(inp["g_ff1"][l], 8))
        put("gmix_%d" % l, _colT(inp["g_mix"][l], 8))
        put("gff2_%d" % l, _colT(inp["g_ff2"][l], 8))
    put("gfinal", _colT(inp["g_final"], 8))
    put("gql", _colT(inp["g_q_lora"][0], 3))
    put("gkvl", _colT(inp["g_kv_lora"][0], 2))
    put("bdw", _colT(inp["b_dw"][0], 4))
    put("gln", _colT(inp["g_conv_ln"][0], 4))
    put("bln", _colT(inp["b_conv_ln"][0], 4))
    wdw = np.asarray(inp["w_dw"][0], f)
    put("wdw", wdw.T.reshape(4, 128, 31).transpose(1, 0, 2).reshape(128, 124))
    p64, _, _, _, _ = _rope_meta(64)
    gq = np.asarray(inp["g_q_head"][0], f); gk = np.asarray(inp["g_k_head"][0], f)
    put("gq", np.tile(gq, 2)[:, None]); put("gqp", np.tile(gq[p64], 2)[:, None])
    put("gk", np.tile(gk, 2)[:, None]); put("gkp", np.tile(gk[p64], 2)[:, None])
    put("hmask", np.full((128, 1), 1.0 if sample else 0.0, f))
    put("eps", np.full((128, 1), EPS, f))
    bm = np.asarray(inp["b_mod"], f)
    put("bmod", np.concatenate([_colT(bm[0], 72), _colT(bm[1], 72)], 1))
    ab = np.zeros((128, NQB, KT), f)
    if not sample:
        ab[:] = -30000.0
        for qb in range(NQB):
            ab[:, qb, 4 + 2 * qb:4 + 2 * qb + 2] = 0.0
    put("abias", ab.reshape(128, NQB * KT))
    m["cols"] = cols
    rope = np.zeros((2, 2, 128, T), f)
    c32, s32 = _rope_tables(32, sample)
    rope[0, 0, 64:96], rope[0, 1, 64:96] = c32, s32
    c64, s64 = _rope_tables(64, sample)
    rope[1, 0] = np.concatenate([c64, c64], 0)
    rope[1, 1] = np.concatenate([s64, s64], 0)
    m["rope"] = rope
    return m


_NC_CACHE = {}


def kernel(**inputs):
    inp = {k: np.asarray(v) for k, v in inputs.items()}
    if "nc" not in _NC_CACHE:
        _NC_CACHE["nc"] = build_nc()
    nc = _NC_CACHE["nc"]
    sh = _prep_shared(inp)
    in_maps = [_prep_core(inp, c, sh) for c in range(8)]
    res = run_bass_kernel_spmd(nc, in_maps, core_ids=list(range(8)))
    r = res.results
    f = np.float32
    y_prompt = np.concatenate([np.asarray(r[c]["yT"], f).T.reshape(8, 256, D) for c in range(4)], 0)
    y_sample = np.stack([np.asarray(r[c]["yT"], f).T for c in range(4, 8)], 0)
    ckv = np.concatenate([np.asarray(r[c]["ckvT"], f).T.reshape(8, 1, 256, 256) for c in range(4)], 0)
    kr = np.concatenate([np.asarray(r[c]["krT"], f).T.reshape(8, 1, 256, 32) for c in range(4)], 0)
    gk = np.concatenate([np.asarray(r[c]["gkT"], f).T.reshape(8, 1, 256, 4, 64) for c in range(4)], 0)
    gvv = np.concatenate([np.asarray(r[c]["gv"], f).reshape(8, 1, 256, 4, 64) for c in range(4)], 0)
    return (np.ascontiguousarray(y_prompt), np.ascontiguousarray(y_sample), np.ascontiguousarray(ckv),
            np.ascontiguousarray(kr), np.ascontiguousarray(gk), np.ascontiguousarray(gvv))
```

```python
import numpy as np
import concourse.bass as bass
import concourse.mybir as mybir
from concourse.bass_utils import run_bass_kernel_spmd

F32 = mybir.dt.float32
BF16 = mybir.dt.bfloat16
ALU = mybir.AluOpType
AF = mybir.ActivationFunctionType

T = 2048
D = 1024
NCTX = 512
NK = NCTX + T
KT = NK // 128
NQB = T // 256
DFF = 2816
EPS = 1e-6

_COLS = {}
_o = 0
for _n, _w in [("cond", 8), ("gff1_0", 8), ("gmix_0", 8), ("gff2_0", 8), ("gff1_1", 8), ("gmix_1", 8),
               ("gff2_1", 8), ("gfinal", 8), ("gql", 3), ("gkvl", 2), ("bdw", 4), ("gln", 4), ("bln", 4),
               ("wdw", 124), ("gq", 1), ("gqp", 1), ("gk", 1), ("gkp", 1), ("hmask", 1), ("eps", 1),
               ("bmod", 144), ("abias", 160)]:
    _COLS[_n] = _o
    _o += _w
NCOLS = _o


def _rope_meta(d):
    half = d // 2
    nf = half // 2
    partner = np.zeros(d, np.int64)
    sign = np.zeros(d, np.float32)
    sec = np.zeros(d, np.int64)
    fr = np.zeros(d, np.int64)
    for i in range(d):
        s, ii = i // half, i % half
        f, part = ii % nf, ii // nf
        partner[i] = i + nf if part == 0 else i - nf
        sign[i] = -1.0 if part == 0 else 1.0
        sec[i] = s
        fr[i] = f
    inv = (np.float32(10000.0) ** (-np.arange(0, half, 2, dtype=np.float32) / np.float32(half))).astype(np.float32)
    return partner, sign, sec, fr, inv


def _rope_tables(d, sample):
    partner, sign, sec, fr, inv = _rope_meta(d)
    if not sample:
        return np.ones((d, T), np.float32), np.zeros((d, T), np.float32)
    t = np.arange(T)
    pos = np.stack([(t // 64).astype(np.float32), (t % 64).astype(np.float32)], 0)
    ang = pos[sec] * inv[fr][:, None]
    ang = ang.astype(np.float32)
    return np.cos(ang).astype(np.float32), (np.sin(ang).astype(np.float32) * sign[:, None]).astype(np.float32)


def build_nc():
    nc = bass.Bass("TRN2", target_bir_lowering=False)
    try:
        _build(nc)
    except Exception as ex:
        if type(ex).__name__ != "_Stop":
            raise
    return nc


def _build(nc):

    def din(name, shape):
        return nc.dram_tensor(name, list(shape), F32, kind="ExternalInput").ap()

    def dout(name, shape):
        return nc.dram_tensor(name, list(shape), F32, kind="ExternalOutput").ap()

    xT_d = din("xT", [D, T])
    cols_d = din("cols", [128, NCOLS])
    wmod_d = din("wmod", [2, D, 9216])
    wffin_d = din("wffin", [4, 22, 128, 2048])
    wffout_d = din("wffout", [4, DFF, D])
    wina_d = din("wina", [D, 1856])
    wqu_d = din("wqu2", [384, 1536])
    wkvu_d = din("wkvu", [256, 1024])
    wouta_d = din("wouta", [D, D])
    winc_d = din("winc", [D, 2816])
    woutc_d = din("woutc", [D, D])
    rope_d = din("rope", [2, 2, 128, T])
    ckvctx_d = din("ckvctxT", [256, NCTX])
    krctx_d = din("krctxT", [32, NCTX])
    gkctx_d = din("gkctxT", [256, NCTX])
    gvctx_d = din("gvctx", [NCTX, 256])
    ident_d = din("ident", [128, 128])
    yT_d = dout("yT", [D, T])
    ckvT_d = dout("ckvT", [256, T])
    krT_d = dout("krT", [32, T])
    gkT_d = dout("gkT", [256, T])
    gv_d = dout("gv", [T, 256])

    ARENA = 212736
    arena = nc.alloc_sbuf_tensor("arena", [128, ARENA], mybir.dt.uint8).ap()

    def view(off, shape, dt):
        esz = 4 if dt == F32 else 2
        n = int(np.prod(shape)) * esz
        assert off % 4 == 0 and off + n <= ARENA, (off, n)
        v = arena[:, off:off + n].bitcast(dt)
        if len(shape) == 2:
            v = v.rearrange("p (a b) -> p a b", a=shape[0])
        elif len(shape) == 3:
            v = v.rearrange("p (a b c) -> p a b c", a=shape[0], b=shape[1])
        return v

    ps = nc.alloc_psum_tensor("ps", [128, 4096], F32).ap()

    def bank(b, n=1):
        return ps[:, b * 512:(b + n) * 512]

    class Sch:
        def __init__(s):
            s.engs = {"pe": nc.tensor, "act": nc.scalar, "dve": nc.vector, "pool": nc.gpsimd, "sp": nc.sync}
            s.sems, s.cnt = {}, {}
            s.waited = {e: {} for e in s.engs}
            s.lastw, s.rds = {}, {}
            s.pend = {e: [] for e in s.engs}
            s.stores = set()
            s.floor = None

        def sem(s, key):
            if key not in s.sems:
                s.sems[key] = nc.alloc_semaphore("s%d" % len(s.sems))
                s.cnt[key] = 0
            return s.sems[key]

        def _deps(s, e, rd, wr):
            deps = {}

            def add(ev):
                if ev is None:
                    return
                k, v = ev
                if v is None:
                    assert k == e == "pe", ("dependency on unsignalled op", k, e)
                    return
                if deps.get(k, 0) < v:
                    deps[k] = v

            add(s.floor)
            for r in rd:
                add(s.lastw.get(r))
            for w in wr:
                add(s.lastw.get(w))
                for ev in s.rds.get(w, {}).values():
                    add(ev)
            for k, v in deps.items():
                if k == e and e == "pe":
                    continue
                if s.waited[e].get(k, 0) >= v:
                    continue
                s.engs[e].wait_ge(s.sem(k), v)
                s.waited[e][k] = v

        def _record(s, ev, rd, wr):
            for r in rd:
                s.rds.setdefault(r, {})[ev[0]] = ev
            for w in wr:
                s.lastw[w] = ev
                s.rds[w] = {}

        @staticmethod
        def _exp(lst):
            out = []
            for r in lst:
                if r == "hT" or r == "xT":
                    out += [(r, t) for t in range(4)]
                else:
                    out.append(r)
            return out

        def op(s, e, fn, rd=(), wr=(), sig=True):
            rd, wr = s._exp(rd), s._exp(wr)
            psr = [r for r in rd if isinstance(r, tuple) and r[0] == "ps"]
            if psr:
                rd = [r for r in rd if not (isinstance(r, tuple) and r[0] == "ps")]
                wr = list(wr) + psr
            s._deps(e, rd, wr)
            inst = fn()
            ev = [e, None]
            s.pend[e].append(ev)
            s._record(ev, rd, wr)
            if sig:
                s.sem(e)
                s.cnt[e] += 1
                inst.then_inc(s.sems[e], 1)
                for p in s.pend[e]:
                    p[1] = s.cnt[e]
                s.pend[e] = []
            return inst

        def dma(s, q, out, in_, rd=(), wr=(), key=None, store=False):
            rd, wr = s._exp(rd), s._exp(wr)
            s._deps(q, rd, wr)
            sm = s.sem(key)
            s.cnt[key] += 16
            s.engs[q].dma_start(out=out, in_=in_).then_inc(sm, 16)
            ev = [key, s.cnt[key]]
            s._record(ev, rd, wr)
            if store:
                s.stores.add(key)

        def fence(s):
            res = sorted(set(s.lastw.keys()) | set(s.rds.keys()), key=str)
            s.op("dve", lambda: nc.vector.memset(fcell, 0.0), rd=(), wr=res)
            s.floor = s.lastw[res[0]]

        def finish(s):
            for key in sorted(s.stores, key=str):
                nc.sync.wait_ge(s.sems[key], s.cnt[key])

    S = Sch()

    class _Stop(Exception):
        pass

    def checkpoint(name):
        import os
        if os.environ.get("KSTOP", "") == name:
            for c in range(8):
                S.dma("sp", yT_d[c * 128:(c + 1) * 128, :], xT[:, c, :], rd=["xT"], key="st_dbg", store=True)
            S.finish()
            raise _Stop()

    def mm(out, lhsT, rhs, start, stop, rd, wr, sig=True, skip=False):
        if skip:
            return S.op("pe", lambda: nc.tensor.matmul(out, lhsT, rhs, start=start, stop=stop, skip_group_check=True),
                        rd, wr, sig)
        return S.op("pe", lambda: nc.tensor.matmul(out, lhsT, rhs, start=start, stop=stop), rd, wr, sig)

    def act(out, in_, func, rd, wr, bias=None, scale=None):
        kw = {}
        if bias is not None:
            kw["bias"] = bias
        if scale is not None:
            kw["scale"] = scale
        return S.op("act", lambda: nc.scalar.activation(out=out, in_=in_, func=func, **kw), rd, wr)

    def tt(out, a, b, op, rd, wr, e="dve"):
        return S.op(e, lambda: S.engs[e].tensor_tensor(out, a, b, op), rd, wr)

    def ts(out, a, s1, s2, op0, op1, rd, wr, e="dve"):
        return S.op(e, lambda: S.engs[e].tensor_scalar(out, a, s1, s2, op0, op1), rd, wr)

    def stt(out, a, sc, b, op0, op1, rd, wr):
        return S.op("dve", lambda: nc.vector.scalar_tensor_tensor(out, a, sc, b, op0, op1), rd, wr)

    def recip(out, a, rd, wr):
        return S.op("dve", lambda: nc.vector.reciprocal(out, a), rd, wr)

    def cpy(out, a, rd, wr, e="dve"):
        return S.op(e, lambda: S.engs[e].tensor_copy(out, a), rd, wr)

    XT = 0
    HT = 65536
    CONST = 98304
    SC = 102400
    R = 126976
    RSZ = ARENA - R
    xT = view(XT, [8, T], F32)
    hT = view(HT, [8, T], BF16)
    cols = view(CONST, [NCOLS], F32)
    o = CONST + 2304
    ones = view(o, [128], BF16); o += 256
    blk = view(o, [128], BF16); o += 256
    ones32 = view(o, [64], F32); o += 256
    modT = view(o, [2, 72], F32); o += 576
    der = view(o, [2, 48], F32); o += 384
    e_bf = view(o, [8], BF16); o += 32
    fcell = view(o, [1], F32); o += 32
    assert o <= SC
    sq = [view(SC + i * 4096, [4, 512], BF16) for i in range(2)]
    rt = [view(SC + 8192 + i * 2048, [512], F32) for i in range(2)]
    tmp = [view(SC + 12288 + i * 2048, [512], F32) for i in range(2)]
    sa = [view(SC + 16384 + i * 1024, [512], BF16) for i in range(2)]
    tmp2 = [view(SC + 18432 + i * 2048, [512], F32) for i in range(2)]
    ident = view(SC + 22528, [128], BF16)
    dg = [view(SC + 22784 + i * 256, [128], BF16) for i in range(4)]
    rrow = view(SC, [1024], F32)
    bcs = [view(SC + 4096 + i * 1024, [256], F32) for i in range(2)]

    def col(name, i=0, n=1):
        o_ = _COLS[name] + i
        return cols[:, o_:o_ + n]

    S.dma("sp", cols, cols_d, wr=["cols"], key="ld_cols")
    for t4 in range(4):
        S.dma("sp", xT[:, :, t4 * 512:(t4 + 1) * 512],
              xT_d[:, t4 * 512:(t4 + 1) * 512].rearrange("(c p) t -> p c t", p=128),
              wr=[("xT", t4)], key=("ld_x", t4))
    S.op("dve", lambda: nc.vector.memset(ones, 1.0), wr=["ones"])
    S.op("dve", lambda: nc.vector.memset(blk, 0.0), wr=["blk"])
    S.op("dve", lambda: nc.vector.memset(blk[0:64, 0:64], 1.0), wr=["blk"])
    S.op("dve", lambda: nc.vector.memset(blk[64:128, 64:128], 1.0), wr=["blk"])
    S.op("dve", lambda: nc.vector.memset(ones32, 1.0), wr=["ones32"])
    act(e_bf, col("cond", 0, 8), AF.Silu, ["cols"], ["e"])
    S.dma("pool", ident, ident_d, wr=["ident"], key=("ld", "ident"))

    WM = [view(R + 65536 + i * 8192, [8, 512], BF16) for i in range(2)]
    mod_jobs = [(l, p) for l in range(2) for p in range(18)]

    def mod_issue(k):
        l, p = mod_jobs[k]
        slot = k % 2
        S.dma("pool", WM[slot], wmod_d[l, :, p * 512:(p + 1) * 512].rearrange("(kc p) n -> p kc n", p=128),
              wr=[("wm", slot)], key=("ld_wm", slot))

    def mod_compute(k):
        l, p = mod_jobs[k]
        slot = k % 2
        for c in range(4):
            cc = l * 72 + p * 4 + c
            for kc in range(8):
                mm(bank(7)[:, cc:cc + 1], WM[slot][:, kc, c * 128:(c + 1) * 128], e_bf[:, kc:kc + 1],
                   kc == 0, kc == 7, [("wm", slot), "e"], [("ps", 7)], sig=(kc == 7))

    def mod_evac(k0, k1):
        for l in range(2):
            ps_ = [p for (ll, p) in mod_jobs[k0:k1] if ll == l]
            if not ps_:
                continue
            c0, c1 = min(ps_) * 4, (max(ps_) + 1) * 4
            tt(modT[:, l, c0:c1], bank(7)[:, l * 72 + c0:l * 72 + c1], col("bmod", l * 72 + c0, c1 - c0), ALU.add,
               [("ps", 7), "cols"], ["mod"])

    MODP = [("gff1_%d", 1, 2, 0.5), ("gmix_%d", 4, 5, 1.0), ("gff2_%d", 7, 8, 0.5)]

    def mod_der_A(l, i):
        gname, sci, gi, gs = MODP[i]
        A = der[:, l, i * 16:i * 16 + 8]
        ts(A, modT[:, l, sci * 8:sci * 8 + 8], 1.0, 1.0, ALU.mult, ALU.add, ["mod"], ["derA"])
        tt(A, A, col(gname % l, 0, 8), ALU.mult, ["derA", "cols"], ["derA"])

    def mod_der_G(l, i):
        gname, sci, gi, gs = MODP[i]
        G = der[:, l, i * 16 + 8:i * 16 + 16]
        ts(G, modT[:, l, gi * 8:gi * 8 + 8], gs, 0.0, ALU.mult, ALU.add, ["mod"], ["der"])

    def mod_der(l, i):
        mod_der_A(l, i)
        mod_der_G(l, i)

    mod_issue(0)
    for k in range(6):
        if k + 1 < 6:
            mod_issue(k + 1)
        mod_compute(k)
        if k == 3:
            mod_evac(0, 4)
            mod_der_A(0, 0)
    mod_evac(4, 6)
    mod_der_G(0, 0)

    checkpoint("mod")

    def rms_rstd(src_fn, nch, ncols, inv_n, rd, lhs, rti, sqi, psb):
        for c0 in range(0, nch, 4):
            n = min(4, nch - c0)
            sqt = sq[(sqi + c0 // 4) % 2]
            for c in range(n):
                act(sqt[:, c, 0:ncols], src_fn(c0 + c), AF.Square, rd, [("sq", (sqi + c0 // 4) % 2)])
            for c in range(n):
                mm(bank(psb)[:, 0:ncols], lhs, sqt[:, c, 0:ncols], (c0 + c) == 0, (c0 + c) == nch - 1,
                   [("sq", (sqi + c0 // 4) % 2), "ones", "blk"], [("ps", psb)], sig=((c0 + c) == nch - 1 or c == n - 1))
        act(rt[rti][:, 0:ncols], bank(psb)[:, 0:ncols], AF.Ln, [("ps", psb), "cols"], [("rt", rti)],
            bias=col("eps"), scale=inv_n)
        act(rt[rti][:, 0:ncols], rt[rti][:, 0:ncols], AF.Exp, [("rt", rti)], [("rt", rti)], scale=-0.5)

    def norm_t4(l, which, t4):
        A = der[:, l, which * 16:which * 16 + 8]
        shi = [0, 3, 6][which]
        tsl = slice(t4 * 512, (t4 + 1) * 512)
        rms_rstd(lambda c: xT[:, c, tsl], 8, 512, 1.0 / D, [("xT", t4)], ones, t4 % 2, 0, 6 + t4 % 2)
        for kc in range(8):
            k2 = kc % 2
            stt(tmp[k2], xT[:, kc, tsl], A[:, kc:kc + 1], rt[t4 % 2], ALU.mult, ALU.mult,
                [("xT", t4), "derA", ("rt", t4 % 2)], [("tmp", k2)])
            act(hT[:, kc, tsl], tmp[k2], AF.Identity, [("tmp", k2), "mod"], [("hT", t4)],
                bias=modT[:, l, shi * 8 + kc:shi * 8 + kc + 1], scale=1.0)

    def norm_mod(l, which):
        for t4 in range(4):
            norm_t4(l, which, t4)

    def final_t4(t4):
        tsl = slice(t4 * 512, (t4 + 1) * 512)
        rms_rstd(lambda c: xT[:, c, tsl], 8, 512, 1.0 / D, [("xT", t4)], ones, t4 % 2, 0, 6 + t4 % 2)
        for kc in range(8):
            k2 = kc % 2
            stt(tmp2[k2], xT[:, kc, tsl], col("gfinal", kc), rt[t4 % 2], ALU.mult, ALU.mult,
                [("xT", t4), "cols", ("rt", t4 % 2)], [("tmp2", k2)])
            S.dma("sp", yT_d[kc * 128:(kc + 1) * 128, tsl], tmp2[k2], rd=[("tmp2", k2)], key=("st_t2", k2), store=True)

    def ffn(l, f, side=None, ders=(), fence=True, do_norm=True, tail=None):
        which = 0 if f == 0 else 2
        if fence:
            S.fence()
        if do_norm:
            norm_mod(l, which)
        G = der[:, l, which * 16 + 8:which * 16 + 16]
        wi_idx = l * 2 + f
        NB = 4
        gT = view(R, [8, T], BF16)
        wi = [view(R + 32768 + i * 4096, [8, 256], BF16) for i in range(NB)]
        wo = view(R + 49152, [8, 1024], BF16)

        def load_wi(j):
            S.dma("pool", wi[j % NB], wffin_d[wi_idx, j].rearrange("p (kc n) -> p kc n", kc=8),
                  wr=[("wi", j % NB)], key=("ld_wi", j % NB))

        for j in range(NB):
            load_wi(j)
        step = 0
        for (g0, gn) in [(0, 8), (8, 7), (15, 7)]:
            last = (g0 == 15)
            S.dma("pool", wo[:, 0:gn, :], wffout_d[wi_idx, g0 * 128:(g0 + gn) * 128, :].rearrange("(j p) d -> p j d", p=128),
                  wr=["wo"], key="ld_wo")
            def side_hook(j):
                if side is not None:
                    k0, k1 = side
                    if k0 + j < k1:
                        mod_issue(k0 + j)
                    if j >= 1 and k0 + j - 1 < k1:
                        mod_compute(k0 + j - 1)

            def body(j, jj, t4):
                nonlocal step
                w = wi[j % NB]
                tsl = slice(t4 * 512, (t4 + 1) * 512)
                k = step % 2
                step += 1
                for half in range(2):
                    b = half * 2 + k
                    for kc in range(8):
                        mm(bank(b), w[:, kc, half * 128:(half + 1) * 128], hT[:, kc, tsl], kc == 0, kc == 7,
                           [("wi", j % NB), ("hT", t4)], [("ps", b)], sig=(kc == 7))
                act(sa[k], bank(k), AF.Silu, [("ps", k)], [("sa", k)])
                tt(gT[:, jj, tsl], sa[k], bank(2 + k), ALU.mult, [("sa", k), ("ps", 2 + k)], [("gT", jj)])

            jstart = 0
            if g0 == 0 and do_norm:
                for j in range(3):
                    side_hook(j)
                for t4 in range(4):
                    for j in range(3):
                        body(j, j, t4)
                for j in range(3):
                    load_wi(j + NB)
                jstart = 3
            for jj in range(jstart, gn):
                j = g0 + jj
                side_hook(j)
                for t4 in range(4):
                    body(j, jj, t4)
                if j + NB < 22:
                    load_wi(j + NB)
            if last and side is not None:
                mod_evac(*side)
                for (ll, ii) in ders:
                    mod_der(ll, ii)
            for t4 in range(4):
                for dc in range(8):
                    if last and tail is not None and t4 >= 1 and dc == 4:
                        tail(t4 - 1)
                    tsl = slice(t4 * 512, (t4 + 1) * 512)
                    b = 4 + (dc + t4) % 2
                    for jj in range(gn):
                        mm(bank(b), wo[:, jj, dc * 128:(dc + 1) * 128], gT[:, jj, tsl], jj == 0, jj == gn - 1,
                           ["wo", ("gT", jj)], [("ps", b)], sig=(jj == gn - 1))
                    stt(xT[:, dc, tsl], bank(b), G[:, dc:dc + 1], xT[:, dc, tsl], ALU.mult, ALU.add,
                        [("ps", b), "der", ("xT", t4)], [("xT", t4)])
            if last and tail is not None:
                tail(3)

    def load_w(dst, src, res, key=None):
        S.dma("pool", dst, src, wr=[res], key=("ld", res))

    osb = [view(SC + i * 4096, [1024], F32) for i in range(2)]
    rrow2 = view(SC + 8192, [1024], F32)

    def attention(q_fn, kT_fn, v_fn, scale, pT, attn_write, outproj_dc, qprep, shared, dummy=None, nbanks=(6, 7),
                  work=None, flush=True):
        steps = [(qb, kt) for qb in range(NQB) for kt in range(KT)]
        if work is None:
            work = []

        def emit_S(si):
            qb, kt = steps[si]
            sb = si % 2
            if shared:
                for ip in range(2):
                    lhsT, krd = kT_fn(0, kt)
                    rhs, qrd = q_fn(ip, qb)
                    mm(bank(sb * 2 + ip).rearrange("p (a b) -> p a b", a=2), lhsT, rhs, True, True, krd + qrd,
                       [("ps", sb * 2 + ip)], sig=True)
            else:
                for i in range(4):
                    lhsT, krd = kT_fn(i, kt)
                    rhs, qrd = q_fn(i, qb)
                    mm(bank(sb * 2, 2)[:, i * 256:(i + 1) * 256], lhsT, rhs, True, True, krd + qrd,
                       [("ps", sb * 2 + i // 2)], sig=(i % 2 == 1))

        def recip_item(qb, i):
            def f():
                slot = qb % 2
                csl = slice(i * 256, (i + 1) * 256)
                recip(rrow2[64:65, csl], osb[slot][64:65, csl], [("osb", slot)], [("rrow", i)])
            return f

        def norm_item(qb, i):
            def f():
                slot = qb % 2
                csl = slice(i * 256, (i + 1) * 256)
                nb = nbanks[i % len(nbanks)]
                mm(bank(nb)[0:64, 0:256], ones32[64:65, 0:64], rrow2[64:65, csl], True, True,
                   [("rrow", i), "ones32"], [("ps", nb)])
                attn_write(qb, i, osb[slot][0:64, csl], bank(nb)[0:64, 0:256], [("osb", slot), ("ps", nb)])
            return f

        for it in qprep(0):
            it()
        emit_S(0)
        qitems = []
        for si, (qb, kt) in enumerate(steps):
            if kt == 15 and qb + 1 < NQB:
                qitems = list(qprep(qb + 1))
            if kt >= 16 and qitems:
                qitems.pop(0)()
            if si + 1 < len(steps):
                emit_S(si + 1)
            sb = si % 2
            pslot = si % 3
            act(pT[pslot], bank(sb * 2, 2), AF.Exp, [("ps", sb * 2), ("ps", sb * 2 + 1), "cols"], [("pT", pslot)],
                bias=col("abias", qb * KT + kt), scale=scale)
            if shared:
                for ip in range(2):
                    vv, vrd = v_fn(0, kt)
                    mm(bank(4 + ip)[:, :], vv, pT[pslot][:, ip * 512:(ip + 1) * 512], kt == 0, kt == KT - 1,
                       vrd + [("pT", pslot)], [("ps", 4 + ip)], sig=(ip == 1), skip=True)
            else:
                for i in range(4):
                    vv, vrd = v_fn(i, kt)
                    ob = 4 + i // 2
                    mm(bank(ob)[:, (i % 2) * 256:(i % 2 + 1) * 256], vv, pT[pslot][:, i * 256:(i + 1) * 256],
                       (kt == 0 and i % 2 == 0), kt == KT - 1, vrd + [("pT", pslot)], [("ps", ob)],
                       sig=(i == 3), skip=True)
            if dummy is not None:
                dummy()
            if work:
                work.pop(0)()
            if kt == KT - 1:
                slot = qb % 2
                cpy(osb[slot][0:65, 0:512], bank(4)[0:65, :], [("ps", 4)], [("osb", slot)])
                cpy(osb[slot][0:65, 512:1024], bank(5)[0:65, :], [("ps", 5)], [("osb", slot)])
                for i in range(4):
                    work.append(recip_item(qb, i))
                for i in range(4):
                    work.append(norm_item(qb, i))
                for dc in range(8):
                    work.append((lambda qb=qb, dc=dc: outproj_dc(qb, dc)))
        while flush and work:
            work.pop(0)()

    ffn(0, 0, side=(6, 24), ders=[(0, 1), (0, 2), (1, 0)], tail=lambda t4: norm_t4(0, 1, t4), fence=False)
    checkpoint("ffn00")
    checkpoint("n01")
    Gm0 = der[:, 0, 24:32]
    cqn = view(R, [3, T], BF16)
    ckvn = view(R + 12288, [2, NK], BF16)
    krall = view(R + 22528, [NK], BF16)
    cosT = view(R + 27648, [T], F32)
    sinT = view(R + 35840, [T], F32)
    GLU = R + 44032
    glu = view(GLU, [4, 8, 286], BF16)
    gl = view(GLU, [4, T], BF16)
    WS = R + 62336
    wpc = [view(WS + i * 6144, [8, 384], BF16) for i in range(3)]
    yc = view(HT, [4, T], F32)

    import os
    SK = os.environ.get("KSKIP", "")
    S.fence()
    if "a" not in SK:
        S.dma("sp", cosT, rope_d[0, 0], wr=["cos"], key="ld_cos")
        S.dma("sp", sinT, rope_d[0, 1], wr=["sin"], key="ld_sin")
    if "b" not in SK:
        for c in range(2):
            load_w(ckvn[:, c, 0:NCTX], ckvctx_d[c * 128:(c + 1) * 128, :], "ckvn_ctx", "ld_ctx")
    if "e" not in SK:
        load_w(krall[64:96, 0:NCTX], krctx_d, "kr_ctx", "ld_ctx")
    if "c" not in SK:
        S.op("pool", lambda: nc.gpsimd.memset(glu[:, :, 0, 0:15], 0.0), wr=["glu_pad"])
        S.op("pool", lambda: nc.gpsimd.memset(glu[:, :, 7, 271:286], 0.0), wr=["glu_pad"])

    pieces = [(0, 384), (384, 256), (640, 192)] + [(832 + 256 * i, 256) for i in range(4)]

    def load_piece(pi):
        c0, w_ = pieces[pi]
        load_w(wpc[pi % 3][:, :, 0:w_], wina_d[:, c0:c0 + w_].rearrange("(kc p) n -> p kc n", p=128),
               ("wpc", pi % 3), ("ld_wpc", pi % 3))

    if "d" not in SK:
        for pi in range(3):
            load_piece(pi)

    checkpoint("mla_ld")

    def proj(w, col0, m, t4, b, wres):
        tsl = slice(t4 * 512, (t4 + 1) * 512)
        for kc in range(8):
            mm(bank(b)[0:m, :], w[:, kc, col0:col0 + m], hT[:, kc, tsl], kc == 0, kc == 7, [wres, ("hT", t4)],
               [("ps", b)], sig=(kc == 7))

    for pi, (nch, gname, inv_n) in enumerate([(3, "gql", 1.0 / 384), (2, "gkvl", 1.0 / 256)]):
        w = wpc[pi % 3]
        for t4 in range(4):
            tsl = slice(t4 * 512, (t4 + 1) * 512)
            for c in range(nch):
                proj(w, c * 128, 128, t4, c, ("wpc", pi % 3))
            rms_rstd(lambda c: bank(c), nch, 512, inv_n, [("ps", c_) for c_ in range(nch)], ones, t4 % 2, 0, 4 + t4 % 2)
            for c in range(nch):
                if pi == 0:
                    stt(tmp[c % 2], bank(c), col(gname, c), rt[t4 % 2], ALU.mult, ALU.mult,
                        [("ps", c), "cols", ("rt", t4 % 2)], [("tmp", c % 2)])
                    act(cqn[:, c, tsl], tmp[c % 2], AF.Copy, [("tmp", c % 2)], ["cqn"])
                else:
                    k2 = (t4 * 2 + c) % 2
                    stt(tmp2[k2], bank(c), col(gname, c), rt[t4 % 2], ALU.mult, ALU.mult,
                        [("ps", c), "cols", ("rt", t4 % 2)], [("tmp2", k2)])
                    act(ckvn[:, c, NCTX + t4 * 512:NCTX + (t4 + 1) * 512], tmp2[k2], AF.Copy, [("tmp2", k2)], ["ckvn"])
                    S.dma("sp", ckvT_d[c * 128:(c + 1) * 128, tsl], tmp2[k2], rd=[("tmp2", k2)], key=("st_t2", k2),
                          store=True)
        load_piece(pi + 3)
    checkpoint("mla_p1")
    w = wpc[2]
    for t4 in range(4):
        tsl = slice(t4 * 512, (t4 + 1) * 512)
        proj(w, 0, 96, t4, 0, ("wpc", 2))
        proj(w, 96, 96, t4, 1, ("wpc", 2))
        k2 = t4 % 2
        tt(tmp[0][64:96, :], bank(0)[64:96, :], cosT[64:96, tsl], ALU.mult, [("ps", 0), "cos"], [("tmp", 0)])
        tt(tmp[1][64:96, :], bank(1)[64:96, :], sinT[64:96, tsl], ALU.mult, [("ps", 1), "sin"], [("tmp", 1)])
        tt(tmp2[k2][64:96, :], tmp[0][64:96, :], tmp[1][64:96, :], ALU.add, [("tmp", 0), ("tmp", 1)], [("tmp2", k2)])
        act(krall[64:96, NCTX + t4 * 512:NCTX + (t4 + 1) * 512], tmp2[k2][64:96, :], AF.Copy, [("tmp2", k2)], ["krall"])
        S.dma("sp", krT_d[:, tsl], tmp2[k2][64:96, :], rd=[("tmp2", k2)], key=("st_t2", k2), store=True)
    load_piece(5)
    checkpoint("mla_p2")
    for c in range(4):
        pi = 3 + c
        w = wpc[pi % 3]
        for t4 in range(4):
            k = t4 % 2
            proj(w, 0, 128, t4, k, ("wpc", pi % 3))
            proj(w, 128, 128, t4, 2 + k, ("wpc", pi % 3))
            act(tmp[k], bank(2 + k), AF.Sigmoid, [("ps", 2 + k)], [("tmp", k)])
            tt(glu[:, c, 2 * t4:2 * t4 + 2, 15:271], bank(k).rearrange("p (a b) -> p a b", a=2),
               tmp[k].rearrange("p (a b) -> p a b", a=2), ALU.mult, [("ps", k), ("tmp", k)], [("glu", c)])
        if pi + 3 < 7:
            load_piece(pi + 3)
        ts(glu[:, c, 1:8, 0:15], glu[:, c, 0:7, 256:271], col("hmask"), 0.0, ALU.mult, ALU.add,
           [("glu", c), "cols"], [("glu", c)])
        ts(glu[:, c, 0:7, 271:286], glu[:, c, 1:8, 15:30], col("hmask"), 0.0, ALU.mult, ALU.add,
           [("glu", c), "cols"], [("glu", c)])
    checkpoint("mla_proj")
    wqu = view(WS, [3, 1536], BF16)
    wkvu = view(WS + 9216, [2, 1024], BF16)
    woa = view(WS + 13312, [4, 1024], BF16)
    S.fence()
    load_w(woa, wouta_d[512:1024, :].rearrange("(kc p) n -> p kc n", p=128), "woa", "ld_woa")
    for c in range(4):
        for j in range(31):
            ds_ = (c * 31 + j) % 4
            ts(dg[ds_], ident, col("wdw", c * 31 + j), 0.0, ALU.mult, ALU.add, ["ident", "cols"], [("dg", ds_)])
            for t4 in range(4):
                mm(bank(t4).rearrange("p (a b) -> p a b", a=2), dg[ds_], glu[:, c, 2 * t4:2 * t4 + 2, j:j + 256],
                   j == 0, j == 30, [("dg", ds_), ("glu", c), "glu_pad"], [("ps", t4)], sig=(t4 == 3))
        for t4 in range(4):
            act(yc[:, c, t4 * 512:(t4 + 1) * 512], bank(t4), AF.Identity, [("ps", t4), "cols", "hT"], ["hT"],
                bias=col("bdw", c), scale=1.0)
    for t4 in range(4):
        tsl = slice(t4 * 512, (t4 + 1) * 512)
        for c in range(4):
            act(sq[0][:, c, :], yc[:, c, tsl], AF.Copy, ["hT"], [("sq", 0)])
            act(sq[1][:, c, :], yc[:, c, tsl], AF.Square, ["hT"], [("sq", 1)])
        for c in range(4):
            mm(bank(6), ones, sq[0][:, c, :], c == 0, c == 3, [("sq", 0), "ones"], [("ps", 6)], sig=(c == 3))
        for c in range(4):
            mm(bank(7), ones, sq[1][:, c, :], c == 0, c == 3, [("sq", 1), "ones"], [("ps", 7)], sig=(c == 3))
        mean, m2 = tmp2[0], tmp2[1]
        ts(mean, bank(6), 1.0 / 512, 0.0, ALU.mult, ALU.add, [("ps", 6)], [("tmp2", 0)])
        tt(m2, mean, mean, ALU.mult, [("tmp2", 0)], [("tmp2", 1)])
        stt(m2, bank(7), 1.0 / 512, m2, ALU.mult, ALU.subtract, [("ps", 7), ("tmp2", 1)], [("tmp2", 1)])
        act(rt[0], m2, AF.Ln, [("tmp2", 1), "cols"], [("rt", 0)], bias=col("eps"), scale=1.0)
        act(rt[0], rt[0], AF.Exp, [("rt", 0)], [("rt", 0)], scale=-0.5)
        for c in range(4):
            k = c % 2
            tt(tmp[k], yc[:, c, tsl], mean, ALU.subtract, ["hT", ("tmp2", 0)], [("tmp", k)])
            tt(tmp[k], tmp[k], rt[0], ALU.mult, [("tmp", k), ("rt", 0)], [("tmp", k)])
            act(gl[:, c, tsl], tmp[k], AF.Silu, [("tmp", k), "cols"] + [("glu", cc) for cc in range(4)],
                [("gl", c)], bias=col("bln", c), scale=col("gln", c))
    for dc in range(8):
        for t4 in range(4):
            tsl = slice(t4 * 512, (t4 + 1) * 512)
            b = 6 + (dc * 4 + t4) % 2
            for c in range(4):
                mm(bank(b), woa[:, c, dc * 128:(dc + 1) * 128], gl[:, c, tsl], c == 0, c == 3,
                   ["woa", ("gl", c)], [("ps", b)], sig=(c == 3))
            stt(xT[:, dc, tsl], bank(b), Gm0[:, dc:dc + 1], xT[:, dc, tsl], ALU.mult, ALU.add,
                [("ps", b), "der", ("xT", t4)], [("xT", t4)])
    checkpoint("mla_conv")
    S.fence()
    load_w(wqu, wqu_d.rearrange("(kc p) n -> p kc n", p=128), "wqu", "ld_wqu")
    load_w(wkvu, wkvu_d.rearrange("(kc p) n -> p kc n", p=128), "wkvu", "ld_wkvu")
    Vt = view(HT, [KT, 4, 65], BF16)
    VtF = view(HT, [KT * 4 * 65 + 63], BF16)
    KTt = view(HT + 10496, [4, NK], BF16)
    ATT = GLU
    Qt = [view(ATT + i * 2048, [4, 256], BF16) for i in range(2)]
    pT = [view(ATT + 4096 + i * 2048, [1024], BF16) for i in range(3)]
    attnb = [view(ATT + 10240 + i * 1024, [2, 256], BF16) for i in range(2)]
    gl_res = [("gl", c) for c in range(4)]
    for i_ in range(2):
        S.op("pool", lambda i_=i_: nc.gpsimd.memset(Qt[i_][64:128, :, :], 0.0), wr=[("Qt", i_)])
    for i_ in range(4):
        S.op("pool", lambda i_=i_: nc.gpsimd.memset(KTt[64:128, i_, :], 0.0), wr=[("KT", i_)])
    mla_work = []
    for g in range(2):
        load_w(woa[:, 2 * g:2 * g + 2, :], wouta_d[g * 256:(g + 1) * 256, :].rearrange("(kc p) n -> p kc n", p=128),
               "woa", "ld_woa")
        S.op("pool", lambda: nc.gpsimd.memset(Vt[:, :, :, 64:65], 1.0), rd=["hT"], wr=["Vt1"])
        for kt in range(KT):
            b = 6 + kt % 2
            for c in range(2):
                rhs = wkvu[:, c, g * 512:(g + 1) * 512].rearrange("p (h d) -> p h d", h=4)[:, :, 64:128]
                mm(bank(b)[:, 0:256].rearrange("p (h d) -> p h d", h=4), ckvn[:, c, kt * 128:(kt + 1) * 128], rhs,
                   c == 0, c == 1, ["ckvn", "ckvn_ctx", "wkvu"], [("ps", b)], sig=(c == 1))
            if kt % 2 == 0:
                cpy(Vt[:, kt, :, 0:64], bank(b)[:, 0:256].rearrange("p (h d) -> p h d", h=4), [("ps", b), "hT"], ["Vt"])
            else:
                act(Vt[:, kt, :, 0:64], bank(b)[:, 0:256].rearrange("p (h d) -> p h d", h=4), AF.Copy,
                    [("ps", b), "hT"], ["Vt"])
        for i in range(4):
            h = 4 * g + i
            for k5 in range(5):
                b = 6 + (i * 5 + k5) % 2
                for c in range(2):
                    mm(bank(b)[0:64, :], wkvu[:, c, h * 128:h * 128 + 64], ckvn[:, c, k5 * 512:(k5 + 1) * 512],
                       c == 0, c == 1, ["ckvn", "ckvn_ctx", "wkvu"], [("ps", b)], sig=(c == 1))
                if k5 % 2 == 0:
                    cpy(KTt[0:64, i, k5 * 512:(k5 + 1) * 512], bank(b)[0:64, :], [("ps", b), "hT"], [("KT", i)])
                else:
                    act(KTt[0:64, i, k5 * 512:(k5 + 1) * 512], bank(b)[0:64, :], AF.Copy, [("ps", b), "hT"], [("KT", i)])
            S.dma("sp", KTt[64:96, i, :], krall[64:96, :], rd=["krall", "kr_ctx"], wr=[("KT", i)],
                  key=("ld", ("KTkr", i)))

        def qprep(qb, g=g):
            return [(lambda i2=i2: qprep_i2(qb, i2)) for i2 in range(2)]

        def qprep_i2(qb, i2, g=g):
            qs = slice(qb * 256, (qb + 1) * 256)
            qt = Qt[qb % 2]
            if True:
                for ii in range(2):
                    i = i2 * 2 + ii
                    h = 4 * g + i
                    for ab in range(2):
                        for c in range(3):
                            mm(bank(6 + ab)[0:96, ii * 256:(ii + 1) * 256],
                               wqu[:, c, ab * 768 + h * 96:ab * 768 + (h + 1) * 96], cqn[:, c, qs], c == 0, c == 2,
                               ["wqu", "cqn"], [("ps", 6 + ab)], sig=(c == 2))
                for ii in range(2):
                    i = i2 * 2 + ii
                    A = bank(6)[:, ii * 256:(ii + 1) * 256]
                    B = bank(7)[:, ii * 256:(ii + 1) * 256]
                    cpy(qt[0:64, i, :], A[0:64, :], [("ps", 6)] + gl_res, [("Qt", qb % 2)])
                    tt(tmp[0][64:96, 0:256], A[64:96, :], cosT[64:96, qs], ALU.mult, [("ps", 6), "cos"], [("tmp", 0)])
                    tt(tmp[1][64:96, 0:256], B[64:96, :], sinT[64:96, qs], ALU.mult, [("ps", 7), "sin"], [("tmp", 1)])
                    tt(qt[64:96, i, :], tmp[0][64:96, 0:256], tmp[1][64:96, 0:256], ALU.add,
                       [("tmp", 0), ("tmp", 1)] + gl_res, [("Qt", qb % 2)])

        def q_fn(i, qb):
            return Qt[qb % 2][:, i, :], [("Qt", qb % 2)]

        def kT_fn(i, kt):
            return KTt[:, i, kt * 128:(kt + 1) * 128], [("KT", i)]

        def v_fn(i, kt):
            o_ = (kt * 4 + i) * 65
            return VtF[:, o_:o_ + 128], ["Vt", "Vt1"]

        def attn_write(qb, i, o_sb, b_ps, rd):
            ab = attnb[qb % 2]
            tt(ab[(i % 2) * 64:(i % 2) * 64 + 64, i // 2, :], o_sb, b_ps, ALU.mult, rd, [("attn", qb % 2)])

        def outproj_dc(qb, dc, g=g):
            qs = slice(qb * 256, (qb + 1) * 256)
            ab = attnb[qb % 2]
            b = 6 + dc % 2
            for pr in range(2):
                mm(bank(b)[:, 0:256], woa[:, 2 * g + pr, dc * 128:(dc + 1) * 128], ab[:, pr, :], pr == 0, pr == 1,
                   ["woa", ("attn", qb % 2)], [("ps", b)], sig=(pr == 1))
            stt(xT[:, dc, qs], bank(b)[:, 0:256], Gm0[:, dc:dc + 1], xT[:, dc, qs], ALU.mult, ALU.add,
                [("ps", b), "der", ("xT", qb // 2)], [("xT", qb // 2)])

        attention(q_fn, kT_fn, v_fn, float(96 ** -0.5), pT, attn_write, outproj_dc, qprep, False,
                  work=mla_work, flush=(g == 1))
    checkpoint("mla_attn")
    ffn(0, 1, side=(24, 36), ders=[(1, 1), (1, 2)], tail=lambda t4: norm_t4(1, 0, t4))
    checkpoint("ffn01")

    ffn(1, 0, fence=False, do_norm=False, tail=lambda t4: norm_t4(1, 1, t4))
    checkpoint("ffn10")
    Gm1 = der[:, 1, 24:32]
    QTa = view(R, [8, T], BF16)
    KTa = view(R + 32768, [2, NK], BF16)
    Vg = view(R + 43008, [KT, 4, 65], BF16)
    VgF = view(R + 43008, [KT * 4 * 65 + 63], BF16)
    cos1 = view(R + 53440, [T], F32)
    sin1 = view(R + 61632, [T], F32)
    wsl = [view(R + 69824 + i * 4096, [8, 256], BF16) for i in range(3)]
    S.fence()
    S.dma("sp", cos1, rope_d[1, 0], wr=["cos"], key="ld_cos")
    S.dma("sp", sin1, rope_d[1, 1], wr=["sin"], key="ld_sin")
    for m in range(2):
        load_w(KTa[:, m, 0:NCTX], gkctx_d[m * 128:(m + 1) * 128, :], "KTa_ctx", "ld_ctx")
    vstage = view(SC + 12288, [4, 256], BF16)
    S.dma("pool", vstage, gvctx_d.rearrange("(kt p) n -> p kt n", p=128), wr=[("tmp", 0)], key=("ld", "vstage"))
    for kt_ in range(4):
        cpy(Vg[:, kt_, :, 0:64], vstage[:, kt_, :].rearrange("p (h d) -> p h d", d=64), [("tmp", 0)], ["Vg_ctx"])
    S.op("pool", lambda: nc.gpsimd.memset(Vg[:, :, :, 64:65], 1.0), wr=["Vg1"])

    def load_pc(pi):
        load_w(wsl[pi % 3], winc_d[:, pi * 256:(pi + 1) * 256].rearrange("(kc p) n -> p kc n", p=128),
               ("wsl", pi % 3), ("ld_wsl", pi % 3))

    for pi in range(3):
        load_pc(pi)
    checkpoint("g_ld")
    gtasks = [(pi, t4) for pi in range(10) for t4 in range(4)]

    def g_chains(idx):
        pi, t4 = gtasks[idx]
        k = idx % 2
        proj(wsl[pi % 3], 0, 128, t4, k, ("wsl", pi % 3))
        proj(wsl[pi % 3], 128, 128, t4, 2 + k, ("wsl", pi % 3))
        if t4 == 3 and pi + 3 < 11:
            load_pc(pi + 3)

    def g_post(idx):
        pi, t4 = gtasks[idx]
        k = idx % 2
        isq = pi < 8
        gname, gpname = ("gq", "gqp") if isq else ("gk", "gkp")
        tsl = slice(t4 * 512, (t4 + 1) * 512)
        rms_rstd(lambda c: bank(k), 1, 512, 1.0 / 64, [("ps", k)], blk, k, k, 4 + k)
        stt(tmp[0], bank(k), col(gname), cos1[:, tsl], ALU.mult, ALU.mult, [("ps", k), "cols", "cos"], [("tmp", 0)])
        stt(tmp[1], bank(2 + k), col(gpname), sin1[:, tsl], ALU.mult, ALU.mult, [("ps", 2 + k), "cols", "sin"],
            [("tmp", 1)])
        tt(tmp[0], tmp[0], tmp[1], ALU.add, [("tmp", 0), ("tmp", 1)], [("tmp", 0)])
        if isq:
            tt(QTa[:, pi, tsl], tmp[0], rt[k], ALU.mult, [("tmp", 0), ("rt", k)], ["QTa"])
        else:
            m = pi - 8
            tt(tmp2[k], tmp[0], rt[k], ALU.mult, [("tmp", 0), ("rt", k)], [("tmp2", k)])
            act(KTa[:, m, NCTX + t4 * 512:NCTX + (t4 + 1) * 512], tmp2[k], AF.Copy, [("tmp2", k)], ["KTa"])
            S.dma("sp", gkT_d[m * 128:(m + 1) * 128, tsl], tmp2[k], rd=[("tmp2", k)], key=("st_t2", k), store=True)

    for idx in range(len(gtasks) + 1):
        if idx < len(gtasks):
            g_chains(idx)
        if idx >= 1:
            g_post(idx - 1)
    w = wsl[10 % 3]
    for t16 in range(16):
        b = t16 % 2
        for kc in range(8):
            mm(bank(b)[:, 0:256], hT[:, kc, t16 * 128:(t16 + 1) * 128], w[:, kc, :], kc == 0, kc == 7,
               [("wsl", 10 % 3), ("hT", t16 // 4)], [("ps", b)], sig=(kc == 7))
        if "x" not in SK:
            act(tmp2[b][:, 0:256], bank(b)[:, 0:256], AF.Copy, [("ps", b)], [("tmp2", b)])
        if "w" not in SK:
            S.dma("sp", gv_d[t16 * 128:(t16 + 1) * 128, :], tmp2[b][:, 0:256], rd=[("tmp2", b)], key=("st_t2", b), store=True)
        if "y" not in SK:
            cpy(Vg[:, 4 + t16, :, 0:64], bank(b)[:, 0:256].rearrange("p (h d) -> p h d", h=4), [("ps", b)], ["Vg"])
    checkpoint("gqa_proj")
    S.fence()
    woc = view(HT, [8, 1024], BF16)
    load_w(woc, woutc_d.rearrange("(kc p) n -> p kc n", p=128), "hT", "ld_woc")
    ATT1 = R + 53440
    pT1 = [view(ATT1 + i * 2048, [1024], BF16) for i in range(3)]
    attn1 = [view(ATT1 + 6144 + i * 2048, [4, 256], BF16) for i in range(2)]
    gqa_work = []
    for g in range(4):
        m, r = g // 2, g % 2

        Qz = [view(ATT1 + 10240 + i * 2048, [4, 256], BF16) for i in range(2)]
        for i_ in range(2):
            S.op("pool", lambda i_=i_, r=r: nc.gpsimd.memset(Qz[i_][(1 - r) * 64:(1 - r) * 64 + 64, :, :], 0.0),
                 wr=[("Qz", i_)])

        def qprep(qb, m=m, r=r):
            return [lambda: qprep1(qb)]

        def qprep1(qb, m=m, r=r):
            cpy(Qz[qb % 2][r * 64:r * 64 + 64, :, :], QTa[r * 64:r * 64 + 64, 4 * m:4 * m + 4, qb * 256:(qb + 1) * 256],
                ["QTa"], [("Qz", qb % 2)], e="pool")

        def q_fn(ip, qb):
            return Qz[qb % 2][:, 2 * ip:2 * ip + 2, :], [("Qz", qb % 2)]

        def kT_fn(i, kt, m=m):
            return KTa[:, m, kt * 128:(kt + 1) * 128], ["KTa", "KTa_ctx"]

        def v_fn(i, kt, g=g):
            o_ = (kt * 4 + g) * 65
            return VgF[:, o_:o_ + 128], ["Vg", "Vg1", "Vg_ctx"]

        def attn_write(qb, i, o_sb, b_ps, rd, r=r):
            ab = attn1[qb % 2]
            tt(ab[r * 64:r * 64 + 64, i, :], o_sb, b_ps, ALU.mult, rd, [("attn", qb % 2)])

        def outproj_dc(qb, dc, m=m, r=r):
            qs = slice(qb * 256, (qb + 1) * 256)
            ab = attn1[qb % 2]
            b = 6 + dc % 2
            for i in range(4):
                mm(bank(b)[:, 0:256], woc[r * 64:r * 64 + 64, 4 * m + i, dc * 128:(dc + 1) * 128],
                   ab[r * 64:r * 64 + 64, i, :], i == 0, i == 3, ["hT", ("attn", qb % 2)], [("ps", b)], sig=(i == 3))
            stt(xT[:, dc, qs], bank(b)[:, 0:256], Gm1[:, dc:dc + 1], xT[:, dc, qs], ALU.mult, ALU.add,
                [("ps", b), "der", ("xT", qb // 2)], [("xT", qb // 2)])

        NDUM = int(os.environ.get("KDUM", "0"))

        def dummy(m=m):
            for _ in range(NDUM):
                mm(bank(6)[:, 0:256], woc[:, 0, 0:128], QTa[:, 0, 0:256], True, True, ["hT", "QTa"], [("ps", 6)],
                   sig=False)

        attention(q_fn, kT_fn, v_fn, float(64 ** -0.5), pT1, attn_write, outproj_dc, qprep, True,
                  dummy=dummy if NDUM else None, nbanks=(6, 7) if not NDUM else (7,),
                  work=gqa_work, flush=(g == 3))
    checkpoint("gqa_attn")
    ffn(1, 1, tail=final_t4)
    checkpoint("ffn11")

    S.finish()


def _prep_shared(inp):
    f = np.float32
    sh = {}
    sh["wmod"] = np.ascontiguousarray(inp["w_mod"], dtype=f)
    sh["ident"] = np.eye(128, dtype=f)
    wffin = np.empty((4, 22, 128, 2048), f)
    wffout = np.empty((4, DFF, D), f)
    for l in range(2):
        for fi, (ni, no) in enumerate([("w_ff1_in", "w_ff1_out"), ("w_ff2_in", "w_ff2_out")]):
            w = np.asarray(inp[ni][l], f)
            a = w[:, :DFF].reshape(8, 128, 22, 128)
            b = w[:, DFF:].reshape(8, 128, 22, 128)
            ab = np.stack([a, b], axis=3)
            wffin[l * 2 + fi] = ab.transpose(2, 1, 0, 3, 4).reshape(22, 128, 2048)
            wffout[l * 2 + fi] = np.asarray(inp[no][l], f)
    sh["wffin"], sh["wffout"] = wffin, wffout
    wa = np.asarray(inp["w_in_a"][0], f)
    p32, _, _, _, _ = _rope_meta(32)
    krA = np.zeros((D, 96), f); krB = np.zeros((D, 96), f)
    krA[:, 64:] = wa[:, 640:672]
    krB[:, 64:] = wa[:, 640:672][:, p32]
    u = wa[:, 672:]
    ab = [np.concatenate([u[:, c * 128:(c + 1) * 128], u[:, 512 + c * 128:512 + (c + 1) * 128]], 1) for c in range(4)]
    sh["wina"] = np.ascontiguousarray(np.concatenate([wa[:, 0:640], krA, krB] + ab, 1))
    wq = np.asarray(inp["w_q_up"][0], f)
    wqB = np.zeros_like(wq)
    for h in range(8):
        wqB[:, h * 96 + 64:(h + 1) * 96] = wq[:, h * 96 + 64:(h + 1) * 96][:, p32]
    sh["wqu2"] = np.ascontiguousarray(np.concatenate([wq, wqB], 1))
    sh["wkvu"] = np.ascontiguousarray(inp["w_kv_up"][0], dtype=f)
    sh["wouta"] = np.ascontiguousarray(inp["w_out_a"][0], dtype=f)
    wc = np.asarray(inp["w_in_c"][0], f)
    p64, _, _, _, _ = _rope_meta(64)
    pcs = []
    for m in range(2):
        for j in range(4):
            hs = [8 * m + j, 8 * m + 4 + j]
            A = np.concatenate([wc[:, h * 64:(h + 1) * 64] for h in hs], 1)
            B = np.concatenate([wc[:, h * 64:(h + 1) * 64][:, p64] for h in hs], 1)
            pcs += [A, B]
    for m in range(2):
        hs = [2 * m, 2 * m + 1]
        A = np.concatenate([wc[:, 1024 + h * 64:1024 + (h + 1) * 64] for h in hs], 1)
        B = np.concatenate([wc[:, 1024 + h * 64:1024 + (h + 1) * 64][:, p64] for h in hs], 1)
        pcs += [A, B]
    pcs.append(wc[:, 1280:1536])
    sh["winc"] = np.ascontiguousarray(np.concatenate(pcs, 1))
    wo = np.asarray(inp["w_out_c"][0], f)
    wop = np.empty_like(wo)
    for m in range(2):
        for j in range(4):
            for r in range(2):
                h = 8 * m + 4 * r + j
                wop[(4 * m + j) * 128 + r * 64:(4 * m + j) * 128 + r * 64 + 64] = wo[h * 64:(h + 1) * 64]
    sh["woutc"] = wop
    return sh


def _colT(v, n):
    return np.asarray(v, np.float32).reshape(n, 128).T


def _prep_core(inp, core, sh):
    f = np.float32
    sample = core >= 4
    m = dict(sh)
    if sample:
        b = core - 4
        x = np.asarray(inp["x_sample"][b], f)
        cond = np.asarray(inp["c"][b], f)
        m["ckvctxT"] = np.ascontiguousarray(np.asarray(inp["cache_mla_ckv"][b, 0], f).T)
        m["krctxT"] = np.ascontiguousarray(np.asarray(inp["cache_mla_krope"][b, 0], f).T)
        m["gkctxT"] = np.ascontiguousarray(np.asarray(inp["cache_gqa_k"][b, 0], f).reshape(NCTX, 256).T)
        m["gvctx"] = np.ascontiguousarray(np.asarray(inp["cache_gqa_v"][b, 0], f).reshape(NCTX, 256))
    else:
        x = np.asarray(inp["x_prompt"][core * 8:(core + 1) * 8], f).reshape(T, D)
        cond = np.asarray(inp["c_ctx"], f)
        m["ckvctxT"] = np.zeros((256, NCTX), f)
        m["krctxT"] = np.zeros((32, NCTX), f)
        m["gkctxT"] = np.zeros((256, NCTX), f)
        m["gvctx"] = np.zeros((NCTX, 256), f)
    m["xT"] = np.ascontiguousarray(x.T)
    cols = np.zeros((128, NCOLS), f)

    def put(name, arr):
        arr = np.asarray(arr, f)
        cols[:, _COLS[name]:_COLS[name] + arr.shape[1]] = arr

    put("cond", _colT(cond, 8))
    for l in range(2):
        put("gff1_%d" % l, _colT(inp["g_ff1"][l], 8))
        put("gmix_%d" % l, _colT(inp["g_mix"][l], 8))
        put("gff2_%d" % l, _colT(inp["g_ff2"][l], 8))
    put("gfinal", _colT(inp["g_final"], 8))
    put("gql", _colT(inp["g_q_lora"][0], 3))
    put("gkvl", _colT(inp["g_kv_lora"][0], 2))
    put("bdw", _colT(inp["b_dw"][0], 4))
    put("gln", _colT(inp["g_conv_ln"][0], 4))
    put("bln", _colT(inp["b_conv_ln"][0], 4))
    wdw = np.asarray(inp["w_dw"][0], f)
    put("wdw", wdw.T.reshape(4, 128, 31).transpose(1, 0, 2).reshape(128, 124))
    p64, _, _, _, _ = _rope_meta(64)
    gq = np.asarray(inp["g_q_head"][0], f); gk = np.asarray(inp["g_k_head"][0], f)
    put("gq", np.tile(gq, 2)[:, None]); put("gqp", np.tile(gq[p64], 2)[:, None])
    put("gk", np.tile(gk, 2)[:, None]); put("gkp", np.tile(gk[p64], 2)[:, None])
    put("hmask", np.full((128, 1), 1.0 if sample else 0.0, f))
    put("eps", np.full((128, 1), EPS, f))
    bm = np.asarray(inp["b_mod"], f)
    put("bmod", np.concatenate([_colT(bm[0], 72), _colT(bm[1], 72)], 1))
    ab = np.zeros((128, NQB, KT), f)
    if not sample:
        ab[:] = -30000.0
        for qb in range(NQB):
            ab[:, qb, 4 + 2 * qb:4 + 2 * qb + 2] = 0.0
    put("abias", ab.reshape(128, NQB * KT))
    m["cols"] = cols
    rope = np.zeros((2, 2, 128, T), f)
    c32, s32 = _rope_tables(32, sample)
    rope[0, 0, 64:96], rope[0, 1, 64:96] = c32, s32
    c64, s64 = _rope_tables(64, sample)
    rope[1, 0] = np.concatenate([c64, c64], 0)
    rope[1, 1] = np.concatenate([s64, s64], 0)
    m["rope"] = rope
    return m


_NC_CACHE = {}


def kernel(**inputs):
    inp = {k: np.asarray(v) for k, v in inputs.items()}
    if "nc" not in _NC_CACHE:
        _NC_CACHE["nc"] = build_nc()
    nc = _NC_CACHE["nc"]
    sh = _prep_shared(inp)
    in_maps = [_prep_core(inp, c, sh) for c in range(8)]
    res = run_bass_kernel_spmd(nc, in_maps, core_ids=list(range(8)))
    r = res.results
    f = np.float32
    y_prompt = np.concatenate([np.asarray(r[c]["yT"], f).T.reshape(8, 256, D) for c in range(4)], 0)
    y_sample = np.stack([np.asarray(r[c]["yT"], f).T for c in range(4, 8)], 0)
    ckv = np.concatenate([np.asarray(r[c]["ckvT"], f).T.reshape(8, 1, 256, 256) for c in range(4)], 0)
    kr = np.concatenate([np.asarray(r[c]["krT"], f).T.reshape(8, 1, 256, 32) for c in range(4)], 0)
    gk = np.concatenate([np.asarray(r[c]["gkT"], f).T.reshape(8, 1, 256, 4, 64) for c in range(4)], 0)
    gvv = np.concatenate([np.asarray(r[c]["gv"], f).reshape(8, 1, 256, 4, 64) for c in range(4)], 0)
    return (np.ascontiguousarray(y_prompt), np.ascontiguousarray(y_sample), np.ascontiguousarray(ckv),
            np.ascontiguousarray(kr), np.ascontiguousarray(gk), np.ascontiguousarray(gvv))
```

```python
import numpy as np
import concourse.bass as bass
import concourse.mybir as mybir
from concourse.bass_utils import run_bass_kernel_spmd

F32 = mybir.dt.float32
BF16 = mybir.dt.bfloat16
ALU = mybir.AluOpType
AF = mybir.ActivationFunctionType

T = 2048
D = 1024
NCTX = 512
NK = NCTX + T
KT = NK // 128
NQB = T // 256
DFF = 2816
EPS = 1e-6

_COLS = {}
_o = 0
for _n, _w in [("cond", 8), ("gff1_0", 8), ("gmix_0", 8), ("gff2_0", 8), ("gff1_1", 8), ("gmix_1", 8),
               ("gff2_1", 8), ("gfinal", 8), ("gql", 3), ("gkvl", 2), ("bdw", 4), ("gln", 4), ("bln", 4),
               ("wdw", 124), ("gq", 1), ("gqp", 1), ("gk", 1), ("gkp", 1), ("hmask", 1), ("eps", 1),
               ("bmod", 144), ("abias", 160)]:
    _COLS[_n] = _o
    _o += _w
NCOLS = _o


def _rope_meta(d):
    half = d // 2
    nf = half // 2
    partner = np.zeros(d, np.int64)
    sign = np.zeros(d, np.float32)
    sec = np.zeros(d, np.int64)
    fr = np.zeros(d, np.int64)
    for i in range(d):
        s, ii = i // half, i % half
        f, part = ii % nf, ii // nf
        partner[i] = i + nf if part == 0 else i - nf
        sign[i] = -1.0 if part == 0 else 1.0
        sec[i] = s
        fr[i] = f
    inv = (np.float32(10000.0) ** (-np.arange(0, half, 2, dtype=np.float32) / np.float32(half))).astype(np.float32)
    return partner, sign, sec, fr, inv


def _rope_tables(d, sample):
    partner, sign, sec, fr, inv = _rope_meta(d)
    if not sample:
        return np.ones((d, T), np.float32), np.zeros((d, T), np.float32)
    t = np.arange(T)
    pos = np.stack([(t // 64).astype(np.float32), (t % 64).astype(np.float32)], 0)
    ang = pos[sec] * inv[fr][:, None]
    ang = ang.astype(np.float32)
    return np.cos(ang).astype(np.float32), (np.sin(ang).astype(np.float32) * sign[:, None]).astype(np.float32)


def build_nc():
    nc = bass.Bass("TRN2", target_bir_lowering=False)
    try:
        _build(nc)
    except Exception as ex:
        if type(ex).__name__ != "_Stop":
            raise
    return nc


def _build(nc):

    def din(name, shape):
        return nc.dram_tensor(name, list(shape), F32, kind="ExternalInput").ap()

    def dout(name, shape):
        return nc.dram_tensor(name, list(shape), F32, kind="ExternalOutput").ap()

    xT_d = din("xT", [D, T])
    cols_d = din("cols", [128, NCOLS])
    wmod_d = din("wmod", [2, D, 9216])
    wffin_d = din("wffin", [4, 22, 128, 2048])
    wffout_d = din("wffout", [4, DFF, D])
    wina_d = din("wina", [D, 1856])
    wqu_d = din("wqu2", [384, 1536])
    wkvu_d = din("wkvu", [256, 1024])
    wouta_d = din("wouta", [D, D])
    winc_d = din("winc", [D, 2816])
    woutc_d = din("woutc", [D, D])
    rope_d = din("rope", [2, 2, 128, T])
    ckvctx_d = din("ckvctxT", [256, NCTX])
    krctx_d = din("krctxT", [32, NCTX])
    gkctx_d = din("gkctxT", [256, NCTX])
    gvctx_d = din("gvctx", [NCTX, 256])
    ident_d = din("ident", [128, 128])
    yT_d = dout("yT", [D, T])
    ckvT_d = dout("ckvT", [256, T])
    krT_d = dout("krT", [32, T])
    gkT_d = dout("gkT", [256, T])
    gv_d = dout("gv", [T, 256])

    ARENA = 212736
    arena = nc.alloc_sbuf_tensor("arena", [128, ARENA], mybir.dt.uint8).ap()

    def view(off, shape, dt):
        esz = 4 if dt == F32 else 2
        n = int(np.prod(shape)) * esz
        assert off % 4 == 0 and off + n <= ARENA, (off, n)
        v = arena[:, off:off + n].bitcast(dt)
        if len(shape) == 2:
            v = v.rearrange("p (a b) -> p a b", a=shape[0])
        elif len(shape) == 3:
            v = v.rearrange("p (a b c) -> p a b c", a=shape[0], b=shape[1])
        return v

    ps = nc.alloc_psum_tensor("ps", [128, 4096], F32).ap()

    def bank(b, n=1):
        return ps[:, b * 512:(b + n) * 512]

    class Sch:
        def __init__(s):
            s.engs = {"pe": nc.tensor, "act": nc.scalar, "dve": nc.vector, "pool": nc.gpsimd, "sp": nc.sync}
            s.sems, s.cnt = {}, {}
            s.waited = {e: {} for e in s.engs}
            s.lastw, s.rds = {}, {}
            s.pend = {e: [] for e in s.engs}
            s.stores = set()
            s.floor = None

        def sem(s, key):
            if key not in s.sems:
                s.sems[key] = nc.alloc_semaphore("s%d" % len(s.sems))
                s.cnt[key] = 0
            return s.sems[key]

        def _deps(s, e, rd, wr):
            deps = {}

            def add(ev):
                if ev is None:
                    return
                k, v = ev
                if v is None:
                    assert k == e == "pe", ("dependency on unsignalled op", k, e)
                    return
                if deps.get(k, 0) < v:
                    deps[k] = v

            add(s.floor)
            for r in rd:
                add(s.lastw.get(r))
            for w in wr:
                add(s.lastw.get(w))
                for ev in s.rds.get(w, {}).values():
                    add(ev)
            for k, v in deps.items():
                if k == e and e == "pe":
                    continue
                if s.waited[e].get(k, 0) >= v:
                    continue
                s.engs[e].wait_ge(s.sem(k), v)
                s.waited[e][k] = v

        def _record(s, ev, rd, wr):
            for r in rd:
                s.rds.setdefault(r, {})[ev[0]] = ev
            for w in wr:
                s.lastw[w] = ev
                s.rds[w] = {}

        @staticmethod
        def _exp(lst):
            out = []
            for r in lst:
                if r == "hT" or r == "xT":
                    out += [(r, t) for t in range(4)]
                else:
                    out.append(r)
            return out

        def op(s, e, fn, rd=(), wr=(), sig=True):
            rd, wr = s._exp(rd), s._exp(wr)
            psr = [r for r in rd if isinstance(r, tuple) and r[0] == "ps"]
            if psr:
                rd = [r for r in rd if not (isinstance(r, tuple) and r[0] == "ps")]
                wr = list(wr) + psr
            s._deps(e, rd, wr)
            inst = fn()
            ev = [e, None]
            s.pend[e].append(ev)
            s._record(ev, rd, wr)
            if sig:
                s.sem(e)
                s.cnt[e] += 1
                inst.then_inc(s.sems[e], 1)
                for p in s.pend[e]:
                    p[1] = s.cnt[e]
                s.pend[e] = []
            return inst

        def dma(s, q, out, in_, rd=(), wr=(), key=None, store=False):
            rd, wr = s._exp(rd), s._exp(wr)
            s._deps(q, rd, wr)
            sm = s.sem(key)
            s.cnt[key] += 16
            s.engs[q].dma_start(out=out, in_=in_).then_inc(sm, 16)
            ev = [key, s.cnt[key]]
            s._record(ev, rd, wr)
            if store:
                s.stores.add(key)

        def fence(s):
            res = sorted(set(s.lastw.keys()) | set(s.rds.keys()), key=str)
            s.op("dve", lambda: nc.vector.memset(fcell, 0.0), rd=(), wr=res)
            s.floor = s.lastw[res[0]]

        def finish(s):
            for key in sorted(s.stores, key=str):
                nc.sync.wait_ge(s.sems[key], s.cnt[key])

    S = Sch()

    class _Stop(Exception):
        pass

    def checkpoint(name):
        import os
        if os.environ.get("KSTOP", "") == name:
            for c in range(8):
                S.dma("sp", yT_d[c * 128:(c + 1) * 128, :], xT[:, c, :], rd=["xT"], key="st_dbg", store=True)
            S.finish()
            raise _Stop()

    def mm(out, lhsT, rhs, start, stop, rd, wr, sig=True, skip=False):
        if skip:
            return S.op("pe", lambda: nc.tensor.matmul(out, lhsT, rhs, start=start, stop=stop, skip_group_check=True),
                        rd, wr, sig)
        return S.op("pe", lambda: nc.tensor.matmul(out, lhsT, rhs, start=start, stop=stop), rd, wr, sig)

    def act(out, in_, func, rd, wr, bias=None, scale=None):
        kw = {}
        if bias is not None:
            kw["bias"] = bias
        if scale is not None:
            kw["scale"] = scale
        return S.op("act", lambda: nc.scalar.activation(out=out, in_=in_, func=func, **kw), rd, wr)

    def tt(out, a, b, op, rd, wr, e="dve"):
        return S.op(e, lambda: S.engs[e].tensor_tensor(out, a, b, op), rd, wr)

    def ts(out, a, s1, s2, op0, op1, rd, wr, e="dve"):
        return S.op(e, lambda: S.engs[e].tensor_scalar(out, a, s1, s2, op0, op1), rd, wr)

    def stt(out, a, sc, b, op0, op1, rd, wr):
        return S.op("dve", lambda: nc.vector.scalar_tensor_tensor(out, a, sc, b, op0, op1), rd, wr)

    def recip(out, a, rd, wr):
        return S.op("dve", lambda: nc.vector.reciprocal(out, a), rd, wr)

    def cpy(out, a, rd, wr, e="dve"):
        return S.op(e, lambda: S.engs[e].tensor_copy(out, a), rd, wr)

    XT = 0
    HT = 65536
    CONST = 98304
    SC = 102400
    R = 126976
    RSZ = ARENA - R
    xT = view(XT, [8, T], F32)
    hT = view(HT, [8, T], BF16)
    cols = view(CONST, [NCOLS], F32)
    o = CONST + 2304
    ones = view(o, [128], BF16); o += 256
    blk = view(o, [128], BF16); o += 256
    ones32 = view(o, [64], F32); o += 256
    modT = view(o, [2, 72], F32); o += 576
    der = view(o, [2, 48], F32); o += 384
    e_bf = view(o, [8], BF16); o += 32
    fcell = view(o, [1], F32); o += 32
    assert o <= SC
    sq = [view(SC + i * 4096, [4, 512], BF16) for i in range(2)]
    rt = [view(SC + 8192 + i * 2048, [512], F32) for i in range(2)]
    tmp = [view(SC + 12288 + i * 2048, [512], F32) for i in range(2)]
    sa = [view(SC + 16384 + i * 1024, [512], BF16) for i in range(2)]
    tmp2 = [view(SC + 18432 + i * 2048, [512], F32) for i in range(2)]
    ident = view(SC + 22528, [128], BF16)
    dg = [view(SC + 22784 + i * 256, [128], BF16) for i in range(4)]
    rrow = view(SC, [1024], F32)
    bcs = [view(SC + 4096 + i * 1024, [256], F32) for i in range(2)]

    def col(name, i=0, n=1):
        o_ = _COLS[name] + i
        return cols[:, o_:o_ + n]

    S.dma("sp", cols, cols_d, wr=["cols"], key="ld_cols")
    for t4 in range(4):
        S.dma("sp", xT[:, :, t4 * 512:(t4 + 1) * 512],
              xT_d[:, t4 * 512:(t4 + 1) * 512].rearrange("(c p) t -> p c t", p=128),
              wr=[("xT", t4)], key=("ld_x", t4))
    S.op("dve", lambda: nc.vector.memset(ones, 1.0), wr=["ones"])
    S.op("dve", lambda: nc.vector.memset(blk, 0.0), wr=["blk"])
    S.op("dve", lambda: nc.vector.memset(blk[0:64, 0:64], 1.0), wr=["blk"])
    S.op("dve", lambda: nc.vector.memset(blk[64:128, 64:128], 1.0), wr=["blk"])
    S.op("dve", lambda: nc.vector.memset(ones32, 1.0), wr=["ones32"])
    act(e_bf, col("cond", 0, 8), AF.Silu, ["cols"], ["e"])
    S.dma("pool", ident, ident_d, wr=["ident"], key=("ld", "ident"))

    WM = [view(R + 65536 + i * 8192, [8, 512], BF16) for i in range(2)]
    mod_jobs = [(l, p) for l in range(2) for p in range(18)]

    def mod_issue(k):
        l, p = mod_jobs[k]
        slot = k % 2
        S.dma("pool", WM[slot], wmod_d[l, :, p * 512:(p + 1) * 512].rearrange("(kc p) n -> p kc n", p=128),
              wr=[("wm", slot)], key=("ld_wm", slot))

    def mod_compute(k):
        l, p = mod_jobs[k]
        slot = k % 2
        for c in range(4):
            cc = l * 72 + p * 4 + c
            for kc in range(8):
                mm(bank(7)[:, cc:cc + 1], WM[slot][:, kc, c * 128:(c + 1) * 128], e_bf[:, kc:kc + 1],
                   kc == 0, kc == 7, [("wm", slot), "e"], [("ps", 7)], sig=(kc == 7))

    def mod_evac(k0, k1):
        for l in range(2):
            ps_ = [p for (ll, p) in mod_jobs[k0:k1] if ll == l]
            if not ps_:
                continue
            c0, c1 = min(ps_) * 4, (max(ps_) + 1) * 4
            tt(modT[:, l, c0:c1], bank(7)[:, l * 72 + c0:l * 72 + c1], col("bmod", l * 72 + c0, c1 - c0), ALU.add,
               [("ps", 7), "cols"], ["mod"])

    MODP = [("gff1_%d", 1, 2, 0.5), ("gmix_%d", 4, 5, 1.0), ("gff2_%d", 7, 8, 0.5)]

    def mod_der_A(l, i):
        gname, sci, gi, gs = MODP[i]
        A = der[:, l, i * 16:i * 16 + 8]
        ts(A, modT[:, l, sci * 8:sci * 8 + 8], 1.0, 1.0, ALU.mult, ALU.add, ["mod"], ["derA"])
        tt(A, A, col(gname % l, 0, 8), ALU.mult, ["derA", "cols"], ["derA"])

    def mod_der_G(l, i):
        gname, sci, gi, gs = MODP[i]
        G = der[:, l, i * 16 + 8:i * 16 + 16]
        ts(G, modT[:, l, gi * 8:gi * 8 + 8], gs, 0.0, ALU.mult, ALU.add, ["mod"], ["der"])

    def mod_der(l, i):
        mod_der_A(l, i)
        mod_der_G(l, i)

    mod_issue(0)
    for k in range(6):
        if k + 1 < 6:
            mod_issue(k + 1)
        mod_compute(k)
        if k == 3:
            mod_evac(0, 4)
            mod_der_A(0, 0)
    mod_evac(4, 6)
    mod_der_G(0, 0)

    checkpoint("mod")

    def rms_rstd(src_fn, nch, ncols, inv_n, rd, lhs, rti, sqi, psb):
        for c0 in range(0, nch, 4):
            n = min(4, nch - c0)
            sqt = sq[(sqi + c0 // 4) % 2]
            for c in range(n):
                act(sqt[:, c, 0:ncols], src_fn(c0 + c), AF.Square, rd, [("sq", (sqi + c0 // 4) % 2)])
            for c in range(n):
                mm(bank(psb)[:, 0:ncols], lhs, sqt[:, c, 0:ncols], (c0 + c) == 0, (c0 + c) == nch - 1,
                   [("sq", (sqi + c0 // 4) % 2), "ones", "blk"], [("ps", psb)], sig=((c0 + c) == nch - 1 or c == n - 1))
        act(rt[rti][:, 0:ncols], bank(psb)[:, 0:ncols], AF.Ln, [("ps", psb), "cols"], [("rt", rti)],
            bias=col("eps"), scale=inv_n)
        act(rt[rti][:, 0:ncols], rt[rti][:, 0:ncols], AF.Exp, [("rt", rti)], [("rt", rti)], scale=-0.5)

    def norm_t4(l, which, t4):
        A = der[:, l, which * 16:which * 16 + 8]
        shi = [0, 3, 6][which]
        tsl = slice(t4 * 512, (t4 + 1) * 512)
        rms_rstd(lambda c: xT[:, c, tsl], 8, 512, 1.0 / D, [("xT", t4)], ones, t4 % 2, 0, 6 + t4 % 2)
        for kc in range(8):
            k2 = kc % 2
            stt(tmp[k2], xT[:, kc, tsl], A[:, kc:kc + 1], rt[t4 % 2], ALU.mult, ALU.mult,
                [("xT", t4), "derA", ("rt", t4 % 2)], [("tmp", k2)])
            act(hT[:, kc, tsl], tmp[k2], AF.Identity, [("tmp", k2), "mod"], [("hT", t4)],
                bias=modT[:, l, shi * 8 + kc:shi * 8 + kc + 1], scale=1.0)

    def norm_mod(l, which):
        for t4 in range(4):
            norm_t4(l, which, t4)

    def final_t4(t4):
        tsl = slice(t4 * 512, (t4 + 1) * 512)
        rms_rstd(lambda c: xT[:, c, tsl], 8, 512, 1.0 / D, [("xT", t4)], ones, t4 % 2, 0, 6 + t4 % 2)
        for kc in range(8):
            k2 = kc % 2
            stt(tmp2[k2], xT[:, kc, tsl], col("gfinal", kc), rt[t4 % 2], ALU.mult, ALU.mult,
                [("xT", t4), "cols", ("rt", t4 % 2)], [("tmp2", k2)])
            S.dma("sp", yT_d[kc * 128:(kc + 1) * 128, tsl], tmp2[k2], rd=[("tmp2", k2)], key=("st_t2", k2), store=True)

    def ffn(l, f, side=None, ders=(), fence=True, do_norm=True, tail=None):
        which = 0 if f == 0 else 2
        if fence:
            S.fence()
        if do_norm:
            norm_mod(l, which)
        G = der[:, l, which * 16 + 8:which * 16 + 16]
        wi_idx = l * 2 + f
        NB = 4
        gT = view(R, [8, T], BF16)
        wi = [view(R + 32768 + i * 4096, [8, 256], BF16) for i in range(NB)]
        wo = view(R + 49152, [8, 1024], BF16)

        def load_wi(j):
            S.dma("pool", wi[j % NB], wffin_d[wi_idx, j].rearrange("p (kc n) -> p kc n", kc=8),
                  wr=[("wi", j % NB)], key=("ld_wi", j % NB))

        for j in range(NB):
            load_wi(j)
        step = 0
        for (g0, gn) in [(0, 8), (8, 7), (15, 7)]:
            last = (g0 == 15)
            S.dma("pool", wo[:, 0:gn, :], wffout_d[wi_idx, g0 * 128:(g0 + gn) * 128, :].rearrange("(j p) d -> p j d", p=128),
                  wr=["wo"], key="ld_wo")
            def side_hook(j):
                if side is not None:
                    k0, k1 = side
                    if k0 + j < k1:
                        mod_issue(k0 + j)
                    if j >= 1 and k0 + j - 1 < k1:
                        mod_compute(k0 + j - 1)

            def body(j, jj, t4):
                nonlocal step
                w = wi[j % NB]
                tsl = slice(t4 * 512, (t4 + 1) * 512)
                k = step % 2
                step += 1
                for half in range(2):
                    b = half * 2 + k
                    for kc in range(8):
                        mm(bank(b), w[:, kc, half * 128:(half + 1) * 128], hT[:, kc, tsl], kc == 0, kc == 7,
                           [("wi", j % NB), ("hT", t4)], [("ps", b)], sig=(kc == 7))
                act(sa[k], bank(k), AF.Silu, [("ps", k)], [("sa", k)])
                tt(gT[:, jj, tsl], sa[k], bank(2 + k), ALU.mult, [("sa", k), ("ps", 2 + k)], [("gT", jj)])

            jstart = 0
            if g0 == 0 and do_norm:
                for j in range(3):
                    side_hook(j)
                for t4 in range(4):
                    for j in range(3):
                        body(j, j, t4)
                for j in range(3):
                    load_wi(j + NB)
                jstart = 3
            for jj in range(jstart, gn):
                j = g0 + jj
                side_hook(j)
                for t4 in range(4):
                    body(j, jj, t4)
                if j + NB < 22:
                    load_wi(j + NB)
            if last and side is not None:
                mod_evac(*side)
                for (ll, ii) in ders:
                    mod_der(ll, ii)
            for t4 in range(4):
                for dc in range(8):
                    if last and tail is not None and t4 >= 1 and dc == 4:
                        tail(t4 - 1)
                    tsl = slice(t4 * 512, (t4 + 1) * 512)
                    b = 4 + (dc + t4) % 2
                    for jj in range(gn):
                        mm(bank(b), wo[:, jj, dc * 128:(dc + 1) * 128], gT[:, jj, tsl], jj == 0, jj == gn - 1,
                           ["wo", ("gT", jj)], [("ps", b)], sig=(jj == gn - 1))
                    stt(xT[:, dc, tsl], bank(b), G[:, dc:dc + 1], xT[:, dc, tsl], ALU.mult, ALU.add,
                        [("ps", b), "der", ("xT", t4)], [("xT", t4)])
            if last and tail is not None:
                tail(3)

    def load_w(dst, src, res, key=None):
        S.dma("pool", dst, src, wr=[res], key=("ld", res))

    osb = [view(SC + i * 4096, [1024], F32) for i in range(2)]
    rrow2 = view(SC + 8192, [1024], F32)

    def attention(q_fn, kT_fn, v_fn, scale, pT, attn_write, outproj_dc, qprep, shared, dummy=None, nbanks=(6, 7),
                  work=None, flush=True):
        steps = [(qb, kt) for qb in range(NQB) for kt in range(KT)]
        if work is None:
            work = []

        def emit_S(si):
            qb, kt = steps[si]
            sb = si % 2
            if shared:
                for ip in range(2):
                    lhsT, krd = kT_fn(0, kt)
                    rhs, qrd = q_fn(ip, qb)
                    mm(bank(sb * 2 + ip).rearrange("p (a b) -> p a b", a=2), lhsT, rhs, True, True, krd + qrd,
                       [("ps", sb * 2 + ip)], sig=True)
            else:
                for i in range(4):
                    lhsT, krd = kT_fn(i, kt)
                    rhs, qrd = q_fn(i, qb)
                    mm(bank(sb * 2, 2)[:, i * 256:(i + 1) * 256], lhsT, rhs, True, True, krd + qrd,
                       [("ps", sb * 2 + i // 2)], sig=(i % 2 == 1))

        def recip_item(qb, i):
            def f():
                slot = qb % 2
                csl = slice(i * 256, (i + 1) * 256)
                recip(rrow2[64:65, csl], osb[slot][64:65, csl], [("osb", slot)], [("rrow", i)])
            return f

        def norm_item(qb, i):
            def f():
                slot = qb % 2
                csl = slice(i * 256, (i + 1) * 256)
                nb = nbanks[i % len(nbanks)]
                mm(bank(nb)[0:64, 0:256], ones32[64:65, 0:64], rrow2[64:65, csl], True, True,
                   [("rrow", i), "ones32"], [("ps", nb)])
                attn_write(qb, i, osb[slot][0:64, csl], bank(nb)[0:64, 0:256], [("osb", slot), ("ps", nb)])
            return f

        for it in qprep(0):
            it()
        emit_S(0)
        qitems = []
        for si, (qb, kt) in enumerate(steps):
            if kt == 15 and qb + 1 < NQB:
                qitems = list(qprep(qb + 1))
            if kt >= 16 and qitems:
                qitems.pop(0)()
            if si + 1 < len(steps):
                emit_S(si + 1)
            sb = si % 2
            pslot = si % 3
            act(pT[pslot], bank(sb * 2, 2), AF.Exp, [("ps", sb * 2), ("ps", sb * 2 + 1), "cols"], [("pT", pslot)],
                bias=col("abias", qb * KT + kt), scale=scale)
            if shared:
                for ip in range(2):
                    vv, vrd = v_fn(0, kt)
                    mm(bank(4 + ip)[:, :], vv, pT[pslot][:, ip * 512:(ip + 1) * 512], kt == 0, kt == KT - 1,
                       vrd + [("pT", pslot)], [("ps", 4 + ip)], sig=(ip == 1), skip=True)
            else:
                for i in range(4):
                    vv, vrd = v_fn(i, kt)
                    ob = 4 + i // 2
                    mm(bank(ob)[:, (i % 2) * 256:(i % 2 + 1) * 256], vv, pT[pslot][:, i * 256:(i + 1) * 256],
                       (kt == 0 and i % 2 == 0), kt == KT - 1, vrd + [("pT", pslot)], [("ps", ob)],
                       sig=(i == 3), skip=True)
            if dummy is not None:
                dummy()
            if work:
                work.pop(0)()
            if kt == KT - 1:
                slot = qb % 2
                cpy(osb[slot][0:65, 0:512], bank(4)[0:65, :], [("ps", 4)], [("osb", slot)])
                cpy(osb[slot][0:65, 512:1024], bank(5)[0:65, :], [("ps", 5)], [("osb", slot)])
                for i in range(4):
                    work.append(recip_item(qb, i))
                for i in range(4):
                    work.append(norm_item(qb, i))
                for dc in range(8):
                    work.append((lambda qb=qb, dc=dc: outproj_dc(qb, dc)))
        while flush and work:
            work.pop(0)()

    ffn(0, 0, side=(6, 24), ders=[(0, 1), (0, 2), (1, 0)], tail=lambda t4: norm_t4(0, 1, t4), fence=False)
    checkpoint("ffn00")
    checkpoint("n01")
    Gm0 = der[:, 0, 24:32]
    cqn = view(R, [3, T], BF16)
    ckvn = view(R + 12288, [2, NK], BF16)
    krall = view(R + 22528, [NK], BF16)
    cosT = view(R + 27648, [T], F32)
    sinT = view(R + 35840, [T], F32)
    GLU = R + 44032
    glu = view(GLU, [4, 8, 286], BF16)
    gl = view(GLU, [4, T], BF16)
    WS = R + 62336
    wpc = [view(WS + i * 6144, [8, 384], BF16) for i in range(3)]
    yc = view(HT, [4, T], F32)

    import os
    SK = os.environ.get("KSKIP", "")
    S.fence()
    if "a" not in SK:
        S.dma("sp", cosT, rope_d[0, 0], wr=["cos"], key="ld_cos")
        S.dma("sp", sinT, rope_d[0, 1], wr=["sin"], key="ld_sin")
    if "b" not in SK:
        for c in range(2):
            load_w(ckvn[:, c, 0:NCTX], ckvctx_d[c * 128:(c + 1) * 128, :], "ckvn_ctx", "ld_ctx")
    if "e" not in SK:
        load_w(krall[64:96, 0:NCTX], krctx_d, "kr_ctx", "ld_ctx")
    if "c" not in SK:
        S.op("pool", lambda: nc.gpsimd.memset(glu[:, :, 0, 0:15], 0.0), wr=["glu_pad"])
        S.op("pool", lambda: nc.gpsimd.memset(glu[:, :, 7, 271:286], 0.0), wr=["glu_pad"])

    pieces = [(0, 384), (384, 256), (640, 192)] + [(832 + 256 * i, 256) for i in range(4)]

    def load_piece(pi):
        c0, w_ = pieces[pi]
        load_w(wpc[pi % 3][:, :, 0:w_], wina_d[:, c0:c0 + w_].rearrange("(kc p) n -> p kc n", p=128),
               ("wpc", pi % 3), ("ld_wpc", pi % 3))

    if "d" not in SK:
        for pi in range(3):
            load_piece(pi)

    checkpoint("mla_ld")

    def proj(w, col0, m, t4, b, wres):
        tsl = slice(t4 * 512, (t4 + 1) * 512)
        for kc in range(8):
            mm(bank(b)[0:m, :], w[:, kc, col0:col0 + m], hT[:, kc, tsl], kc == 0, kc == 7, [wres, ("hT", t4)],
               [("ps", b)], sig=(kc == 7))

    for pi, (nch, gname, inv_n) in enumerate([(3, "gql", 1.0 / 384), (2, "gkvl", 1.0 / 256)]):
        w = wpc[pi % 3]
        for t4 in range(4):
            tsl = slice(t4 * 512, (t4 + 1) * 512)
            for c in range(nch):
                proj(w, c * 128, 128, t4, c, ("wpc", pi % 3))
            rms_rstd(lambda c: bank(c), nch, 512, inv_n, [("ps", c_) for c_ in range(nch)], ones, t4 % 2, 0, 4 + t4 % 2)
            for c in range(nch):
                if pi == 0:
                    stt(tmp[c % 2], bank(c), col(gname, c), rt[t4 % 2], ALU.mult, ALU.mult,
                        [("ps", c), "cols", ("rt", t4 % 2)], [("tmp", c % 2)])
                    act(cqn[:, c, tsl], tmp[c % 2], AF.Copy, [("tmp", c % 2)], ["cqn"])
                else:
                    k2 = (t4 * 2 + c) % 2
                    stt(tmp2[k2], bank(c), col(gname, c), rt[t4 % 2], ALU.mult, ALU.mult,
                        [("ps", c), "cols", ("rt", t4 % 2)], [("tmp2", k2)])
                    act(ckvn[:, c, NCTX + t4 * 512:NCTX + (t4 + 1) * 512], tmp2[k2], AF.Copy, [("tmp2", k2)], ["ckvn"])
                    S.dma("sp", ckvT_d[c * 128:(c + 1) * 128, tsl], tmp2[k2], rd=[("tmp2", k2)], key=("st_t2", k2),
                          store=True)
        load_piece(pi + 3)
    checkpoint("mla_p1")
    w = wpc[2]
    for t4 in range(4):
        tsl = slice(t4 * 512, (t4 + 1) * 512)
        proj(w, 0, 96, t4, 0, ("wpc", 2))
        proj(w, 96, 96, t4, 1, ("wpc", 2))
        k2 = t4 % 2
        tt(tmp[0][64:96, :], bank(0)[64:96, :], cosT[64:96, tsl], ALU.mult, [("ps", 0), "cos"], [("tmp", 0)])
        tt(tmp[1][64:96, :], bank(1)[64:96, :], sinT[64:96, tsl], ALU.mult, [("ps", 1), "sin"], [("tmp", 1)])
        tt(tmp2[k2][64:96, :], tmp[0][64:96, :], tmp[1][64:96, :], ALU.add, [("tmp", 0), ("tmp", 1)], [("tmp2", k2)])
        act(krall[64:96, NCTX + t4 * 512:NCTX + (t4 + 1) * 512], tmp2[k2][64:96, :], AF.Copy, [("tmp2", k2)], ["krall"])
        S.dma("sp", krT_d[:, tsl], tmp2[k2][64:96, :], rd=[("tmp2", k2)], key=("st_t2", k2), store=True)
    load_piece(5)
    checkpoint("mla_p2")
    for c in range(4):
        pi = 3 + c
        w = wpc[pi % 3]
        for t4 in range(4):
            k = t4 % 2
            proj(w, 0, 128, t4, k, ("wpc", pi % 3))
            proj(w, 128, 128, t4, 2 + k, ("wpc", pi % 3))
            act(tmp[k], bank(2 + k), AF.Sigmoid, [("ps", 2 + k)], [("tmp", k)])
            tt(glu[:, c, 2 * t4:2 * t4 + 2, 15:271], bank(k).rearrange("p (a b) -> p a b", a=2),
               tmp[k].rearrange("p (a b) -> p a b", a=2), ALU.mult, [("ps", k), ("tmp", k)], [("glu", c)])
        if pi + 3 < 7:
            load_piece(pi + 3)
        ts(glu[:, c, 1:8, 0:15], glu[:, c, 0:7, 256:271], col("hmask"), 0.0, ALU.mult, ALU.add,
           [("glu", c), "cols"], [("glu", c)])
        ts(glu[:, c, 0:7, 271:286], glu[:, c, 1:8, 15:30], col("hmask"), 0.0, ALU.mult, ALU.add,
           [("glu", c), "cols"], [("glu", c)])
    checkpoint("mla_proj")
    wqu = view(WS, [3, 1536], BF16)
    wkvu = view(WS + 9216, [2, 1024], BF16)
    woa = view(WS + 13312, [4, 1024], BF16)
    S.fence()
    load_w(woa, wouta_d[512:1024, :].rearrange("(kc p) n -> p kc n", p=128), "woa", "ld_woa")
    for c in range(4):
        for j in range(31):
            ds_ = (c * 31 + j) % 4
            ts(dg[ds_], ident, col("wdw", c * 31 + j), 0.0, ALU.mult, ALU.add, ["ident", "cols"], [("dg", ds_)])
            for t4 in range(4):
                mm(bank(t4).rearrange("p (a b) -> p a b", a=2), dg[ds_], glu[:, c, 2 * t4:2 * t4 + 2, j:j + 256],
                   j == 0, j == 30, [("dg", ds_), ("glu", c), "glu_pad"], [("ps", t4)], sig=(t4 == 3))
        for t4 in range(4):
            act(yc[:, c, t4 * 512:(t4 + 1) * 512], bank(t4), AF.Identity, [("ps", t4), "cols", "hT"], ["hT"],
                bias=col("bdw", c), scale=1.0)
    for t4 in range(4):
        tsl = slice(t4 * 512, (t4 + 1) * 512)
        for c in range(4):
            act(sq[0][:, c, :], yc[:, c, tsl], AF.Copy, ["hT"], [("sq", 0)])
            act(sq[1][:, c, :], yc[:, c, tsl], AF.Square, ["hT"], [("sq", 1)])
        for c in range(4):
            mm(bank(6), ones, sq[0][:, c, :], c == 0, c == 3, [("sq", 0), "ones"], [("ps", 6)], sig=(c == 3))
        for c in range(4):
            mm(bank(7), ones, sq[1][:, c, :], c == 0, c == 3, [("sq", 1), "ones"], [("ps", 7)], sig=(c == 3))
        mean, m2 = tmp2[0], tmp2[1]
        ts(mean, bank(6), 1.0 / 512, 0.0, ALU.mult, ALU.add, [("ps", 6)], [("tmp2", 0)])
        tt(m2, mean, mean, ALU.mult, [("tmp2", 0)], [("tmp2", 1)])
        stt(m2, bank(7), 1.0 / 512, m2, ALU.mult, ALU.subtract, [("ps", 7), ("tmp2", 1)], [("tmp2", 1)])
        act(rt[0], m2, AF.Ln, [("tmp2", 1), "cols"], [("rt", 0)], bias=col("eps"), scale=1.0)
        act(rt[0], rt[0], AF.Exp, [("rt", 0)], [("rt", 0)], scale=-0.5)
        for c in range(4):
            k = c % 2
            tt(tmp[k], yc[:, c, tsl], mean, ALU.subtract, ["hT", ("tmp2", 0)], [("tmp", k)])
            tt(tmp[k], tmp[k], rt[0], ALU.mult, [("tmp", k), ("rt", 0)], [("tmp", k)])
            act(gl[:, c, tsl], tmp[k], AF.Silu, [("tmp", k), "cols"] + [("glu", cc) for cc in range(4)],
                [("gl", c, t4)], bias=col("bln", c), scale=col("gln", c))
    for t4 in range(4):
        for dc in range(8):
            tsl = slice(t4 * 512, (t4 + 1) * 512)
            b = 6 + (dc + t4) % 2
            for c in range(4):
                mm(bank(b), woa[:, c, dc * 128:(dc + 1) * 128], gl[:, c, tsl], c == 0, c == 3,
                   ["woa", ("gl", c, t4)], [("ps", b)], sig=(c == 3))
            stt(xT[:, dc, tsl], bank(b), Gm0[:, dc:dc + 1], xT[:, dc, tsl], ALU.mult, ALU.add,
                [("ps", b), "der", ("xT", t4)], [("xT", t4)])
    checkpoint("mla_conv")
    S.fence()
    load_w(wqu, wqu_d.rearrange("(kc p) n -> p kc n", p=128), "wqu", "ld_wqu")
    load_w(wkvu, wkvu_d.rearrange("(kc p) n -> p kc n", p=128), "wkvu", "ld_wkvu")
    Vt = view(HT, [KT, 4, 65], BF16)
    VtF = view(HT, [KT * 4 * 65 + 63], BF16)
    KTt = view(HT + 10496, [4, NK], BF16)
    ATT = GLU
    Qt = [view(ATT + i * 2048, [4, 256], BF16) for i in range(2)]
    pT = [view(ATT + 4096 + i * 2048, [1024], BF16) for i in range(3)]
    attnb = [view(ATT + 10240 + i * 1024, [2, 256], BF16) for i in range(2)]
    gl_res = [("gl", c, t_) for c in range(4) for t_ in range(4)]
    for i_ in range(2):
        S.op("pool", lambda i_=i_: nc.gpsimd.memset(Qt[i_][64:128, :, :], 0.0), wr=[("Qt", i_)])
    for i_ in range(4):
        S.op("pool", lambda i_=i_: nc.gpsimd.memset(KTt[64:128, i_, :], 0.0), wr=[("KT", i_)])
    mla_work = []
    for g in range(2):
        load_w(woa[:, 2 * g:2 * g + 2, :], wouta_d[g * 256:(g + 1) * 256, :].rearrange("(kc p) n -> p kc n", p=128),
               "woa", "ld_woa")
        S.op("pool", lambda: nc.gpsimd.memset(Vt[:, :, :, 64:65], 1.0), rd=["hT"], wr=["Vt1"])
        for kt in range(KT):
            b = 6 + kt % 2
            for c in range(2):
                rhs = wkvu[:, c, g * 512:(g + 1) * 512].rearrange("p (h d) -> p h d", h=4)[:, :, 64:128]
                mm(bank(b)[:, 0:256].rearrange("p (h d) -> p h d", h=4), ckvn[:, c, kt * 128:(kt + 1) * 128], rhs,
                   c == 0, c == 1, ["ckvn", "ckvn_ctx", "wkvu"], [("ps", b)], sig=(c == 1))
            if kt % 2 == 0:
                cpy(Vt[:, kt, :, 0:64], bank(b)[:, 0:256].rearrange("p (h d) -> p h d", h=4), [("ps", b), "hT"], ["Vt"])
            else:
                act(Vt[:, kt, :, 0:64], bank(b)[:, 0:256].rearrange("p (h d) -> p h d", h=4), AF.Copy,
                    [("ps", b), "hT"], ["Vt"])
        for i in range(4):
            h = 4 * g + i
            for k5 in range(5):
                b = 6 + (i * 5 + k5) % 2
                for c in range(2):
                    mm(bank(b)[0:64, :], wkvu[:, c, h * 128:h * 128 + 64], ckvn[:, c, k5 * 512:(k5 + 1) * 512],
                       c == 0, c == 1, ["ckvn", "ckvn_ctx", "wkvu"], [("ps", b)], sig=(c == 1))
                if k5 % 2 == 0:
                    cpy(KTt[0:64, i, k5 * 512:(k5 + 1) * 512], bank(b)[0:64, :], [("ps", b), "hT"], [("KT", i)])
                else:
                    act(KTt[0:64, i, k5 * 512:(k5 + 1) * 512], bank(b)[0:64, :], AF.Copy, [("ps", b), "hT"], [("KT", i)])
            S.dma("sp", KTt[64:96, i, :], krall[64:96, :], rd=["krall", "kr_ctx"], wr=[("KT", i)],
                  key=("ld", ("KTkr", i)))

        def qprep(qb, g=g):
            return [(lambda i2=i2: qprep_i2(qb, i2)) for i2 in range(2)]

        def qprep_i2(qb, i2, g=g):
            qs = slice(qb * 256, (qb + 1) * 256)
            qt = Qt[qb % 2]
            if True:
                for ii in range(2):
                    i = i2 * 2 + ii
                    h = 4 * g + i
                    for ab in range(2):
                        for c in range(3):
                            mm(bank(6 + ab)[0:96, ii * 256:(ii + 1) * 256],
                               wqu[:, c, ab * 768 + h * 96:ab * 768 + (h + 1) * 96], cqn[:, c, qs], c == 0, c == 2,
                               ["wqu", "cqn"], [("ps", 6 + ab)], sig=(c == 2))
                for ii in range(2):
                    i = i2 * 2 + ii
                    A = bank(6)[:, ii * 256:(ii + 1) * 256]
                    B = bank(7)[:, ii * 256:(ii + 1) * 256]
                    cpy(qt[0:64, i, :], A[0:64, :], [("ps", 6)] + gl_res, [("Qt", qb % 2)])
                    tt(tmp[0][64:96, 0:256], A[64:96, :], cosT[64:96, qs], ALU.mult, [("ps", 6), "cos"], [("tmp", 0)])
                    tt(tmp[1][64:96, 0:256], B[64:96, :], sinT[64:96, qs], ALU.mult, [("ps", 7), "sin"], [("tmp", 1)])
                    tt(qt[64:96, i, :], tmp[0][64:96, 0:256], tmp[1][64:96, 0:256], ALU.add,
                       [("tmp", 0), ("tmp", 1)] + gl_res, [("Qt", qb % 2)])

        def q_fn(i, qb):
            return Qt[qb % 2][:, i, :], [("Qt", qb % 2)]

        def kT_fn(i, kt):
            return KTt[:, i, kt * 128:(kt + 1) * 128], [("KT", i)]

        def v_fn(i, kt):
            o_ = (kt * 4 + i) * 65
            return VtF[:, o_:o_ + 128], ["Vt", "Vt1"]

        def attn_write(qb, i, o_sb, b_ps, rd):
            ab = attnb[qb % 2]
            tt(ab[(i % 2) * 64:(i % 2) * 64 + 64, i // 2, :], o_sb, b_ps, ALU.mult, rd, [("attn", qb % 2)])

        def outproj_dc(qb, dc, g=g):
            qs = slice(qb * 256, (qb + 1) * 256)
            ab = attnb[qb % 2]
            b = 6 + dc % 2
            for pr in range(2):
                mm(bank(b)[:, 0:256], woa[:, 2 * g + pr, dc * 128:(dc + 1) * 128], ab[:, pr, :], pr == 0, pr == 1,
                   ["woa", ("attn", qb % 2)], [("ps", b)], sig=(pr == 1))
            stt(xT[:, dc, qs], bank(b)[:, 0:256], Gm0[:, dc:dc + 1], xT[:, dc, qs], ALU.mult, ALU.add,
                [("ps", b), "der", ("xT", qb // 2)], [("xT", qb // 2)])

        attention(q_fn, kT_fn, v_fn, float(96 ** -0.5), pT, attn_write, outproj_dc, qprep, False,
                  work=mla_work, flush=(g == 1))
    checkpoint("mla_attn")
    ffn(0, 1, side=(24, 36), ders=[(1, 1), (1, 2)], tail=lambda t4: norm_t4(1, 0, t4))
    checkpoint("ffn01")

    ffn(1, 0, fence=False, do_norm=False, tail=lambda t4: norm_t4(1, 1, t4))
    checkpoint("ffn10")
    Gm1 = der[:, 1, 24:32]
    QTa = view(R, [8, T], BF16)
    KTa = view(R + 32768, [2, NK], BF16)
    Vg = view(R + 43008, [KT, 4, 65], BF16)
    VgF = view(R + 43008, [KT * 4 * 65 + 63], BF16)
    cos1 = view(R + 53440, [T], F32)
    sin1 = view(R + 61632, [T], F32)
    wsl = [view(R + 69824 + i * 4096, [8, 256], BF16) for i in range(3)]
    S.fence()
    S.dma("sp", cos1, rope_d[1, 0], wr=["cos"], key="ld_cos")
    S.dma("sp", sin1, rope_d[1, 1], wr=["sin"], key="ld_sin")
    for m in range(2):
        load_w(KTa[:, m, 0:NCTX], gkctx_d[m * 128:(m + 1) * 128, :], "KTa_ctx", "ld_ctx")
    vstage = view(SC + 12288, [4, 256], BF16)
    S.dma("pool", vstage, gvctx_d.rearrange("(kt p) n -> p kt n", p=128), wr=[("tmp", 0)], key=("ld", "vstage"))
    for kt_ in range(4):
        cpy(Vg[:, kt_, :, 0:64], vstage[:, kt_, :].rearrange("p (h d) -> p h d", d=64), [("tmp", 0)], ["Vg_ctx"])
    S.op("pool", lambda: nc.gpsimd.memset(Vg[:, :, :, 64:65], 1.0), wr=["Vg1"])

    def load_pc(pi):
        load_w(wsl[pi % 3], winc_d[:, pi * 256:(pi + 1) * 256].rearrange("(kc p) n -> p kc n", p=128),
               ("wsl", pi % 3), ("ld_wsl", pi % 3))

    for pi in range(3):
        load_pc(pi)
    checkpoint("g_ld")
    gtasks = [(pi, t4) for pi in range(10) for t4 in range(4)]

    def g_chains(idx):
        pi, t4 = gtasks[idx]
        k = idx % 2
        proj(wsl[pi % 3], 0, 128, t4, k, ("wsl", pi % 3))
        proj(wsl[pi % 3], 128, 128, t4, 2 + k, ("wsl", pi % 3))
        if t4 == 3 and pi + 3 < 11:
            load_pc(pi + 3)

    def g_post(idx):
        pi, t4 = gtasks[idx]
        k = idx % 2
        isq = pi < 8
        gname, gpname = ("gq", "gqp") if isq else ("gk", "gkp")
        tsl = slice(t4 * 512, (t4 + 1) * 512)
        rms_rstd(lambda c: bank(k), 1, 512, 1.0 / 64, [("ps", k)], blk, k, k, 4 + k)
        stt(tmp[0], bank(k), col(gname), cos1[:, tsl], ALU.mult, ALU.mult, [("ps", k), "cols", "cos"], [("tmp", 0)])
        stt(tmp[1], bank(2 + k), col(gpname), sin1[:, tsl], ALU.mult, ALU.mult, [("ps", 2 + k), "cols", "sin"],
            [("tmp", 1)])
        tt(tmp[0], tmp[0], tmp[1], ALU.add, [("tmp", 0), ("tmp", 1)], [("tmp", 0)])
        if isq:
            tt(QTa[:, pi, tsl], tmp[0], rt[k], ALU.mult, [("tmp", 0), ("rt", k)], ["QTa"])
        else:
            m = pi - 8
            tt(tmp2[k], tmp[0], rt[k], ALU.mult, [("tmp", 0), ("rt", k)], [("tmp2", k)])
            act(KTa[:, m, NCTX + t4 * 512:NCTX + (t4 + 1) * 512], tmp2[k], AF.Copy, [("tmp2", k)], ["KTa"])
            S.dma("sp", gkT_d[m * 128:(m + 1) * 128, tsl], tmp2[k], rd=[("tmp2", k)], key=("st_t2", k), store=True)

    for idx in range(len(gtasks) + 1):
        if idx < len(gtasks):
            g_chains(idx)
        if idx >= 1:
            g_post(idx - 1)
    w = wsl[10 % 3]
    for t16 in range(16):
        b = t16 % 2
        for kc in range(8):
            mm(bank(b)[:, 0:256], hT[:, kc, t16 * 128:(t16 + 1) * 128], w[:, kc, :], kc == 0, kc == 7,
               [("wsl", 10 % 3), ("hT", t16 // 4)], [("ps", b)], sig=(kc == 7))
        if "x" not in SK:
            act(tmp2[b][:, 0:256], bank(b)[:, 0:256], AF.Copy, [("ps", b)], [("tmp2", b)])
        if "w" not in SK:
            S.dma("sp", gv_d[t16 * 128:(t16 + 1) * 128, :], tmp2[b][:, 0:256], rd=[("tmp2", b)], key=("st_t2", b), store=True)
        if "y" not in SK:
            cpy(Vg[:, 4 + t16, :, 0:64], bank(b)[:, 0:256].rearrange("p (h d) -> p h d", h=4), [("ps", b)], ["Vg"])
    checkpoint("gqa_proj")
    S.fence()
    woc = view(HT, [8, 1024], BF16)
    load_w(woc, woutc_d.rearrange("(kc p) n -> p kc n", p=128), "hT", "ld_woc")
    ATT1 = R + 53440
    pT1 = [view(ATT1 + i * 2048, [1024], BF16) for i in range(3)]
    attn1 = [view(ATT1 + 6144 + i * 2048, [4, 256], BF16) for i in range(2)]
    gqa_work = []
    for g in range(4):
        m, r = g // 2, g % 2

        Qz = [view(ATT1 + 10240 + i * 2048, [4, 256], BF16) for i in range(2)]
        for i_ in range(2):
            S.op("pool", lambda i_=i_, r=r: nc.gpsimd.memset(Qz[i_][(1 - r) * 64:(1 - r) * 64 + 64, :, :], 0.0),
                 wr=[("Qz", i_)])

        def qprep(qb, m=m, r=r):
            return [lambda: qprep1(qb)]

        def qprep1(qb, m=m, r=r):
            cpy(Qz[qb % 2][r * 64:r * 64 + 64, :, :], QTa[r * 64:r * 64 + 64, 4 * m:4 * m + 4, qb * 256:(qb + 1) * 256],
                ["QTa"], [("Qz", qb % 2)], e="pool")

        def q_fn(ip, qb):
            return Qz[qb % 2][:, 2 * ip:2 * ip + 2, :], [("Qz", qb % 2)]

        def kT_fn(i, kt, m=m):
            return KTa[:, m, kt * 128:(kt + 1) * 128], ["KTa", "KTa_ctx"]

        def v_fn(i, kt, g=g):
            o_ = (kt * 4 + g) * 65
            return VgF[:, o_:o_ + 128], ["Vg", "Vg1", "Vg_ctx"]

        def attn_write(qb, i, o_sb, b_ps, rd, r=r):
            ab = attn1[qb % 2]
            tt(ab[r * 64:r * 64 + 64, i, :], o_sb, b_ps, ALU.mult, rd, [("attn", qb % 2)])

        def outproj_dc(qb, dc, m=m, r=r):
            qs = slice(qb * 256, (qb + 1) * 256)
            ab = attn1[qb % 2]
            b = 6 + dc % 2
            for i in range(4):
                mm(bank(b)[:, 0:256], woc[r * 64:r * 64 + 64, 4 * m + i, dc * 128:(dc + 1) * 128],
                   ab[r * 64:r * 64 + 64, i, :], i == 0, i == 3, ["hT", ("attn", qb % 2)], [("ps", b)], sig=(i == 3))
            stt(xT[:, dc, qs], bank(b)[:, 0:256], Gm1[:, dc:dc + 1], xT[:, dc, qs], ALU.mult, ALU.add,
                [("ps", b), "der", ("xT", qb // 2)], [("xT", qb // 2)])

        NDUM = int(os.environ.get("KDUM", "0"))

        def dummy(m=m):
            for _ in range(NDUM):
                mm(bank(6)[:, 0:256], woc[:, 0, 0:128], QTa[:, 0, 0:256], True, True, ["hT", "QTa"], [("ps", 6)],
                   sig=False)

        attention(q_fn, kT_fn, v_fn, float(64 ** -0.5), pT1, attn_write, outproj_dc, qprep, True,
                  dummy=dummy if NDUM else None, nbanks=(6, 7) if not NDUM else (7,),
                  work=gqa_work, flush=(g == 3))
    checkpoint("gqa_attn")
    ffn(1, 1, tail=final_t4)
    checkpoint("ffn11")

    S.finish()


def _prep_shared(inp):
    f = np.float32
    sh = {}
    sh["wmod"] = np.ascontiguousarray(inp["w_mod"], dtype=f)
    sh["ident"] = np.eye(128, dtype=f)
    wffin = np.empty((4, 22, 128, 2048), f)
    wffout = np.empty((4, DFF, D), f)
    for l in range(2):
        for fi, (ni, no) in enumerate([("w_ff1_in", "w_ff1_out"), ("w_ff2_in", "w_ff2_out")]):
            w = np.asarray(inp[ni][l], f)
            a = w[:, :DFF].reshape(8, 128, 22, 128)
            b = w[:, DFF:].reshape(8, 128, 22, 128)
            ab = np.stack([a, b], axis=3)
            wffin[l * 2 + fi] = ab.transpose(2, 1, 0, 3, 4).reshape(22, 128, 2048)
            wffout[l * 2 + fi] = np.asarray(inp[no][l], f)
    sh["wffin"], sh["wffout"] = wffin, wffout
    wa = np.asarray(inp["w_in_a"][0], f)
    p32, _, _, _, _ = _rope_meta(32)
    krA = np.zeros((D, 96), f); krB = np.zeros((D, 96), f)
    krA[:, 64:] = wa[:, 640:672]
    krB[:, 64:] = wa[:, 640:672][:, p32]
    u = wa[:, 672:]
    ab = [np.concatenate([u[:, c * 128:(c + 1) * 128], u[:, 512 + c * 128:512 + (c + 1) * 128]], 1) for c in range(4)]
    sh["wina"] = np.ascontiguousarray(np.concatenate([wa[:, 0:640], krA, krB] + ab, 1))
    wq = np.asarray(inp["w_q_up"][0], f)
    wqB = np.zeros_like(wq)
    for h in range(8):
        wqB[:, h * 96 + 64:(h + 1) * 96] = wq[:, h * 96 + 64:(h + 1) * 96][:, p32]
    sh["wqu2"] = np.ascontiguousarray(np.concatenate([wq, wqB], 1))
    sh["wkvu"] = np.ascontiguousarray(inp["w_kv_up"][0], dtype=f)
    sh["wouta"] = np.ascontiguousarray(inp["w_out_a"][0], dtype=f)
    wc = np.asarray(inp["w_in_c"][0], f)
    p64, _, _, _, _ = _rope_meta(64)
    pcs = []
    for m in range(2):
        for j in range(4):
            hs = [8 * m + j, 8 * m + 4 + j]
            A = np.concatenate([wc[:, h * 64:(h + 1) * 64] for h in hs], 1)
            B = np.concatenate([wc[:, h * 64:(h + 1) * 64][:, p64] for h in hs], 1)
            pcs += [A, B]
    for m in range(2):
        hs = [2 * m, 2 * m + 1]
        A = np.concatenate([wc[:, 1024 + h * 64:1024 + (h + 1) * 64] for h in hs], 1)
        B = np.concatenate([wc[:, 1024 + h * 64:1024 + (h + 1) * 64][:, p64] for h in hs], 1)
        pcs += [A, B]
    pcs.append(wc[:, 1280:1536])
    sh["winc"] = np.ascontiguousarray(np.concatenate(pcs, 1))
    wo = np.asarray(inp["w_out_c"][0], f)
    wop = np.empty_like(wo)
    for m in range(2):
        for j in range(4):
            for r in range(2):
                h = 8 * m + 4 * r + j
                wop[(4 * m + j) * 128 + r * 64:(4 * m + j) * 128 + r * 64 + 64] = wo[h * 64:(h + 1) * 64]
    sh["woutc"] = wop
    return sh


def _colT(v, n):
    return np.asarray(v, np.float32).reshape(n, 128).T


def _prep_core(inp, core, sh):
    f = np.float32
    sample = core >= 4
    m = dict(sh)
    if sample:
        b = core - 4
        x = np.asarray(inp["x_sample"][b], f)
        cond = np.asarray(inp["c"][b], f)
        m["ckvctxT"] = np.ascontiguousarray(np.asarray(inp["cache_mla_ckv"][b, 0], f).T)
        m["krctxT"] = np.ascontiguousarray(np.asarray(inp["cache_mla_krope"][b, 0], f).T)
        m["gkctxT"] = np.ascontiguousarray(np.asarray(inp["cache_gqa_k"][b, 0], f).reshape(NCTX, 256).T)
        m["gvctx"] = np.ascontiguousarray(np.asarray(inp["cache_gqa_v"][b, 0], f).reshape(NCTX, 256))
    else:
        x = np.asarray(inp["x_prompt"][core * 8:(core + 1) * 8], f).reshape(T, D)
        cond = np.asarray(inp["c_ctx"], f)
        m["ckvctxT"] = np.zeros((256, NCTX), f)
        m["krctxT"] = np.zeros((32, NCTX), f)
        m["gkctxT"] = np.zeros((256, NCTX), f)
        m["gvctx"] = np.zeros((NCTX, 256), f)
    m["xT"] = np.ascontiguousarray(x.T)
    cols = np.zeros((128, NCOLS), f)

    def put(name, arr):
        arr = np.asarray(arr, f)
        cols[:, _COLS[name]:_COLS[name] + arr.shape[1]] = arr

    put("cond", _colT(cond, 8))
    for l in range(2):
        put("gff1_%d" % l, _colT(inp["g_ff1"][l], 8))
        put("gmix_%d" % l, _colT(inp["g_mix"][l], 8))
        put("gff2_%d" % l, _colT(inp["g_ff2"][l], 8))
    put("gfinal", _colT(inp["g_final"], 8))
    put("gql", _colT(inp["g_q_lora"][0], 3))
    put("gkvl", _colT(inp["g_kv_lora"][0], 2))
    put("bdw", _colT(inp["b_dw"][0], 4))
    put("gln", _colT(inp["g_conv_ln"][0], 4))
    put("bln", _colT(inp["b_conv_ln"][0], 4))
    wdw = np.asarray(inp["w_dw"][0], f)
    put("wdw", wdw.T.reshape(4, 128, 31).transpose(1, 0, 2).reshape(128, 124))
    p64, _, _, _, _ = _rope_meta(64)
    gq = np.asarray(inp["g_q_head"][0], f); gk = np.asarray(inp["g_k_head"][0], f)
    put("gq", np.tile(gq, 2)[:, None]); put("gqp", np.tile(gq[p64], 2)[:, None])
    put("gk", np.tile(gk, 2)[:, None]); put("gkp", np.tile(gk[p64], 2)[:, None])
    put("hmask", np.full((128, 1), 1.0 if sample else 0.0, f))
    put("eps", np.full((128, 1), EPS, f))
    bm = np.asarray(inp["b_mod"], f)
    put("bmod", np.concatenate([_colT(bm[0], 72), _colT(bm[1], 72)], 1))
    ab = np.zeros((128, NQB, KT), f)
    if not sample:
        ab[:] = -30000.0
        for qb in range(NQB):
            ab[:, qb, 4 + 2 * qb:4 + 2 * qb + 2] = 0.0
    put("abias", ab.reshape(128, NQB * KT))
    m["cols"] = cols
    rope = np.zeros((2, 2, 128, T), f)
    c32, s32 = _rope_tables(32, sample)
    rope[0, 0, 64:96], rope[0, 1, 64:96] = c32, s32
    c64, s64 = _rope_tables(64, sample)
    rope[1, 0] = np.concatenate([c64, c64], 0)
    rope[1, 1] = np.concatenate([s64, s64], 0)
    m["rope"] = rope
    return m


_NC_CACHE = {}


def kernel(**inputs):
    inp = {k: np.asarray(v) for k, v in inputs.items()}
    if "nc" not in _NC_CACHE:
        _NC_CACHE["nc"] = build_nc()
    nc = _NC_CACHE["nc"]
    sh = _prep_shared(inp)
    in_maps = [_prep_core(inp, c, sh) for c in range(8)]
    res = run_bass_kernel_spmd(nc, in_maps, core_ids=list(range(8)))
    r = res.results
    f = np.float32
    y_prompt = np.concatenate([np.asarray(r[c]["yT"], f).T.reshape(8, 256, D) for c in range(4)], 0)
    y_sample = np.stack([np.asarray(r[c]["yT"], f).T for c in range(4, 8)], 0)
    ckv = np.concatenate([np.asarray(r[c]["ckvT"], f).T.reshape(8, 1, 256, 256) for c in range(4)], 0)
    kr = np.concatenate([np.asarray(r[c]["krT"], f).T.reshape(8, 1, 256, 32) for c in range(4)], 0)
    gk = np.concatenate([np.asarray(r[c]["gkT"], f).T.reshape(8, 1, 256, 4, 64) for c in range(4)], 0)
    gvv = np.concatenate([np.asarray(r[c]["gv"], f).reshape(8, 1, 256, 4, 64) for c in range(4)], 0)
    return (np.ascontiguousarray(y_prompt), np.ascontiguousarray(y_sample), np.ascontiguousarray(ckv),
            np.ascontiguousarray(kr), np.ascontiguousarray(gk), np.ascontiguousarray(gvv))
```
